# Optimizing a Trainium2 kernel written in Bass

```python
import math
import jax
import jax.numpy as jnp
from jax import lax
import numpy as np

D_MODEL = 1024
BATCH = 4
SEQ = 4096
DEPTH = 1
DEC_BATCH = 128
DEC_SEQ = 1
PAST_LEN = 8192
PAGE_SIZE = 128

POOL_WINDOWS = (2, 4, 8, 16)
POOL_GROUPS = len(POOL_WINDOWS)
POOL_WIDTH = D_MODEL // 2
POOL_GROUP_DIM = POOL_WIDTH // POOL_GROUPS
POOL_STATE_ROWS = max(POOL_WINDOWS) - 1
HEAD_DIM = 64
N_Q_HEADS = (D_MODEL // 2) // HEAD_DIM
N_KV_HEADS = 2
GQA_GROUP = N_Q_HEADS // N_KV_HEADS
ATTN_WIDTH = N_Q_HEADS * HEAD_DIM
KV_WIDTH = N_KV_HEADS * HEAD_DIM
WINDOW = 128
N_BUCKETS = 32
MAX_DISTANCE = 128
RMS_EPS = 1e-6
IN_COLS = 2 * POOL_WIDTH + 2 * ATTN_WIDTH + 2 * KV_WIDTH + 2 * D_MODEL

kernel_name = "gated_pool_swa_hybrid_step"


def rmsnorm(x, g):
    xf = x.astype(jnp.float32)
    y = xf * lax.rsqrt(jnp.mean(xf * xf, axis=-1, keepdims=True) + RMS_EPS) * g.astype(jnp.float32)
    return y.astype(x.dtype)


def t5_bucket(rel):
    n = jnp.maximum(rel, 0)
    max_exact = N_BUCKETS // 2
    nf = jnp.maximum(n, 1).astype(jnp.float32)
    log_b = max_exact + (jnp.log(nf / max_exact) / math.log(MAX_DISTANCE / max_exact)
                         * (N_BUCKETS - max_exact)).astype(jnp.int32)
    return jnp.where(n < max_exact, n, jnp.minimum(log_b, N_BUCKETS - 1))


def split_projection(xn, w_in):
    h = jnp.einsum('btd,dc->btc', xn, w_in)
    sizes = (POOL_WIDTH, POOL_WIDTH, ATTN_WIDTH, KV_WIDTH, KV_WIDTH, ATTN_WIDTH, D_MODEL, D_MODEL)
    offs = [int(o) for o in np.cumsum(sizes)[:-1]]
    u, z_pool, q, k, v, z_attn, g_pool, g_attn = jnp.split(h, offs, axis=-1)
    b, t = h.shape[0], h.shape[1]
    q = q.reshape(b, t, N_KV_HEADS, GQA_GROUP, HEAD_DIM)
    k = k.reshape(b, t, N_KV_HEADS, HEAD_DIM)
    v = v.reshape(b, t, N_KV_HEADS, HEAD_DIM)
    return u, z_pool, q, k, v, z_attn, g_pool, g_attn


def multiscale_pool(u_ext, p0, n_new):
    length = u_ext.shape[1]
    uf = u_ext.astype(jnp.float32)
    pos = p0 + jnp.arange(length)
    outs = []
    for g, w in enumerate(POOL_WINDOWS):
        ug = uf[..., g * POOL_GROUP_DIM:(g + 1) * POOL_GROUP_DIM]
        cs = lax.cumsum(ug, axis=1)
        shifted = jnp.pad(cs, ((0, 0), (w, 0), (0, 0)))[:, :length]
        cnt = jnp.minimum(pos + 1, w).astype(jnp.float32)[None, :, None]
        outs.append((cs - shifted) / cnt - ug)
    pooled = jnp.concatenate(outs, axis=-1)[:, length - n_new:]
    return pooled.astype(u_ext.dtype)


def sink_attention(q, k, v, qpos, kpos, rel_bias, sinks):
    s = jnp.einsum('...qkgd,...skd->...kgqs', q, k).astype(jnp.float32) * (HEAD_DIM ** -0.5)
    rel = qpos[..., :, None] - kpos[..., None, :]
    valid = (rel >= 0) & (rel < WINDOW) & (kpos[..., None, :] >= 0)
    bias = jnp.moveaxis(rel_bias.astype(jnp.float32)[t5_bucket(rel)], -1, -3)
    bias = bias.reshape(bias.shape[:-3] + (N_KV_HEADS, GQA_GROUP) + bias.shape[-2:])
    logits = jnp.where(valid[..., None, None, :, :], s + bias, -jnp.inf)
    sink = sinks.astype(jnp.float32).reshape(N_KV_HEADS, GQA_GROUP, 1, 1)
    m = jnp.maximum(jnp.max(logits, axis=-1, keepdims=True), sink)
    p = jnp.exp(logits - m)
    denom = jnp.sum(p, axis=-1, keepdims=True) + jnp.exp(sink - m)
    return jnp.einsum('...kgqs,...skd->...qkgd', (p / denom).astype(v.dtype), v)


def attn_prompt(q, k, v, rel_bias, sinks):
    b, t = q.shape[0], q.shape[1]
    nb = t // WINDOW
    qb = q.reshape(b, nb, WINDOW, N_KV_HEADS, GQA_GROUP, HEAD_DIM)
    kb = k.reshape(b, nb, WINDOW, N_KV_HEADS, HEAD_DIM)
    vb = v.reshape(b, nb, WINDOW, N_KV_HEADS, HEAD_DIM)
    pad = ((0, 0), (1, 0), (0, 0), (0, 0), (0, 0))
    kband = jnp.concatenate([jnp.pad(kb, pad)[:, :nb], kb], axis=2)
    vband = jnp.concatenate([jnp.pad(vb, pad)[:, :nb], vb], axis=2)
    qpos = jnp.arange(t).reshape(nb, WINDOW)
    kpos = jnp.concatenate([qpos - WINDOW, qpos], axis=1)
    o = sink_attention(qb, kband, vband, qpos, kpos, rel_bias, sinks)
    return o.reshape(b, t, ATTN_WIDTH)


def attn_sample(q, k_ext, v_ext, rel_bias, sinks):
    b, s_new = q.shape[0], q.shape[1]
    qpos = PAST_LEN + jnp.arange(s_new)
    kpos = PAST_LEN - WINDOW + jnp.arange(WINDOW + s_new)
    o = sink_attention(q, k_ext, v_ext, qpos, kpos, rel_bias, sinks)
    return o.reshape(b, s_new, ATTN_WIDTH)


def merge_branches(x, pooled, z_pool, att, z_attn, g_pool, g_attn,
                   w_grp, pool_scale, w_br_pool, w_br_attn, w_out):
    b, t = pooled.shape[0], pooled.shape[1]
    pg = jnp.einsum('btgc,gce->btge', pooled.reshape(b, t, POOL_GROUPS, POOL_GROUP_DIM), w_grp)
    pg = pg.reshape(b, t, POOL_WIDTH)
    br_pool = jnp.einsum('btp,pd->btd', pg * pool_scale * jax.nn.silu(z_pool), w_br_pool)
    br_attn = jnp.einsum('bta,ad->btd', att * jax.nn.silu(z_attn), w_br_attn)
    merged = jax.nn.sigmoid(g_pool) * br_pool + jax.nn.sigmoid(g_attn) * br_attn
    return x + jnp.einsum('btd,de->bte', merged, w_out)


def setup_inputs(seed: int = 0) -> dict:
    key = jax.random.key(seed)
    ks = jax.random.split(key, 15)
    nrm = lambda k, shape, s: jax.random.normal(k, shape, jnp.float32) * s
    return {
        "x_prompt": nrm(ks[0], (BATCH, SEQ, D_MODEL), 1.0),
        "x_sample": nrm(ks[1], (DEC_BATCH, DEC_SEQ, D_MODEL), 1.0),
        "cache_k": nrm(ks[2], (DEPTH, DEC_BATCH, WINDOW, N_KV_HEADS, HEAD_DIM), 1.0),
        "cache_v": nrm(ks[3], (DEPTH, DEC_BATCH, WINDOW, N_KV_HEADS, HEAD_DIM), 1.0),
        "state_pool": nrm(ks[4], (DEPTH, DEC_BATCH, POOL_STATE_ROWS, POOL_WIDTH), 1.0),
        "rel_bias": nrm(ks[5], (N_BUCKETS, N_Q_HEADS), 0.5),
        "g_norm": 1.0 + nrm(ks[6], (DEPTH, D_MODEL), 0.02),
        "w_in": nrm(ks[7], (DEPTH, D_MODEL, IN_COLS), D_MODEL ** -0.5),
        "pool_w_grp": nrm(ks[8], (DEPTH, POOL_GROUPS, POOL_GROUP_DIM, POOL_GROUP_DIM), POOL_GROUP_DIM ** -0.5),
        "pool_scale": 1.0 + nrm(ks[9], (DEPTH, POOL_WIDTH), 0.02),
        "attn_sinks": nrm(ks[10], (DEPTH, N_Q_HEADS), 1.0),
        "w_br_pool": nrm(ks[11], (DEPTH, POOL_WIDTH, D_MODEL), POOL_WIDTH ** -0.5),
        "w_br_attn": nrm(ks[12], (DEPTH, ATTN_WIDTH, D_MODEL), ATTN_WIDTH ** -0.5),
        "w_out": nrm(ks[13], (DEPTH, D_MODEL, D_MODEL), D_MODEL ** -0.5),
        "g_final": 1.0 + nrm(ks[14], (D_MODEL,), 0.02),
    }


def reference(x_prompt, x_sample, cache_k, cache_v, state_pool, rel_bias, g_norm, w_in,
              pool_w_grp, pool_scale, attn_sinks, w_br_pool, w_br_attn, w_out, g_final):
    seq = x_prompt.shape[1]
    dec_seq = x_sample.shape[1]
    xp, xs = x_prompt, x_sample
    kp_l, vp_l, pp_l, ks_l, vs_l, ps_l = [], [], [], [], [], []
    for l in range(DEPTH):
        u, zp, q, k, v, za, gp, ga = split_projection(rmsnorm(xp, g_norm[l]), w_in[l])
        pooled = multiscale_pool(u, 0, seq)
        att = attn_prompt(q, k, v, rel_bias, attn_sinks[l])
        xp = merge_branches(xp, pooled, zp, att, za, gp, ga, pool_w_grp[l], pool_scale[l],
                            w_br_pool[l], w_br_attn[l], w_out[l])
        kp_l.append(k[:, seq - WINDOW:])
        vp_l.append(v[:, seq - WINDOW:])
        pp_l.append(u[:, seq - POOL_STATE_ROWS:])

        u, zp, q, k, v, za, gp, ga = split_projection(rmsnorm(xs, g_norm[l]), w_in[l])
        u_ext = jnp.concatenate([state_pool[l].astype(u.dtype), u], axis=1)
        pooled = multiscale_pool(u_ext, PAST_LEN - POOL_STATE_ROWS, dec_seq)
        k_ext = jnp.concatenate([cache_k[l].astype(k.dtype), k], axis=1)
        v_ext = jnp.concatenate([cache_v[l].astype(v.dtype), v], axis=1)
        att = attn_sample(q, k_ext, v_ext, rel_bias, attn_sinks[l])
        xs = merge_branches(xs, pooled, zp, att, za, gp, ga, pool_w_grp[l], pool_scale[l],
                            w_br_pool[l], w_br_attn[l], w_out[l])
        ks_l.append(k_ext[:, dec_seq:])
        vs_l.append(v_ext[:, dec_seq:])
        ps_l.append(u_ext[:, dec_seq:])
    y_prompt = rmsnorm(xp, g_final)
    y_sample = rmsnorm(xs, g_final)
    return (y_prompt, y_sample, jnp.stack(kp_l), jnp.stack(vp_l), jnp.stack(pp_l),
            jnp.stack(ks_l), jnp.stack(vs_l), jnp.stack(ps_l))
```

```python
import math
import os
from contextlib import ExitStack

import numpy as np
import concourse.bass as bass
import concourse.mybir as mybir
from concourse.bass_utils import run_bass_kernel_spmd

F32 = mybir.dt.float32
BF16 = mybir.dt.bfloat16
AF = mybir.ActivationFunctionType
ALU = mybir.AluOpType

NCORES = 8
D = 1024
NBLK = 16
CB = 2
NCH = NBLK // CB
CT = CB * 128
NS = 16
INC = 4352
U0, ZP0, Q0, K0, V0, ZA0, GP0, GA0 = 0, 512, 1024, 1536, 1664, 1792, 2304, 3328
POOLW = (2, 4, 8, 16)
NEG = -30000.0


class Op:
    __slots__ = ("eng", "fn", "deps", "sig", "sigval", "semkey", "idx")

    def __init__(self, eng, fn):
        self.eng = eng
        self.fn = fn
        self.deps = set()
        self.sig = False
        self.sigval = None
        self.semkey = None


class Sched:
    ENGS = ("tensor", "vector", "scalar", "gpsimd", "sync")

    def __init__(self):
        self.ops = {e: [] for e in self.ENGS}
        self.writers = {}
        self.readers = {}
        self.n = 0
        self.exclusive = set()

    def add(self, eng, fn, reads=(), writes=(), deps=(), semkey=None):
        op = Op(eng, fn)
        op.semkey = semkey
        writes = list(writes) + [r for r in reads if r in self.exclusive]
        reads = [r for r in reads if r not in self.exclusive]
        d = set(x for x in deps if x is not None)
        for r in reads:
            d.update(self.writers.get(r, {}).values())
        for wr in writes:
            d.update(self.writers.get(wr, {}).values())
            d.update(self.readers.get(wr, {}).values())
        for r in reads:
            self.readers.setdefault(r, {})[(eng, semkey)] = op
        for wr in writes:
            self.readers[wr] = {}
            self.writers.setdefault(wr, {})[(eng, semkey)] = op
        d.discard(op)
        if eng == "tensor":
            d = set(x for x in d if x.eng != "tensor")
        op.deps = d
        for x in d:
            x.sig = True
        op.idx = self.n
        self.n += 1
        self.ops[eng].append(op)
        return op

    def emit(self, nc, ctx, final_ops=()):
        eng_sem = {}
        for e in ("tensor", "vector", "scalar", "gpsimd"):
            eng_sem[e] = ctx.enter_context(nc.semaphore("s_" + e))
            c = 0
            for op in self.ops[e]:
                if op.sig and op.semkey is None:
                    c += 1
                    op.sigval = (eng_sem[e], c)
        dma_sem, dma_cnt = {}, {}
        for op in self.ops["sync"] + [o for o in self.ops["gpsimd"] if o.semkey is not None]:
            k = op.semkey
            assert k is not None
            if k not in dma_sem:
                dma_sem[k] = ctx.enter_context(nc.semaphore("d_%d" % len(dma_sem)))
                dma_cnt[k] = 0
            dma_cnt[k] += 16
            op.sigval = (dma_sem[k], dma_cnt[k])
            op.sig = True

        def run(e, h):
            waited = {}
            for op in self.ops[e]:
                for dep in sorted(op.deps, key=lambda x: x.idx):
                    sem, val = dep.sigval
                    if waited.get(id(sem), 0) >= val:
                        continue
                    waited[id(sem)] = val
                    h.wait_ge(sem, val)
                ins = op.fn(h)
                if op.sig:
                    ins.then_inc(op.sigval[0], 16 if op.semkey is not None else 1)

        with nc.Block() as block:
            @block.sync
            def _(h):
                run("sync", h)
                done = {}
                for op in final_ops:
                    sem, val = op.sigval
                    if done.get(id(sem), (None, 0))[1] < val:
                        done[id(sem)] = (sem, val)
                for sem, val in done.values():
                    h.wait_ge(sem, val)

            @block.tensor
            def _(h):
                run("tensor", h)

            @block.vector
            def _(h):
                run("vector", h)

            @block.scalar
            def _(h):
                run("scalar", h)

            @block.gpsimd
            def _(h):
                run("gpsimd", h)


def _t5_bucket(n):
    n = np.maximum(n, 0)
    nf = np.maximum(n, 1).astype(np.float32)
    lb = 16 + (np.log(nf / np.float32(16)) / np.float32(math.log(128 / 16)) * np.float32(16)).astype(np.int32)
    return np.where(n < 16, n, np.minimum(lb, 31))


def _consts():
    rp = np.arange(384)
    rel = rp - 128
    valid = (rel >= 0) & (rel <= 127)
    bk = _t5_bucket(rel)
    eb = np.zeros((33, 384), np.float32)
    eb[bk[valid], rp[valid]] = 1.0
    eb[32, rp[~valid]] = 1.0
    tp = np.arange(128)[:, None]
    t = np.arange(128)[None, :]
    mcp = np.zeros((128, 4, 2, 128), np.float32)
    m0 = np.zeros((2, 128, 4, 128), np.float32)
    cinv0 = np.ones((2, 128, 4, 128), np.float32)
    for g, w in enumerate(POOLW):
        cur = ((tp <= t) & (tp > t - w)).astype(np.float32)
        cur_reg = cur - w * (tp == t)
        prev = (tp - 128 > t - w).astype(np.float32)
        mcp[:, g, 0, :] = prev
        mcp[:, g, 1, :] = cur_reg
        cnt = np.minimum(t + 1, w).astype(np.float32)
        m0[0, :, g, :] = cur - cnt * (tp == t)
        m0[1, :, g, :] = cur_reg
        cinv0[0, :, g, :] = np.broadcast_to(w / cnt, (128, 128))
    sel = np.zeros((240, 4, NS), np.float32)
    for g, w in enumerate(POOLW):
        for b in range(NS):
            for i in range(15):
                if i >= 16 - w:
                    sel[b * 15 + i, g, b] = 1.0
    seld = np.zeros((NS, 4, NS), np.float32)
    for g, w in enumerate(POOLW):
        seld[:, g, :] = np.eye(NS) * (1.0 - w)
    return eb, mcp, m0, cinv0, sel, seld


def build_program():
    nc = bass.Bass("TRN2", target_bir_lowering=False)

    def din(name, shape):
        return nc.dram_tensor(name, list(shape), F32, kind="ExternalInput").ap()

    def dout(name, shape):
        return nc.dram_tensor(name, list(shape), F32, kind="ExternalOutput").ap()

    xc = din("xc", [NBLK + 1, 128, D])
    w_in = din("w_in", [D, INC])
    relb = din("rel_bias", [32, 8])
    gn = din("gn", [128, 8])
    wgrp_d = din("w_grp", [4, 128, 128])
    pscale_d = din("pscale", [128, 4])
    sinks_d = din("sinks", [1, 8])
    wbrp_d = din("w_br_pool", [512, D])
    wbra_d = din("w_br_attn", [512, D])
    wout_d = din("w_out", [D, D])
    gfin_d = din("g_final", [1, D])
    ident_d = din("ident", [128, 128])
    eb_d = din("eb", [33, 384])
    mcp_d = din("mcp", [128, 4 * 2 * 128])
    m0_d = din("m0", [128, 4 * 128])
    cinv0_d = din("cinv0", [128, 4 * 128])
    hmask_d = din("hmask", [128, 1])
    xs_d = din("xsamp", [NS, D])
    ck_d = din("cache_k", [NS, 128, 128])
    cv_d = din("cache_v", [NS, 128, 128])
    sp_d = din("state_pool", [NS, 15, 512])
    sel_d = din("sel", [240, 4 * NS])
    seld_d = din("seld", [NS, 4 * NS])

    y_out = dout("y", [NBLK, 128, D])
    k_out = dout("k_new", [128, 128])
    v_out = dout("v_new", [128, 128])
    p_out = dout("p_new", [15, 512])
    ys_out = dout("ys", [NS, D])
    ks_out = dout("ks_new", [NS, 128, 128])
    vs_out = dout("vs_new", [NS, 128, 128])
    ps_out = dout("ps_new", [NS, 15, 512])
    scr = nc.dram_tensor("scr", [8, 128, 384], F32, kind="Internal").ap()
    scr1 = nc.dram_tensor("scr1", [8, 384], F32, kind="Internal").ap()

    S = Sched()
    S.exclusive = {"bank0", "fm0", "fm1", "tm0", "tm1", "stp", "opv"}
    finals = []
    with ExitStack() as ctx:
        def sb(name, shape, dt=F32):
            return ctx.enter_context(nc.sbuf_tensor("sb_" + name, list(shape), dt))

        def ps(name, shape, dt=F32):
            return ctx.enter_context(nc.psum_tensor("ps_" + name, list(shape), dt))

        win = sb("win", [128, 8, INC], BF16)
        wbrp = sb("wbrp", [128, 4, D], BF16)
        wbra = sb("wbra", [128, 4, D], BF16)
        wout = sb("wout", [128, 8, D], BF16)
        wgrp = sb("wgrp", [128, 4, 128], BF16)
        biasT = sb("biasT", [128, 2, 8, 128])
        biasT0 = sb("biasT0", [128, 8, 128])
        biasS = sb("biasS", [128, 8])
        mcp = sb("mcp", [128, 4, 2, 128], BF16)
        m0 = sb("m0", [128, 4, 128], BF16)
        cinv0 = sb("cinv0", [128, 4, 128])
        ident = sb("ident", [128, 128], BF16)
        identf = sb("identf", [128, 128])
        gfin = sb("gfin", [128, D])
        gnt = sb("gnt", [128, 8])
        pscale = sb("pscale", [128, 4])
        sinkexp = sb("sinkexp", [128, 8])
        hmask = sb("hmask", [128, 1])
        mhalf = sb("mhalf", [128, 1])
        r33 = sb("r33", [33, 8])
        ebt = sb("ebt", [33, 384])
        small = sb("small", [128, 32])
        xs = [sb("xs%d" % i, [128, D]) for i in range(2)]
        xr = sb("xr", [128, D])
        xr2 = sb("xr2", [128, D])
        xn = [sb("xn%d" % i, [128, D], BF16) for i in range(2)]
        junk = sb("junk", [128, D], BF16)
        xT = sb("xT", [128, 8, CT], BF16)
        uring = sb("uring", [128, 3, 512], BF16)
        kring = sb("kring", [128, 3, 128], BF16)
        vring = sb("vring", [128, 3, 2, 65], BF16)
        zat = sb("zat", [128, CB, 512], BF16)
        zpT = sb("zpT", [128, 4, CT], BF16)
        qT = sb("qT", [128, 4, CT], BF16)
        pooledT = sb("pooledT", [128, 4, CT], BF16)
        prodp = sb("prodp", [128, 4, CT], BF16)
        PT = sb("PT", [128, 2, 2, 512], BF16)
        dtmp = sb("dtmp", [128, 8])
        rden = sb("rden", [128, 8])
        attn = sb("attn", [128, 512])
        attg = sb("attg", [128, 512], BF16)
        attT = sb("attT", [128, 4, CT], BF16)
        sgall = sb("sgall", [128, 8, 2, CT], BF16)
        t12 = sb("t12", [128, 2, 2, CT])
        mergedT = sb("mergedT", [128, 8, CT], BF16)
        rr = [sb("r%d" % i, [128, D]) for i in range(2)]
        kvo = sb("kvo", [128, 2, 128])
        uo = sb("uo", [128, 512])
        selb = sb("selb", [128, 3, 4 * NS], BF16)
        qTs = sb("qTs", [128, NS, 4], BF16)
        vaugs = sb("vaugs", [128, NS, 2, 65], BF16)
        nrm = attn
        onesf = sb("onesf", [128, 64])

        bank0 = ps("bank0", [128, 1024], BF16)
        fmb = [ps("fm%d" % i, [128, 512]) for i in range(2)]
        stp2 = ps("stp2", [128, 2, 512])
        tmb = [stp2[:, 0, :], stp2[:, 1, :]]
        stp = ps("stp", [128, 2, 512])
        opv = ps("opv", [128, 4, 65])

        fm_i = [0]

        ACC = [(fmb[0], "fm0"), (tmb[0], "tm0"), (fmb[1], "fm1"), (tmb[1], "tm1")]
        acc_i = [0]

        fm_only = [False]
        fm_allocs = [0]

        def alloc_bank():
            fm_allocs[0] += 1
            if fm_only[0]:
                i = fm_i[0] % 2
                fm_i[0] += 1
                return fmb[i], "fm%d" % i
            i = acc_i[0] % 4
            acc_i[0] += 1
            return ACC[i]

        def acc_slot():
            return alloc_bank()

        def fm_slot():
            t, nme = acc_slot()
            return t[:, 0:CT], nme

        tm_i = [0]

        def tm_slot():
            return acc_slot()

        def dma(out, in_, reads, writes, key, **kw):
            return S.add("sync", lambda e: e.dma_start(out=out, in_=in_, **kw), reads=reads, writes=writes, semkey=key)

        def act(out, in_, func, reads, writes, **kw):
            return S.add("scalar", lambda e: e.activation(out=out, in_=in_, func=func, **kw), reads=reads, writes=writes)

        def vcopy(eng, out, in_, reads, writes):
            return S.add(eng, lambda e: e.tensor_copy(out=out, in_=in_), reads=reads, writes=writes)

        def tt(eng, out, in0, in1, op, reads, writes):
            return S.add(eng, lambda e: e.tensor_tensor(out=out, in0=in0, in1=in1, op=op), reads=reads, writes=writes)

        def tsc(eng, out, in0, s1, s2, op0, op1, reads, writes):
            return S.add(eng, lambda e: e.tensor_scalar(out=out, in0=in0, scalar1=s1, scalar2=s2, op0=op0, op1=op1),
                         reads=reads, writes=writes)

        def mm(out, lhsT, rhs, start, stop, reads, writes):
            return S.add("tensor", lambda e: e.matmul(out, lhsT=lhsT, rhs=rhs, start=start, stop=stop),
                         reads=reads, writes=writes)

        def tr(out, in_, idn, reads, writes):
            return S.add("tensor", lambda e: e.transpose(out=out, in_=in_, identity=idn), reads=reads, writes=writes)

        dma(identf[:], ident_d, [], ["identf"], "c0")
        vcopy("vector", ident[:], identf[:], ["identf"], ["ident"])
        dma(gnt[:], gn, [], ["gnt"], "c1")
        dma(pscale[:], pscale_d, [], ["pscale"], "c2")
        for g, w in enumerate(POOLW):
            tsc("gpsimd", pscale[:, g:g + 1], pscale[:, g:g + 1], 1.0 / w, 1.0, ALU.mult, ALU.mult, ["pscale"], ["pscale"])
        dma(sinkexp[:], sinks_d.partition_broadcast(128), [], ["sinkexp"], "c3")
        act(sinkexp[:], sinkexp[:], AF.Exp, ["sinkexp"], ["sinkexp"])
        dma(gfin[:], gfin_d.partition_broadcast(128), [], ["gfin"], "c4")
        dma(hmask[:], hmask_d, [], ["hmask"], "c5")
        S.add("gpsimd", lambda e: e.memset(mhalf[:], -0.5), writes=["mhalf"])
        S.add("gpsimd", lambda e: e.memset(small[:, 16:18], 0.0), writes=["dmy_in"])
        S.add("gpsimd", lambda e: e.memset(vring[:], 1.0), writes=["vring0", "vring1", "vring2"])
        S.add("gpsimd", lambda e: e.memset(vaugs[:], 1.0), writes=["vaugs"])
        S.add("gpsimd", lambda e: e.memset(onesf[:], 1.0), writes=["onesf"])
        dma(cinv0[:], cinv0_d.rearrange("p (g t) -> p g t", g=4), [], ["cinv0"], "c6")
        dma(rr[0][:, 0:1024], mcp_d, [], ["r0"], "st0")
        vcopy("vector", mcp[:], rr[0][:, 0:1024].rearrange("p (g k t) -> p g k t", g=4, k=2), ["r0"], ["mcp"])
        dma(rr[1][:, 0:512], m0_d, [], ["r1"], "st1")
        vcopy("vector", m0[:], rr[1][:, 0:512].rearrange("p (g t) -> p g t", g=4), ["r1"], ["m0"])

        dma(r33[0:32, :], relb, [], ["r33"], "c7")
        S.add("gpsimd", lambda e: e.memset(r33[32:33, :], NEG), writes=["r33"])
        dma(ebt[:], eb_d, [], ["ebt"], "c8")

        def bias_tables():
            tmt, tmn = tm_slot()
            S.add("tensor", lambda e: e.matmul(tmt[0:8, 0:384], lhsT=r33[:, :], rhs=ebt[:], start=True, stop=True),
                  reads=["r33", "ebt"], writes=[tmn])
            vcopy("vector", uo[0:8, 0:384], tmt[0:8, 0:384], [tmn], ["uo"])
            w1 = S.add("gpsimd", lambda e: e.dma_start(out=scr1, in_=uo[0:8, 0:384]), reads=["uo"], writes=["scr1"], semkey="b0")
            rep = bass.AP(tensor=scr1.tensor, offset=0, ap=[[384, 8], [0, 128], [1, 384]])
            w2 = S.add("gpsimd", lambda e: e.dma_start(out=scr, in_=rep), reads=["scr1"], writes=["scr"], deps=[w1], semkey="b1")
            for kb in range(2):
                src = bass.AP(tensor=scr.tensor, offset=(256 if kb == 0 else 128),
                              ap=[[383, 128], [128 * 384, 8], [1, 128]])
                S.add("gpsimd", lambda e, src=src, kb=kb: e.dma_start(out=biasT[:, kb, :, :], in_=src),
                      reads=["scr"], writes=["biasT"], deps=[w2], semkey="c9")
            srcS = bass.AP(tensor=scr.tensor, offset=255, ap=[[383, 128], [128 * 384, 8], [1, 1]])
            S.add("gpsimd", lambda e: e.dma_start(out=biasS[:].unsqueeze(2), in_=srcS, allow_slow_non_contiguous=True),
                  reads=["scr"], writes=["biasS"], deps=[w2], semkey="c10")
            tsc("vector", biasT0[:], biasT[:, 0, :, :], hmask[:, 0:1], None, ALU.add, ALU.bypass, ["biasT", "hmask"], ["biasT0"])

        cast_i = [0]

        def cast(out, in_, scale_ap, reads, writes):
            i = cast_i[0] % 2
            cast_i[0] += 1
            if i == 1:
                return act(out, in_, AF.Copy, reads, writes, scale=scale_ap)
            return tsc("vector", out, in_, scale_ap, None, ALU.mult, ALU.bypass, reads, writes)

        def f32view(ap2d):
            return ap2d.bitcast(F32)

        stg_big = [(rr[0][:, :], ["r0"], "st0"), (rr[1][:, :], ["r1"], "st1"), (xr[:, :], ["xr"], "xr"),
                   (xr2[:, :], ["xr2"], "xr2"),
                   (f32view(mergedT[:, :, :].rearrange("p a b -> p (a b)")), ["mergedT"], "st4")]
        stg_small = [(f32view(PT[:, 0, :, :].rearrange("p a b -> p (a b)")), ["PT0"], "st5"),
                     (f32view(PT[:, 1, :, :].rearrange("p a b -> p (a b)")), ["PT1"], "st6"),
                     (f32view(attT[:, :, :].rearrange("p a b -> p (a b)")), ["attT"], "st7"),
                     (f32view(prodp[:, :, :].rearrange("p a b -> p (a b)")), ["prodp"], "st8")]
        st_i = [0, 0]

        def stage(n):
            if n <= 512:
                pool_ = stg_small + stg_big
                i = st_i[0] % len(pool_)
                st_i[0] += 1
                return pool_[i]
            i = st_i[1] % len(stg_big)
            st_i[1] += 1
            return stg_big[i]

        def gdma(out, in_, writes, key):
            return S.add("gpsimd", lambda e: e.dma_start(out=out, in_=in_), writes=writes, semkey=key)

        pieces = [(U0, 1024, ["w_u", "w_zp"]), (Q0, 768, ["w_q", "w_kv"]), (ZA0, 512, ["w_za"]),
                  (GP0, 1024, ["w_gp"]), (GA0, 1024, ["w_ga"])]

        def load_pieces(lo, hi):
          for pi in range(lo, hi):
            c0, n, wns = pieces[pi]
            for k in range(8):
                st, sns, sk = stage(n)
                dma(st[:, 0:n], w_in[k * 128:(k + 1) * 128, c0:c0 + n], sns, sns, sk)
                if c0 == Q0:
                    for g in range(2):
                        src = st[:, g * 256:(g + 1) * 256].rearrange("p (j d) -> p j d", j=4)
                        dst = win[:, k, Q0:Q0 + 512].rearrange("p (j g d) -> p j g d", j=4, g=2)[:, :, g, :]
                        if g == 0:
                            vcopy("vector", dst, src, sns, ["w_q"])
                        else:
                            act(dst, src, AF.Copy, sns, ["w_q"])
                    vcopy("vector", win[:, k, K0:K0 + 256], st[:, 512:768], sns, ["w_kv"])
                else:
                    cast_i[0] += 1
                    if cast_i[0] % 2 == 0:
                        vcopy("vector", win[:, k, c0:c0 + n], st[:, 0:n], sns, wns)
                    else:
                        act(win[:, k, c0:c0 + n], st[:, 0:n], AF.Copy, sns, wns)

        def side_weights():
            gdma(wgrp[:, :, :], wgrp_d.rearrange("g c e -> c g e"), ["wgrp"], "gw0")
            for k in range(4):
                gdma(wbrp[:, k, :], wbrp_d[k * 128:(k + 1) * 128, :], ["wbrp"], "gw1")
            for k in range(4):
                gdma(wbra[:, k, :], wbra_d[k * 128:(k + 1) * 128, :], ["wbra"], "gw2")
            for k in range(8):
                gdma(wout[:, k, :], wout_d[k * 128:(k + 1) * 128, :], ["wout"], "gw3")

        def load_x(src_ap, slot, ntok):
            dma(xs[slot][0:ntok, :], src_ap, [], ["xs%d" % slot], "xs%d" % slot)

        def norm_x(slot, ntok):
            xsn, xnn = "xs%d" % slot, "xn%d" % slot
            ssc = small[0:ntok, slot:slot + 1]
            rsc = small[0:ntok, 2 + slot:3 + slot]
            act(junk[0:ntok, :], xs[slot][0:ntok, :], AF.Square, [xsn], ["junk", "ss%d" % slot], accum_out=ssc)
            tsc("gpsimd", rsc, ssc, 1.0 / D, 1e-6, ALU.mult, ALU.add, ["ss%d" % slot], ["rs%d" % slot])
            tt("gpsimd", rsc, rsc, mhalf[0:ntok, :], ALU.pow, ["rs%d" % slot, "mhalf"], ["rs%d" % slot])
            act(xn[slot][0:ntok, :], xs[slot][0:ntok, :], AF.Copy, [xsn, "rs%d" % slot], [xnn], scale=rsc)

        tx_i = [0]

        def transp_x(slot, ntok, col):
            xnn = "xn%d" % slot
            for k in range(8):
                tr(bank0[:, k * 128:k * 128 + ntok], xn[slot][0:ntok, k * 128:(k + 1) * 128],
                   ident[0:ntok, 0:ntok], [xnn, "ident"], ["bank0"])
            dst = xT[:, :, col:col + ntok]
            srcp = bank0[:, :].rearrange("p (k t) -> p k t", k=8)[:, :, 0:ntok]
            tt("vector", dst, srcp, gnt[:, :].unsqueeze(2).broadcast_to([128, 8, ntok]), ALU.mult,
               ["bank0", "gnt"], ["xT"])

        def load_norm_T(src_ap, slot, ntok, col, preloaded=False):
            if not preloaded:
                load_x(src_ap, slot, ntok)
            norm_x(slot, ntok)
            transp_x(slot, ntok, col)

        WRES = {U0: "w_u", V0: "w_kv", K0: "w_kv", ZA0: "w_za"}

        def tok_group(col, ntok, c0, ncol):
            tmt, tmn = tm_slot()
            for k in range(8):
                mm(tmt[0:ntok, 0:ncol], xT[:, k, col:col + ntok], win[:, k, c0:c0 + ncol], k == 0, k == 7,
                   ["xT", WRES[c0]], [tmn])
            return tmt, tmn

        def feat_group(ncols, wt, wn, nk, c0, rhs_fn, rnames):
            fmt, fmn = fm_slot()
            for k in range(nk):
                mm(fmt[:, 0:ncols], wt[:, k, c0:c0 + 128], rhs_fn(k), k == 0, k == nk - 1, [wn] + rnames, [fmn])
            return fmt, fmn

        def tok_stage(bi, col, j, last):
            us = bi % 3
            tmt, tmn = tok_group(col, 128, U0, 512)
            act(uring[:, us, :], tmt[:, :], AF.Copy, [tmn], ["u%d" % us])
            if last:
                vcopy("vector", uo[:], tmt[:, :], [tmn], ["uo"])
                finals.append(dma(p_out, uo[113:128, :], ["uo"], [], "o_p"))
            tmt, tmn = tok_group(col, 128, V0, 128)
            vcopy("vector", vring[:, us, :, 0:64], tmt[:, 0:128].rearrange("p (g d) -> p g d", g=2), [tmn], ["vring%d" % us])
            if last:
                vcopy("vector", kvo[:, 1, :], tmt[:, 0:128], [tmn], ["kvo1"])
                finals.append(dma(v_out, kvo[:, 1, :], ["kvo1"], [], "o_v"))
                tmt, tmn = tok_group(col, 128, K0, 128)
                vcopy("vector", kvo[:, 0, :], tmt[:, 0:128], [tmn], ["kvo0"])
                finals.append(dma(k_out, kvo[:, 0, :], ["kvo0"], [], "o_k"))
            if j is not None:
                tmt, tmn = tok_group(col, 128, ZA0, 512)
                act(zat[:, j, :], tmt[:, :], AF.Silu, [tmn], ["zat%d" % j])

        def k_stage(bis, ncols):
            fmt, fmn = feat_group(ncols, win, "w_kv", 8, K0, lambda k: xT[:, k, 0:ncols], ["xT"])
            for j, bi in enumerate(bis):
                vcopy("vector", kring[:, bi % 3, :], fmt[:, j * 128:(j + 1) * 128], [fmn], ["k%d" % (bi % 3)])

        def feat_stage(bis, n):
            for m in range(4):
                fmt, fmn = feat_group(n, win, "w_zp", 8, ZP0 + m * 128, lambda k: xT[:, k, 0:n], ["xT"])
                act(zpT[:, m, 0:n], fmt[:, 0:n], AF.Silu, [fmn], ["zpT"])
            for m in range(4):
                fmt, fmn = feat_group(n, win, "w_q", 8, Q0 + m * 128, lambda k: xT[:, k, 0:n], ["xT"])
                qdst = qT[:, m, 0:n] if bis is not None else qTs[:, :, m]
                tsc("vector", qdst, fmt[:, 0:n], 0.125, None, ALU.mult, ALU.bypass, [fmn], ["qT"])
            if bis:
                k_stage(bis, n)

        def pg_stage(n):
            for g in range(4):
                fmt, fmn = fm_slot()
                mm(fmt[:, 0:n], wgrp[:, g, :], pooledT[:, g, 0:n], True, True, ["wgrp", "pooledT"], [fmn])
                S.add("vector", lambda e, fmt=fmt, g=g: e.scalar_tensor_tensor(
                    out=prodp[:, g, 0:n], in0=fmt[:, 0:n], scalar=pscale[:, g:g + 1], in1=zpT[:, g, 0:n],
                    op0=ALU.mult, op1=ALU.mult), reads=[fmn, "pscale", "zpT"], writes=["prodp"])

        def att_tail(j0, n, zname, zap):
            tt("gpsimd", attg[0:n, :], attn[0:n, :], zap, ALU.mult, ["attn", zname], ["attg"])
            for m in range(4):
                tr(bank0[:, 512 + m * 128:512 + m * 128 + n], attg[0:n, m * 128:(m + 1) * 128], ident[0:n, 0:n],
                   ["attg", "ident"], ["bank0"])
            act(attT[:, :, j0:j0 + n], bank0[:, 512:1024].rearrange("p (m t) -> p m t", m=4)[:, :, 0:n], AF.Copy,
                ["bank0"], ["attT"])

        def pool_stage(bis):
            for g in range(4):
                fmt, fmn = fm_slot()
                for j, bi in enumerate(bis):
                    o = fmt[:, j * 128:(j + 1) * 128]
                    mm(o, uring[:, (bi - 1) % 3, g * 128:(g + 1) * 128], mcp[:, g, 0, :], True, False,
                       ["u%d" % ((bi - 1) % 3), "mcp"], [fmn])
                    if bi == 1:
                        mm(o, uring[:, bi % 3, g * 128:(g + 1) * 128], m0[:, g, :], False, True, ["u%d" % (bi % 3), "m0"], [fmn])
                    else:
                        mm(o, uring[:, bi % 3, g * 128:(g + 1) * 128], mcp[:, g, 1, :], False, True,
                           ["u%d" % (bi % 3), "mcp"], [fmn])
                if bis[0] == 1:
                    tt("vector", pooledT[:, g, 0:128], fmt[:, 0:128], cinv0[:, g, :], ALU.mult, [fmn, "cinv0"], ["pooledT"])
                    vcopy("vector", pooledT[:, g, 128:256], fmt[:, 128:256], [fmn], ["pooledT"])
                else:
                    act(pooledT[:, g, :], fmt[:, :], AF.Copy, [fmn], ["pooledT"])
            pg_stage(CT)

        def gpga_group(m, which, n):
            c0 = (GP0 if which == 0 else GA0) + m * 128
            f, nme = feat_group(n, win, "w_gp" if which == 0 else "w_ga", 8, c0, lambda k: xT[:, k, 0:n], ["xT"])
            act(sgall[:, m, which, 0:n], f[:, 0:n], AF.Tanh, [nme], ["sg%d_%d" % (m, which)], scale=0.5)

        def gate_steps(m, which, n):
            st_ = {}

            def s0():
                c0 = (GP0 if which == 0 else GA0) + m * 128
                st_["f"], st_["n"] = feat_group(n, win, "w_gp" if which == 0 else "w_ga", 8, c0,
                                                lambda k: xT[:, k, 0:n], ["xT"])

            def s1():
                act(sgall[:, m, which, 0:n], st_["f"][:, 0:n], AF.Tanh, [st_["n"]], ["sg%d_%d" % (m, which)], scale=0.5)
            return [s0, s1]

        def norm_steps(slot, ntok):
            xsn, xnn = "xs%d" % slot, "xn%d" % slot
            ssc = small[0:ntok, slot:slot + 1]
            rsc = small[0:ntok, 2 + slot:3 + slot]

            def s0():
                act(junk[0:ntok, :], xs[slot][0:ntok, :], AF.Square, [xsn], ["junk", "ss%d" % slot], accum_out=ssc)

            def s1():
                tsc("gpsimd", rsc, ssc, 1.0 / D, 1e-6, ALU.mult, ALU.add, ["ss%d" % slot], ["rs%d" % slot])
                tt("gpsimd", rsc, rsc, mhalf[0:ntok, :], ALU.pow, ["rs%d" % slot, "mhalf"], ["rs%d" % slot])

            def s2():
                act(xn[slot][0:ntok, :], xs[slot][0:ntok, :], AF.Copy, [xsn, "rs%d" % slot], [xnn], scale=rsc)
            return [s0, s1, s2]

        def preload_table(func):
            act(small[:, 17:18], small[:, 16:17], func, ["dmy_in"], ["dmy_out"])

        STB = [((stp[:, 0, :], stp[:, 1, :]), ("stp", "stp"), stp), ((stp2[:, 0, :], stp2[:, 1, :]), ("tm0", "tm1"), stp2)]

        def attn_chunk(bis, fill, drain):
            items = [(j, bi, g) for j, bi in enumerate(bis) for g in range(2)]

            def ST(i):
                j, bi, g = items[i]
                cs = slice(j * 128, (j + 1) * 128)
                hs = slice(g * 64, (g + 1) * 64)
                gs = slice(g * 4, (g + 1) * 4)
                (b0, b1), (n0, n1), bfull = STB[i % 2]
                pv_, cu_ = (bi - 1) % 3, bi % 3
                mm(b0, kring[hs, pv_, :], qT[hs, :, cs], True, True, ["k%d" % pv_, "qT"], [n0])
                mm(b1, kring[hs, cu_, :], qT[hs, :, cs], True, True, ["k%d" % cu_, "qT"], [n1])
                if bi == 1:
                    tt("vector", b0, b0, biasT0[:, gs, :], ALU.add, [n0, "biasT0"], [n0])
                    tt("vector", b1, b1, biasT[:, 1, gs, :], ALU.add, [n1, "biasT"], [n1])
                else:
                    tt("vector", bfull[:, :, :], bfull[:, :, :], biasT[:, :, gs, :], ALU.add, [n0, n1, "biasT"], [n0, n1])
                act(PT[:, i % 2, :, :], bfull[:, :, :], AF.Exp, [n0, n1], ["PT%d" % (i % 2)])

            def PV(i):
                j, bi, g = items[i]
                gs = slice(g * 4, (g + 1) * 4)
                for jh in range(4):
                    for kb in range(2):
                        rs_ = (bi - 1 + kb) % 3
                        mm(opv[:, jh, :], PT[:, i % 2, kb, jh * 128:(jh + 1) * 128], vring[:, rs_, g, :], kb == 0, kb == 1,
                           ["PT%d" % (i % 2), "vring%d" % rs_], ["opv"])
                tt("vector", dtmp[:, gs], opv[:, :, 64], sinkexp[:, gs], ALU.add, ["opv", "sinkexp"], ["dtmp%d" % g])
                S.add("vector", lambda e, gs=gs: e.reciprocal(out=rden[:, gs], in_=dtmp[:, gs]),
                      reads=["dtmp%d" % g], writes=["rden%d" % g])
                tt("vector", attn[:, g * 256:(g + 1) * 256].rearrange("p (h d) -> p h d", h=4), opv[:, :, 0:64],
                   rden[:, gs].unsqueeze(2).broadcast_to([128, 4, 64]), ALU.mult, ["opv", "rden%d" % g], ["attn"])

            def tail(j):
                att_tail(j * 128, 128, "zat%d" % j, zat[:, j, :])

            ni = len(items)
            ST(0); fill(2)
            ST(1); fill(2)
            for i in range(ni):
                PV(i); fill(2)
                if i + 2 < ni:
                    ST(i + 2); fill(2)
                if i % 2 == 1:
                    tail(i // 2)
                    fill(2)
            drain()

        def merge_stage(n, have_sg=True):
            for m in range(8):
                if not have_sg:
                    gpga_group(m, 0, n)
                    gpga_group(m, 1, n)
                ts_ = m % 2
                n0_, n1_ = "t0" if ts_ == 0 else "t0b", "t1" if ts_ == 0 else "t1b"
                fbp, nbp = feat_group(n, wbrp, "wbrp", 4, m * 128, lambda k: prodp[:, k, 0:n], ["prodp"])
                S.add("vector", lambda e, fbp=fbp, m=m, ts_=ts_: e.scalar_tensor_tensor(
                    out=t12[:, ts_, 0, 0:n], in0=sgall[:, m, 0, 0:n], scalar=1.0, in1=fbp[:, 0:n], op0=ALU.add, op1=ALU.mult),
                    reads=[nbp, "sg%d_0" % m], writes=[n0_])
                fba, nba = feat_group(n, wbra, "wbra", 4, m * 128, lambda k: attT[:, k, 0:n], ["attT"])
                S.add("vector", lambda e, fba=fba, m=m, ts_=ts_: e.scalar_tensor_tensor(
                    out=t12[:, ts_, 1, 0:n], in0=sgall[:, m, 1, 0:n], scalar=1.0, in1=fba[:, 0:n], op0=ALU.add, op1=ALU.mult),
                    reads=[nba, "sg%d_1" % m], writes=[n1_])
                tt("gpsimd", mergedT[:, m, 0:n], t12[:, ts_, 0, 0:n], t12[:, ts_, 1, 0:n], ALU.add, [n0_, n1_], ["mergedT"])

        def fm_full():
            return alloc_bank()

        def final_parts(src_ap, dst_ap, j, ntok, key):
            cs = slice(j * 128, j * 128 + ntok)
            sl = final_i[0] % 2
            final_i[0] += 1
            xrb, xrn = (xr, "xr") if sl == 0 else (xr2, "xr2")
            r, rn = rr[sl], "r%d" % sl

            dma(xrb[0:ntok, :], src_ap, [], [xrn], xrn)

            st_ = {}

            def half_mm(e_):
                tmt, tmn = fm_full()
                st_[e_] = (tmt, tmn)
                for k in range(8):
                    mm(tmt[0:ntok, :], mergedT[:, k, cs], wout[:, k, e_ * 512:(e_ + 1) * 512], k == 0, k == 7,
                       ["mergedT", "wout"], [tmn])

            def half_ev(e_):
                tmt, tmn = st_[e_]
                S.add("vector", lambda e: e.scalar_tensor_tensor(
                    out=r[0:ntok, e_ * 512:(e_ + 1) * 512], in0=tmt[0:ntok, :], scalar=0.5,
                    in1=xrb[0:ntok, e_ * 512:(e_ + 1) * 512], op0=ALU.mult, op1=ALU.add), reads=[tmn, xrn], writes=[rn])

            ssc = small[0:ntok, 4 + sl:5 + sl]
            rsc = small[0:ntok, 6 + sl:7 + sl]

            def s0():
                half_mm(0)

            def s1():
                half_ev(0)
                half_mm(1)

            def s2():
                half_ev(1)

            def s3():
                act(junk[0:ntok, :], r[0:ntok, :], AF.Square, [rn], ["junk", "fs%d" % sl], accum_out=ssc)

            def s4():
                tsc("gpsimd", rsc, ssc, 1.0 / D, 1e-6, ALU.mult, ALU.add, ["fs%d" % sl], ["fr%d" % sl])
                tt("gpsimd", rsc, rsc, mhalf[0:ntok, :], ALU.pow, ["fr%d" % sl, "mhalf"], ["fr%d" % sl])

            def s5():
                S.add("vector", lambda e: e.scalar_tensor_tensor(out=r[0:ntok, :], in0=r[0:ntok, :], scalar=rsc, in1=gfin[0:ntok, :],
                                                                 op0=ALU.mult, op1=ALU.mult),
                      reads=[rn, "fr%d" % sl, "gfin"], writes=[rn])
                finals.append(dma(dst_ap, r[0:ntok, :], [rn], [], key + str(sl)))
            return [s0, s1, s2, s3, s4, s5]

        def final_stage(src_ap, dst_ap, j, ntok, key):
            for f_ in final_parts(src_ap, dst_ap, j, ntok, key):
                f_()

        final_i = [0]

        def sample_stage():
            n = NS
            KS = int(os.environ.get("KS", "99"))
            spf = sp_d.rearrange("b i c -> (b i) c")
            dma(xr[:, 0:512], spf[0:128, :], [], ["xr"], "xr")
            dma(xr[0:112, 512:1024], spf[128:240, :], [], ["xr"], "xr")
            act(uring[:, 1, :], xr[:, 0:512], AF.Copy, ["xr"], ["u1"])
            act(uring[0:112, 2, :], xr[0:112, 512:1024], AF.Copy, ["xr"], ["u2"])
            dma(rr[0][:, 0:64], sel_d[0:128, :], [], ["r0"], "st0")
            dma(rr[0][0:112, 64:128], sel_d[128:240, :], [], ["r0"], "st0")
            dma(rr[0][0:NS, 128:192], seld_d, [], ["r0"], "st0")
            vcopy("vector", selb[:, 0, :], rr[0][:, 0:64], ["r0"], ["selb"])
            vcopy("vector", selb[0:112, 1, :], rr[0][0:112, 64:128], ["r0"], ["selb"])
            vcopy("vector", selb[0:NS, 2, :], rr[0][0:NS, 128:192], ["r0"], ["selb"])

            if KS < 1:
                return
            load_norm_T(xs_d, 0, n, 0)
            tmt, tmn = tok_group(0, n, U0, 512)
            vcopy("vector", uo[0:n, :], tmt[0:n, :], [tmn], ["uo"])
            vcopy("vector", uring[0:n, 0, :], tmt[0:n, :], [tmn], ["u0"])
            finals.append(dma(ps_out[:, 14, :], uo[0:n, :], ["uo"], [], "sp1"))
            tmt, tmn = tok_group(0, n, V0, 128)
            vcopy("vector", kvo[0:n, 1, :], tmt[0:n, 0:128], [tmn], ["kvo1"])
            finals.append(dma(vs_out[:, 127, :], kvo[0:n, 1, :], ["kvo1"], ["vs_out"], "sv1"))
            tmt, tmn = tok_group(0, n, K0, 128)
            vcopy("vector", kvo[0:n, 0, :], tmt[0:n, 0:128], [tmn], ["kvo0"])
            finals.append(dma(ks_out[:, 127, :], kvo[0:n, 0, :], ["kvo0"], ["ks_out"], "sk1"))
            tmt, tmn = tok_group(0, n, ZA0, 512)
            act(zat[0:n, 0, :], tmt[0:n, :], AF.Silu, [tmn], ["zat0"])
            if KS < 2:
                return
            feat_stage(None, n)
            for g in range(4):
                gc = slice(g * 128, (g + 1) * 128)
                fmt, fmn = fm_slot()
                mm(fmt[:, 0:n], uring[:, 1, gc], selb[:, 0, g * n:(g + 1) * n], True, False, ["u1", "selb"], [fmn])
                mm(fmt[:, 0:n], uring[0:112, 2, gc], selb[0:112, 2 - 1, g * n:(g + 1) * n], False, False, ["u2", "selb"], [fmn])
                mm(fmt[:, 0:n], uring[0:n, 0, gc], selb[0:n, 2, g * n:(g + 1) * n], False, True, ["u0", "selb"], [fmn])
                act(pooledT[:, g, 0:n], fmt[:, 0:n], AF.Copy, [fmn], ["pooledT"])
            pg_stage(n)
            if KS < 3:
                return
            for hb in range(2):
                bs = slice(hb * 8, (hb + 1) * 8)
                dma(rr[hb][:, :].rearrange("s (b c) -> s b c", b=8), ks_out[bs].rearrange("b s c -> s b c"),
                    ["ks_out"], ["r%d" % hb], "st%d" % hb)
                dma(xs[hb][:, :].rearrange("s (b c) -> s b c", b=8), vs_out[bs].rearrange("b s c -> s b c"),
                    ["vs_out"], ["xs%d" % hb], "xs%d" % hb)
                act(mergedT[:, hb * 4:(hb + 1) * 4, :], rr[hb][:, :].rearrange("s (a c) -> s a c", a=4), AF.Copy, ["r%d" % hb], ["mergedT"])
                vcopy("vector", vaugs[:, bs, :, 0:64], xs[hb][:, :].rearrange("s (b g d) -> s b g d", b=8, g=2),
                      ["xs%d" % hb], ["vaugs"])
            if KS < 4:
                return
            for q4 in range(4):
                for i in range(4):
                    b_ = q4 * 4 + i
                    tr(bank0[:, i * 128:(i + 1) * 128], mergedT[:, b_ // 2, (b_ % 2) * 128:(b_ % 2 + 1) * 128], ident[:, :], ["mergedT", "ident"], ["bank0"])
                vcopy("vector", PT[:, q4 // 2, q4 % 2, :], bank0[:, 0:512], ["bank0"], ["PT%d" % (q4 // 2)])
            if KS < 5:
                return
            preload_table(AF.Exp)
            for b_ in range(n):
                for g in range(2):
                    hs = slice(g * 64, (g + 1) * 64)
                    c0 = b_ * 8 + g * 4
                    mm(stp[:, 0, c0:c0 + 4], PT[hs, b_ // 8, (b_ // 4) % 2, (b_ % 4) * 128:(b_ % 4 + 1) * 128], qTs[hs, b_, :], True, True, ["PT0", "PT1", "qT"], ["stp"])
            tt("vector", stp[:, 0, 0:128].rearrange("p (b h) -> p b h", b=n),
               stp[:, 0, 0:128].rearrange("p (b h) -> p b h", b=n),
               biasS[:, :].unsqueeze(1).broadcast_to([128, n, 8]), ALU.add, ["stp", "biasS"], ["stp"])
            pts = attg[:, 0:128]
            act(pts, stp[:, 0, 0:128], AF.Exp, ["stp"], ["attg"])
            if KS < 6:
                return
            opf, opn = tm_slot()
            for b_ in range(n):
                for g in range(2):
                    c0 = b_ * 8 + g * 4
                    mm(opf[0:65, c0:c0 + 4], vaugs[:, b_, g, :], pts[:, c0:c0 + 4], True, True, ["vaugs", "attg"], [opn])
            if KS < 7:
                return
            tt("vector", nrm[64:65, 0:128].rearrange("p (b h) -> p b h", b=n),
               opf[64:65, 0:128].rearrange("p (b h) -> p b h", b=n),
               sinkexp[64:65, :].unsqueeze(1).broadcast_to([1, n, 8]), ALU.add, [opn, "sinkexp"], ["attn"])
            S.add("vector", lambda e: e.reciprocal(out=nrm[64:65, 128:256], in_=nrm[64:65, 0:128]), reads=["attn"], writes=["attn"])
            tmt, tmn = tm_slot()
            mm(tmt[0:64, 0:128], onesf[64:65, :], nrm[64:65, 128:256], True, True, ["attn", "onesf"], [tmn])
            vcopy("vector", nrm[0:64, 256:384], tmt[0:64, 0:128], [tmn], ["attn"])
            tt("vector", nrm[0:64, 384:512], opf[0:64, 0:128], nrm[0:64, 256:384], ALU.mult, [opn, "attn"], ["attn"])
            if KS < 8:
                return
            tmt, tmn = tm_slot()
            for h in range(8):
                tr(tmt[0:n, h * 64:(h + 1) * 64], nrm[0:64, 384 + h:512:8], identf[0:64, 0:64], ["attn", "identf"], [tmn])
            vcopy("vector", attn[0:n, :], tmt[0:n, :], [tmn], ["attn"])
            if KS < 9:
                return
            att_tail(0, n, "zat0", zat[0:n, 0, :])
            merge_stage(n, have_sg=False)
            final_stage(xs_d, ys_out, 0, n, "o_ys")

        def chunk_blocks(c):
            return [1 + c * CB + j for j in range(CB)]

        def proj_stage(c):
            bis = chunk_blocks(c)
            for j, bi in enumerate(bis):
                tok_stage(bi, j * 128, j, bi == NBLK)
            feat_stage(bis, CT)
            pool_stage(bis)

        bias_tables()
        load_pieces(0, 2)
        load_norm_T(xc[0], 0, 128, 0)
        tok_stage(0, 0, None, False)
        k_stage([0], 128)
        load_x(xc[1], 1, 128)
        load_x(xc[2], 0, 128)
        for j, bi in enumerate(chunk_blocks(0)):
            norm_x(bi % 2, 128)
            transp_x(bi % 2, 128, j * 128)
        side_weights()
        load_pieces(2, 3)
        for bi in chunk_blocks(1):
            load_x(xc[bi], bi % 2, 128)
        proj_stage(0)
        load_pieces(3, 5)
        finals.append(dma(ks_out[:, 0:127, :], ck_d[:, 1:128, :], [], ["ks_out"], "sk0"))
        finals.append(dma(vs_out[:, 0:127, :], cv_d[:, 1:128, :], [], ["vs_out"], "sv0"))
        finals.append(dma(ps_out[:, 0:14, :], sp_d[:, 1:15, :], [], [], "sp0"))
        preload_table(AF.Exp)

        carry = []
        for c in range(NCH):
            bis = chunk_blocks(c)
            nxt = chunk_blocks(c + 1) if c + 1 < NCH else []
            gates = [gate_steps(m, w, CT) for m in range(8) for w in range(2)]
            others = carry + [norm_steps(bi % 2, 128) for bi in nxt]
            carry = []
            queue = []
            while gates or others:
                for _ in range(2):
                    if gates:
                        queue.append(gates.pop(0))
                if others:
                    queue.append(others.pop(0))
            active = []

            def fill(k, max_alloc=1):
                fm_allocs[0] = 0
                for f_ in list(active):
                    f_.pop(0)()
                    if not f_:
                        active.remove(f_)
                started = 0
                while queue and started < k and fm_allocs[0] < max_alloc:
                    f_ = queue.pop(0)
                    f_.pop(0)()
                    started += 1
                    if f_:
                        active.append(f_)

            def drain():
                fm_only[0] = False
                while queue or active:
                    fill(3, 3)

            fm_only[0] = True
            attn_chunk(bis, fill, drain)
            fm_only[0] = False
            preload_table(AF.Silu)
            if c + 2 < NCH:
                for bi in chunk_blocks(c + 2):
                    load_x(xc[bi], bi % 2, 128)
            for j, bi in enumerate(nxt):
                transp_x(bi % 2, 128, j * 128)
            merge_stage(CT)
            if nxt:
                proj_stage(c + 1)
                preload_table(AF.Exp)
            for j, bi in enumerate(bis):
                carry.append(final_parts(xc[bi], y_out[bi - 1], j, 128, "o_y"))
        for f_ in carry:
            for st_ in f_:
                st_()

        sample_stage()

        S.emit(nc, ctx, final_ops=finals)
    return nc


_PROG = None


def kernel(x_prompt, x_sample, cache_k, cache_v, state_pool, rel_bias, g_norm, w_in,
           pool_w_grp, pool_scale, attn_sinks, w_br_pool, w_br_attn, w_out, g_final):
    global _PROG
    f = lambda a: np.ascontiguousarray(np.asarray(a, dtype=np.float32))
    x_prompt, x_sample, cache_k, cache_v, state_pool = map(f, (x_prompt, x_sample, cache_k, cache_v, state_pool))
    eb, mcp, m0, cinv0, sel, seld = _consts()
    B, T = x_prompt.shape[0], x_prompt.shape[1]
    half = T // 2
    shared = dict(
        w_in=f(w_in)[0], rel_bias=f(rel_bias),
        gn=np.ascontiguousarray(f(g_norm)[0].reshape(8, 128).T),
        w_grp=f(pool_w_grp)[0],
        pscale=np.ascontiguousarray(f(pool_scale)[0].reshape(4, 128).T),
        sinks=f(attn_sinks)[0].reshape(1, 8),
        w_br_pool=f(w_br_pool)[0], w_br_attn=f(w_br_attn)[0], w_out=f(w_out)[0],
        g_final=f(g_final).reshape(1, D),
        ident=np.eye(128, dtype=np.float32), eb=eb,
        mcp=mcp.reshape(128, -1), sel=sel.reshape(240, -1), seld=seld.reshape(NS, -1),
    )
    in_maps = []
    for core in range(NCORES):
        b, hf = core // 2, core % 2
        xcore = np.zeros((NBLK + 1, 128, D), np.float32)
        xcore[1:] = x_prompt[b, hf * half:(hf + 1) * half].reshape(NBLK, 128, D)
        if hf == 1:
            xcore[0] = x_prompt[b, half - 128:half]
        m = dict(shared)
        m["xc"] = xcore
        m["m0"] = np.ascontiguousarray(m0[hf].reshape(128, -1))
        m["cinv0"] = np.ascontiguousarray(cinv0[hf].reshape(128, -1))
        m["hmask"] = np.full((128, 1), 0.0 if hf == 1 else NEG, np.float32)
        sl = slice(core * NS, (core + 1) * NS)
        m["xsamp"] = np.ascontiguousarray(x_sample[sl, 0, :])
        m["cache_k"] = np.ascontiguousarray(cache_k[0, sl].reshape(NS, 128, 128))
        m["cache_v"] = np.ascontiguousarray(cache_v[0, sl].reshape(NS, 128, 128))
        m["state_pool"] = np.ascontiguousarray(state_pool[0, sl])
        in_maps.append(m)
    if _PROG is None:
        _PROG = build_program()
    res = run_bass_kernel_spmd(_PROG, in_maps, core_ids=list(range(NCORES)))
    rs = res.results
    y_prompt = np.zeros((B, T, D), np.float32)
    nk = np.zeros((1, B, 128, 2, 64), np.float32)
    nv = np.zeros((1, B, 128, 2, 64), np.float32)
    npool = np.zeros((1, B, 15, 512), np.float32)
    y_s = np.zeros((128, 1, D), np.float32)
    ks = np.zeros((1, 128, 128, 2, 64), np.float32)
    vs = np.zeros((1, 128, 128, 2, 64), np.float32)
    pss = np.zeros((1, 128, 15, 512), np.float32)
    for core in range(NCORES):
        b, hf = core // 2, core % 2
        r = rs[core]
        y_prompt[b, hf * half:(hf + 1) * half] = np.asarray(r["y"]).reshape(half, D)
        if hf == 1:
            nk[0, b] = np.asarray(r["k_new"]).reshape(128, 2, 64)
            nv[0, b] = np.asarray(r["v_new"]).reshape(128, 2, 64)
            npool[0, b] = np.asarray(r["p_new"])
        sl = slice(core * NS, (core + 1) * NS)
        y_s[sl, 0] = np.asarray(r["ys"])
        ks[0, sl] = np.asarray(r["ks_new"]).reshape(NS, 128, 2, 64)
        vs[0, sl] = np.asarray(r["vs_new"]).reshape(NS, 128, 2, 64)
        pss[0, sl] = np.asarray(r["ps_new"])
    return (y_prompt, y_s, nk, nv, npool, ks, vs, pss)
```

```python
import math
import os
from contextlib import ExitStack

import numpy as np
import concourse.bass as bass
import concourse.mybir as mybir
from concourse.bass_utils import run_bass_kernel_spmd

F32 = mybir.dt.float32
BF16 = mybir.dt.bfloat16
AF = mybir.ActivationFunctionType
ALU = mybir.AluOpType

NCORES = 8
D = 1024
NBLK = 16
CB = 2
NCH = NBLK // CB
CT = CB * 128
NS = 16
INC = 4352
U0, ZP0, Q0, K0, V0, ZA0, GP0, GA0 = 0, 512, 1024, 1536, 1664, 1792, 2304, 3328
POOLW = (2, 4, 8, 16)
NEG = -30000.0


class Op:
    __slots__ = ("eng", "fn", "deps", "sig", "sigval", "semkey", "idx")

    def __init__(self, eng, fn):
        self.eng = eng
        self.fn = fn
        self.deps = set()
        self.sig = False
        self.sigval = None
        self.semkey = None


class Sched:
    ENGS = ("tensor", "vector", "scalar", "gpsimd", "sync")

    def __init__(self):
        self.ops = {e: [] for e in self.ENGS}
        self.writers = {}
        self.readers = {}
        self.n = 0
        self.exclusive = set()

    def add(self, eng, fn, reads=(), writes=(), deps=(), semkey=None):
        op = Op(eng, fn)
        op.semkey = semkey
        writes = list(writes) + [r for r in reads if r in self.exclusive]
        reads = [r for r in reads if r not in self.exclusive]
        d = set(x for x in deps if x is not None)
        for r in reads:
            d.update(self.writers.get(r, {}).values())
        for wr in writes:
            d.update(self.writers.get(wr, {}).values())
            d.update(self.readers.get(wr, {}).values())
        for r in reads:
            self.readers.setdefault(r, {})[(eng, semkey)] = op
        for wr in writes:
            self.readers[wr] = {}
            self.writers.setdefault(wr, {})[(eng, semkey)] = op
        d.discard(op)
        if eng == "tensor":
            d = set(x for x in d if x.eng != "tensor")
        op.deps = d
        for x in d:
            x.sig = True
        op.idx = self.n
        self.n += 1
        self.ops[eng].append(op)
        return op

    def emit(self, nc, ctx, final_ops=()):
        eng_sem = {}
        for e in ("tensor", "vector", "scalar", "gpsimd"):
            eng_sem[e] = ctx.enter_context(nc.semaphore("s_" + e))
            c = 0
            for op in self.ops[e]:
                if op.sig and op.semkey is None:
                    c += 1
                    op.sigval = (eng_sem[e], c)
        dma_sem, dma_cnt = {}, {}
        for op in self.ops["sync"] + [o for o in self.ops["gpsimd"] if o.semkey is not None]:
            k = op.semkey
            assert k is not None
            if k not in dma_sem:
                dma_sem[k] = ctx.enter_context(nc.semaphore("d_%d" % len(dma_sem)))
                dma_cnt[k] = 0
            dma_cnt[k] += 16
            op.sigval = (dma_sem[k], dma_cnt[k])
            op.sig = True

        def run(e, h):
            waited = {}
            for op in self.ops[e]:
                for dep in sorted(op.deps, key=lambda x: x.idx):
                    sem, val = dep.sigval
                    if waited.get(id(sem), 0) >= val:
                        continue
                    waited[id(sem)] = val
                    h.wait_ge(sem, val)
                ins = op.fn(h)
                if op.sig:
                    ins.then_inc(op.sigval[0], 16 if op.semkey is not None else 1)

        with nc.Block() as block:
            @block.sync
            def _(h):
                run("sync", h)
                done = {}
                for op in final_ops:
                    sem, val = op.sigval
                    if done.get(id(sem), (None, 0))[1] < val:
                        done[id(sem)] = (sem, val)
                for sem, val in done.values():
                    h.wait_ge(sem, val)

            @block.tensor
            def _(h):
                run("tensor", h)

            @block.vector
            def _(h):
                run("vector", h)

            @block.scalar
            def _(h):
                run("scalar", h)

            @block.gpsimd
            def _(h):
                run("gpsimd", h)


def _t5_bucket(n):
    n = np.maximum(n, 0)
    nf = np.maximum(n, 1).astype(np.float32)
    lb = 16 + (np.log(nf / np.float32(16)) / np.float32(math.log(128 / 16)) * np.float32(16)).astype(np.int32)
    return np.where(n < 16, n, np.minimum(lb, 31))


def _consts():
    rp = np.arange(384)
    rel = rp - 128
    valid = (rel >= 0) & (rel <= 127)
    bk = _t5_bucket(rel)
    eb = np.zeros((33, 384), np.float32)
    eb[bk[valid], rp[valid]] = 1.0
    eb[32, rp[~valid]] = 1.0
    tp = np.arange(128)[:, None]
    t = np.arange(128)[None, :]
    mcp = np.zeros((128, 4, 2, 128), np.float32)
    m0 = np.zeros((2, 128, 4, 128), np.float32)
    cinv0 = np.ones((2, 128, 4, 128), np.float32)
    for g, w in enumerate(POOLW):
        cur = ((tp <= t) & (tp > t - w)).astype(np.float32)
        cur_reg = cur - w * (tp == t)
        prev = (tp - 128 > t - w).astype(np.float32)
        mcp[:, g, 0, :] = prev
        mcp[:, g, 1, :] = cur_reg
        cnt = np.minimum(t + 1, w).astype(np.float32)
        m0[0, :, g, :] = cur - cnt * (tp == t)
        m0[1, :, g, :] = cur_reg
        cinv0[0, :, g, :] = np.broadcast_to(w / cnt, (128, 128))
    sel = np.zeros((240, 4, NS), np.float32)
    for g, w in enumerate(POOLW):
        for b in range(NS):
            for i in range(15):
                if i >= 16 - w:
                    sel[b * 15 + i, g, b] = 1.0
    seld = np.zeros((NS, 4, NS), np.float32)
    for g, w in enumerate(POOLW):
        seld[:, g, :] = np.eye(NS) * (1.0 - w)
    return eb, mcp, m0, cinv0, sel, seld


def build_program():
    nc = bass.Bass("TRN2", target_bir_lowering=False)

    def din(name, shape):
        return nc.dram_tensor(name, list(shape), F32, kind="ExternalInput").ap()

    def dout(name, shape):
        return nc.dram_tensor(name, list(shape), F32, kind="ExternalOutput").ap()

    xc = din("xc", [NBLK + 1, 128, D])
    w_in = din("w_in", [D, INC])
    relb = din("rel_bias", [32, 8])
    gn = din("gn", [128, 8])
    wgrp_d = din("w_grp", [4, 128, 128])
    pscale_d = din("pscale", [128, 4])
    sinks_d = din("sinks", [1, 8])
    wbrp_d = din("w_br_pool", [512, D])
    wbra_d = din("w_br_attn", [512, D])
    wout_d = din("w_out", [D, D])
    gfin_d = din("g_final", [1, D])
    ident_d = din("ident", [128, 128])
    eb_d = din("eb", [33, 384])
    mcp_d = din("mcp", [128, 4 * 2 * 128])
    m0_d = din("m0", [128, 4 * 128])
    cinv0_d = din("cinv0", [128, 4 * 128])
    hmask_d = din("hmask", [128, 1])
    xs_d = din("xsamp", [NS, D])
    ck_d = din("cache_k", [NS, 128, 128])
    cv_d = din("cache_v", [NS, 128, 128])
    sp_d = din("state_pool", [NS, 15, 512])
    sel_d = din("sel", [240, 4 * NS])
    seld_d = din("seld", [NS, 4 * NS])

    y_out = dout("y", [NBLK, 128, D])
    k_out = dout("k_new", [128, 128])
    v_out = dout("v_new", [128, 128])
    p_out = dout("p_new", [15, 512])
    ys_out = dout("ys", [NS, D])
    ks_out = dout("ks_new", [NS, 128, 128])
    vs_out = dout("vs_new", [NS, 128, 128])
    ps_out = dout("ps_new", [NS, 15, 512])
    scr = nc.dram_tensor("scr", [8, 128, 384], F32, kind="Internal").ap()
    scr1 = nc.dram_tensor("scr1", [8, 384], F32, kind="Internal").ap()

    S = Sched()
    S.exclusive = {"bank0", "fm0", "fm1", "tm0", "tm1", "stp", "opv"}
    finals = []
    with ExitStack() as ctx:
        def sb(name, shape, dt=F32):
            return ctx.enter_context(nc.sbuf_tensor("sb_" + name, list(shape), dt))

        def ps(name, shape, dt=F32):
            return ctx.enter_context(nc.psum_tensor("ps_" + name, list(shape), dt))

        win = sb("win", [128, 8, INC], BF16)
        wbrp = sb("wbrp", [128, 4, D], BF16)
        wbra = sb("wbra", [128, 4, D], BF16)
        wout = sb("wout", [128, 8, D], BF16)
        wgrp = sb("wgrp", [128, 4, 128], BF16)
        biasT = sb("biasT", [128, 2, 8, 128])
        biasT0 = sb("biasT0", [128, 8, 128])
        biasS = sb("biasS", [128, 8])
        mcp = sb("mcp", [128, 4, 2, 128], BF16)
        m0 = sb("m0", [128, 4, 128], BF16)
        cinv0 = sb("cinv0", [128, 4, 128])
        ident = sb("ident", [128, 128], BF16)
        identf = sb("identf", [128, 128])
        gfin = sb("gfin", [128, D])
        gnt = sb("gnt", [128, 8])
        pscale = sb("pscale", [128, 4])
        sinkexp = sb("sinkexp", [128, 8])
        hmask = sb("hmask", [128, 1])
        mhalf = sb("mhalf", [128, 1])
        r33 = sb("r33", [33, 8])
        ebt = sb("ebt", [33, 384])
        small = sb("small", [128, 32])
        xs = [sb("xs%d" % i, [128, D]) for i in range(2)]
        xr = sb("xr", [128, D])
        xr2 = sb("xr2", [128, D])
        xn = [sb("xn%d" % i, [128, D], BF16) for i in range(2)]
        junk = sb("junk", [128, D], BF16)
        xT = sb("xT", [128, 8, CT], BF16)
        uring = sb("uring", [128, 3, 512], BF16)
        kring = sb("kring", [128, 3, 128], BF16)
        vring = sb("vring", [128, 3, 2, 65], BF16)
        zat = sb("zat", [128, CB, 512], BF16)
        zpT = sb("zpT", [128, 4, CT], BF16)
        qT = sb("qT", [128, 4, CT], BF16)
        pooledT = sb("pooledT", [128, 4, CT], BF16)
        prodp = sb("prodp", [128, 4, CT], BF16)
        PT = sb("PT", [128, 2, 2, 512], BF16)
        dtmp = sb("dtmp", [128, 8])
        rden = sb("rden", [128, 8])
        attn = sb("attn", [128, 512])
        attg = sb("attg", [128, 512], BF16)
        attT = sb("attT", [128, 4, CT], BF16)
        sgall = sb("sgall", [128, 8, 2, CT], BF16)
        t12 = sb("t12", [128, 2, 2, CT])
        mergedT = sb("mergedT", [128, 8, CT], BF16)
        rr = [sb("r%d" % i, [128, D]) for i in range(2)]
        kvo = sb("kvo", [128, 2, 128])
        uo = sb("uo", [128, 512])
        selb = sb("selb", [128, 3, 4 * NS], BF16)
        qTs = sb("qTs", [128, NS, 4], BF16)
        vaugs = sb("vaugs", [128, NS, 2, 65], BF16)
        nrm = attn
        onesf = sb("onesf", [128, 64])

        bank0 = ps("bank0", [128, 1024], BF16)
        fmb = [ps("fm%d" % i, [128, 512]) for i in range(2)]
        stp2 = ps("stp2", [128, 2, 512])
        tmb = [stp2[:, 0, :], stp2[:, 1, :]]
        stp = ps("stp", [128, 2, 512])
        opv = ps("opv", [128, 4, 65])

        fm_i = [0]

        ACC = [(fmb[0], "fm0"), (tmb[0], "tm0"), (fmb[1], "fm1"), (tmb[1], "tm1")]
        acc_i = [0]

        fm_only = [False]
        fm_allocs = [0]

        def alloc_bank():
            fm_allocs[0] += 1
            if fm_only[0]:
                i = fm_i[0] % 2
                fm_i[0] += 1
                return fmb[i], "fm%d" % i
            i = acc_i[0] % 4
            acc_i[0] += 1
            return ACC[i]

        def acc_slot():
            return alloc_bank()

        def fm_slot():
            t, nme = acc_slot()
            return t[:, 0:CT], nme

        tm_i = [0]

        def tm_slot():
            return acc_slot()

        def dma(out, in_, reads, writes, key, **kw):
            return S.add("sync", lambda e: e.dma_start(out=out, in_=in_, **kw), reads=reads, writes=writes, semkey=key)

        def act(out, in_, func, reads, writes, **kw):
            return S.add("scalar", lambda e: e.activation(out=out, in_=in_, func=func, **kw), reads=reads, writes=writes)

        def vcopy(eng, out, in_, reads, writes):
            return S.add(eng, lambda e: e.tensor_copy(out=out, in_=in_), reads=reads, writes=writes)

        def tt(eng, out, in0, in1, op, reads, writes):
            return S.add(eng, lambda e: e.tensor_tensor(out=out, in0=in0, in1=in1, op=op), reads=reads, writes=writes)

        def tsc(eng, out, in0, s1, s2, op0, op1, reads, writes):
            return S.add(eng, lambda e: e.tensor_scalar(out=out, in0=in0, scalar1=s1, scalar2=s2, op0=op0, op1=op1),
                         reads=reads, writes=writes)

        def mm(out, lhsT, rhs, start, stop, reads, writes):
            return S.add("tensor", lambda e: e.matmul(out, lhsT=lhsT, rhs=rhs, start=start, stop=stop),
                         reads=reads, writes=writes)

        def tr(out, in_, idn, reads, writes):
            return S.add("tensor", lambda e: e.transpose(out=out, in_=in_, identity=idn), reads=reads, writes=writes)

        dma(identf[:], ident_d, [], ["identf"], "c0")
        vcopy("vector", ident[:], identf[:], ["identf"], ["ident"])
        dma(gnt[:], gn, [], ["gnt"], "c1")
        dma(pscale[:], pscale_d, [], ["pscale"], "c2")
        for g, w in enumerate(POOLW):
            tsc("gpsimd", pscale[:, g:g + 1], pscale[:, g:g + 1], 1.0 / w, 1.0, ALU.mult, ALU.mult, ["pscale"], ["pscale"])
        dma(sinkexp[:], sinks_d.partition_broadcast(128), [], ["sinkexp"], "c3")
        act(sinkexp[:], sinkexp[:], AF.Exp, ["sinkexp"], ["sinkexp"])
        dma(gfin[:], gfin_d.partition_broadcast(128), [], ["gfin"], "c4")
        dma(hmask[:], hmask_d, [], ["hmask"], "c5")
        S.add("gpsimd", lambda e: e.memset(mhalf[:], -0.5), writes=["mhalf"])
        S.add("gpsimd", lambda e: e.memset(small[:, 16:18], 0.0), writes=["dmy_in"])
        S.add("gpsimd", lambda e: e.memset(vring[:], 1.0), writes=["vring0", "vring1", "vring2"])
        S.add("gpsimd", lambda e: e.memset(vaugs[:], 1.0), writes=["vaugs"])
        S.add("gpsimd", lambda e: e.memset(onesf[:], 1.0), writes=["onesf"])
        dma(cinv0[:], cinv0_d.rearrange("p (g t) -> p g t", g=4), [], ["cinv0"], "c6")
        dma(rr[0][:, 0:1024], mcp_d, [], ["r0"], "st0")
        vcopy("vector", mcp[:], rr[0][:, 0:1024].rearrange("p (g k t) -> p g k t", g=4, k=2), ["r0"], ["mcp"])
        dma(rr[1][:, 0:512], m0_d, [], ["r1"], "st1")
        vcopy("vector", m0[:], rr[1][:, 0:512].rearrange("p (g t) -> p g t", g=4), ["r1"], ["m0"])

        dma(r33[0:32, :], relb, [], ["r33"], "c7")
        S.add("gpsimd", lambda e: e.memset(r33[32:33, :], NEG), writes=["r33"])
        dma(ebt[:], eb_d, [], ["ebt"], "c8")

        def bias_tables():
            tmt, tmn = tm_slot()
            S.add("tensor", lambda e: e.matmul(tmt[0:8, 0:384], lhsT=r33[:, :], rhs=ebt[:], start=True, stop=True),
                  reads=["r33", "ebt"], writes=[tmn])
            vcopy("vector", uo[0:8, 0:384], tmt[0:8, 0:384], [tmn], ["uo"])
            w1 = S.add("gpsimd", lambda e: e.dma_start(out=scr1, in_=uo[0:8, 0:384]), reads=["uo"], writes=["scr1"], semkey="b0")
            rep = bass.AP(tensor=scr1.tensor, offset=0, ap=[[384, 8], [0, 128], [1, 384]])
            w2 = S.add("gpsimd", lambda e: e.dma_start(out=scr, in_=rep), reads=["scr1"], writes=["scr"], deps=[w1], semkey="b1")
            for kb in range(2):
                src = bass.AP(tensor=scr.tensor, offset=(256 if kb == 0 else 128),
                              ap=[[383, 128], [128 * 384, 8], [1, 128]])
                S.add("gpsimd", lambda e, src=src, kb=kb: e.dma_start(out=biasT[:, kb, :, :], in_=src),
                      reads=["scr"], writes=["biasT"], deps=[w2], semkey="c9")
            srcS = bass.AP(tensor=scr.tensor, offset=255, ap=[[383, 128], [128 * 384, 8], [1, 1]])
            S.add("gpsimd", lambda e: e.dma_start(out=biasS[:].unsqueeze(2), in_=srcS, allow_slow_non_contiguous=True),
                  reads=["scr"], writes=["biasS"], deps=[w2], semkey="c10")
            tsc("vector", biasT0[:], biasT[:, 0, :, :], hmask[:, 0:1], None, ALU.add, ALU.bypass, ["biasT", "hmask"], ["biasT0"])

        cast_i = [0]

        def cast(out, in_, scale_ap, reads, writes):
            i = cast_i[0] % 2
            cast_i[0] += 1
            if i == 1:
                return act(out, in_, AF.Copy, reads, writes, scale=scale_ap)
            return tsc("vector", out, in_, scale_ap, None, ALU.mult, ALU.bypass, reads, writes)

        def f32view(ap2d):
            return ap2d.bitcast(F32)

        stg_big = [(rr[0][:, :], ["r0"], "st0"), (rr[1][:, :], ["r1"], "st1"), (xr[:, :], ["xr"], "xr"),
                   (xr2[:, :], ["xr2"], "xr2"),
                   (f32view(mergedT[:, :, :].rearrange("p a b -> p (a b)")), ["mergedT"], "st4")]
        stg_small = [(f32view(PT[:, 0, :, :].rearrange("p a b -> p (a b)")), ["PT0"], "st5"),
                     (f32view(PT[:, 1, :, :].rearrange("p a b -> p (a b)")), ["PT1"], "st6"),
                     (f32view(attT[:, :, :].rearrange("p a b -> p (a b)")), ["attT"], "st7"),
                     (f32view(prodp[:, :, :].rearrange("p a b -> p (a b)")), ["prodp"], "st8")]
        st_i = [0, 0]

        def stage(n):
            if n <= 512:
                pool_ = stg_small + stg_big
                i = st_i[0] % len(pool_)
                st_i[0] += 1
                return pool_[i]
            i = st_i[1] % len(stg_big)
            st_i[1] += 1
            return stg_big[i]

        def gdma(out, in_, writes, key):
            return S.add("gpsimd", lambda e: e.dma_start(out=out, in_=in_), writes=writes, semkey=key)

        pieces = [(U0, 512, "w_u"), (K0, 256, "w_kv"), (ZA0, 512, "w_za"), (ZP0, 512, "w_zp"), (Q0, 512, "w_q"),
                  (GP0, 1024, "w_gp"), (GA0, 1024, "w_ga")]
        def load_pieces(lo, hi):
          for pi in range(lo, hi):
            c0, n, wn = pieces[pi]
            for k in range(8):
                st, sns, sk = stage(n)
                dma(st[:, 0:n], w_in[k * 128:(k + 1) * 128, c0:c0 + n], sns, sns, sk)
                if wn == "w_q":
                    for g in range(2):
                        src = st[:, g * 256:(g + 1) * 256].rearrange("p (j d) -> p j d", j=4)
                        dst = win[:, k, Q0:Q0 + 512].rearrange("p (j g d) -> p j g d", j=4, g=2)[:, :, g, :]
                        if g == 0:
                            vcopy("vector", dst, src, sns, [wn])
                        else:
                            act(dst, src, AF.Copy, sns, [wn])
                else:
                    cast_i[0] += 1
                    if cast_i[0] % 2 == 0:
                        vcopy("vector", win[:, k, c0:c0 + n], st[:, 0:n], sns, [wn])
                    else:
                        act(win[:, k, c0:c0 + n], st[:, 0:n], AF.Copy, sns, [wn])

        def side_weights():
            gdma(wgrp[:, :, :], wgrp_d.rearrange("g c e -> c g e"), ["wgrp"], "gw0")
            for k in range(4):
                gdma(wbrp[:, k, :], wbrp_d[k * 128:(k + 1) * 128, :], ["wbrp"], "gw1")
            for k in range(4):
                gdma(wbra[:, k, :], wbra_d[k * 128:(k + 1) * 128, :], ["wbra"], "gw2")
            for k in range(8):
                gdma(wout[:, k, :], wout_d[k * 128:(k + 1) * 128, :], ["wout"], "gw3")

        def load_x(src_ap, slot, ntok):
            dma(xs[slot][0:ntok, :], src_ap, [], ["xs%d" % slot], "xs%d" % slot)

        def norm_x(slot, ntok):
            xsn, xnn = "xs%d" % slot, "xn%d" % slot
            ssc = small[0:ntok, slot:slot + 1]
            rsc = small[0:ntok, 2 + slot:3 + slot]
            act(junk[0:ntok, :], xs[slot][0:ntok, :], AF.Square, [xsn], ["junk", "ss%d" % slot], accum_out=ssc)
            tsc("gpsimd", rsc, ssc, 1.0 / D, 1e-6, ALU.mult, ALU.add, ["ss%d" % slot], ["rs%d" % slot])
            tt("gpsimd", rsc, rsc, mhalf[0:ntok, :], ALU.pow, ["rs%d" % slot, "mhalf"], ["rs%d" % slot])
            act(xn[slot][0:ntok, :], xs[slot][0:ntok, :], AF.Copy, [xsn, "rs%d" % slot], [xnn], scale=rsc)

        tx_i = [0]

        def transp_x(slot, ntok, col):
            xnn = "xn%d" % slot
            for k in range(8):
                tr(bank0[:, k * 128:k * 128 + ntok], xn[slot][0:ntok, k * 128:(k + 1) * 128],
                   ident[0:ntok, 0:ntok], [xnn, "ident"], ["bank0"])
            dst = xT[:, :, col:col + ntok]
            srcp = bank0[:, :].rearrange("p (k t) -> p k t", k=8)[:, :, 0:ntok]
            tt("vector", dst, srcp, gnt[:, :].unsqueeze(2).broadcast_to([128, 8, ntok]), ALU.mult,
               ["bank0", "gnt"], ["xT"])

        def load_norm_T(src_ap, slot, ntok, col, preloaded=False):
            if not preloaded:
                load_x(src_ap, slot, ntok)
            norm_x(slot, ntok)
            transp_x(slot, ntok, col)

        WRES = {U0: "w_u", V0: "w_kv", K0: "w_kv", ZA0: "w_za"}

        def tok_group(col, ntok, c0, ncol):
            tmt, tmn = tm_slot()
            for k in range(8):
                mm(tmt[0:ntok, 0:ncol], xT[:, k, col:col + ntok], win[:, k, c0:c0 + ncol], k == 0, k == 7,
                   ["xT", WRES[c0]], [tmn])
            return tmt, tmn

        def feat_group(ncols, wt, wn, nk, c0, rhs_fn, rnames):
            fmt, fmn = fm_slot()
            for k in range(nk):
                mm(fmt[:, 0:ncols], wt[:, k, c0:c0 + 128], rhs_fn(k), k == 0, k == nk - 1, [wn] + rnames, [fmn])
            return fmt, fmn

        def tok_stage(bi, col, j, last):
            us = bi % 3
            tmt, tmn = tok_group(col, 128, U0, 512)
            act(uring[:, us, :], tmt[:, :], AF.Copy, [tmn], ["u%d" % us])
            if last:
                vcopy("vector", uo[:], tmt[:, :], [tmn], ["uo"])
                finals.append(dma(p_out, uo[113:128, :], ["uo"], [], "o_p"))
            tmt, tmn = tok_group(col, 128, V0, 128)
            vcopy("vector", vring[:, us, :, 0:64], tmt[:, 0:128].rearrange("p (g d) -> p g d", g=2), [tmn], ["vring%d" % us])
            if last:
                vcopy("vector", kvo[:, 1, :], tmt[:, 0:128], [tmn], ["kvo1"])
                finals.append(dma(v_out, kvo[:, 1, :], ["kvo1"], [], "o_v"))
                tmt, tmn = tok_group(col, 128, K0, 128)
                vcopy("vector", kvo[:, 0, :], tmt[:, 0:128], [tmn], ["kvo0"])
                finals.append(dma(k_out, kvo[:, 0, :], ["kvo0"], [], "o_k"))
            if j is not None:
                tmt, tmn = tok_group(col, 128, ZA0, 512)
                act(zat[:, j, :], tmt[:, :], AF.Silu, [tmn], ["zat%d" % j])

        def k_stage(bis, ncols):
            fmt, fmn = feat_group(ncols, win, "w_kv", 8, K0, lambda k: xT[:, k, 0:ncols], ["xT"])
            for j, bi in enumerate(bis):
                vcopy("vector", kring[:, bi % 3, :], fmt[:, j * 128:(j + 1) * 128], [fmn], ["k%d" % (bi % 3)])

        def feat_stage(bis, n):
            for m in range(4):
                fmt, fmn = feat_group(n, win, "w_zp", 8, ZP0 + m * 128, lambda k: xT[:, k, 0:n], ["xT"])
                act(zpT[:, m, 0:n], fmt[:, 0:n], AF.Silu, [fmn], ["zpT"])
            for m in range(4):
                fmt, fmn = feat_group(n, win, "w_q", 8, Q0 + m * 128, lambda k: xT[:, k, 0:n], ["xT"])
                qdst = qT[:, m, 0:n] if bis is not None else qTs[:, :, m]
                tsc("vector", qdst, fmt[:, 0:n], 0.125, None, ALU.mult, ALU.bypass, [fmn], ["qT"])
            if bis:
                k_stage(bis, n)

        def pg_stage(n):
            for g in range(4):
                fmt, fmn = fm_slot()
                mm(fmt[:, 0:n], wgrp[:, g, :], pooledT[:, g, 0:n], True, True, ["wgrp", "pooledT"], [fmn])
                S.add("vector", lambda e, fmt=fmt, g=g: e.scalar_tensor_tensor(
                    out=prodp[:, g, 0:n], in0=fmt[:, 0:n], scalar=pscale[:, g:g + 1], in1=zpT[:, g, 0:n],
                    op0=ALU.mult, op1=ALU.mult), reads=[fmn, "pscale", "zpT"], writes=["prodp"])

        def att_tail(j0, n, zname, zap):
            tt("gpsimd", attg[0:n, :], attn[0:n, :], zap, ALU.mult, ["attn", zname], ["attg"])
            for m in range(4):
                tr(bank0[:, 512 + m * 128:512 + m * 128 + n], attg[0:n, m * 128:(m + 1) * 128], ident[0:n, 0:n],
                   ["attg", "ident"], ["bank0"])
            act(attT[:, :, j0:j0 + n], bank0[:, 512:1024].rearrange("p (m t) -> p m t", m=4)[:, :, 0:n], AF.Copy,
                ["bank0"], ["attT"])

        def pool_stage(bis):
            for g in range(4):
                fmt, fmn = fm_slot()
                for j, bi in enumerate(bis):
                    o = fmt[:, j * 128:(j + 1) * 128]
                    mm(o, uring[:, (bi - 1) % 3, g * 128:(g + 1) * 128], mcp[:, g, 0, :], True, False,
                       ["u%d" % ((bi - 1) % 3), "mcp"], [fmn])
                    if bi == 1:
                        mm(o, uring[:, bi % 3, g * 128:(g + 1) * 128], m0[:, g, :], False, True, ["u%d" % (bi % 3), "m0"], [fmn])
                    else:
                        mm(o, uring[:, bi % 3, g * 128:(g + 1) * 128], mcp[:, g, 1, :], False, True,
                           ["u%d" % (bi % 3), "mcp"], [fmn])
                if bis[0] == 1:
                    tt("vector", pooledT[:, g, 0:128], fmt[:, 0:128], cinv0[:, g, :], ALU.mult, [fmn, "cinv0"], ["pooledT"])
                    vcopy("vector", pooledT[:, g, 128:256], fmt[:, 128:256], [fmn], ["pooledT"])
                else:
                    act(pooledT[:, g, :], fmt[:, :], AF.Copy, [fmn], ["pooledT"])
            pg_stage(CT)

        def gpga_group(m, which, n):
            c0 = (GP0 if which == 0 else GA0) + m * 128
            f, nme = feat_group(n, win, "w_gp" if which == 0 else "w_ga", 8, c0, lambda k: xT[:, k, 0:n], ["xT"])
            act(sgall[:, m, which, 0:n], f[:, 0:n], AF.Tanh, [nme], ["sg%d_%d" % (m, which)], scale=0.5)

        def gate_steps(m, n):
            st_ = {}

            def s0():
                t, nme = alloc_bank()
                st_["t"], st_["n"] = t, nme
                for which in range(2):
                    c0 = (GP0 if which == 0 else GA0) + m * 128
                    wn = "w_gp" if which == 0 else "w_ga"
                    for k in range(8):
                        mm(t[:, which * CT:which * CT + n], win[:, k, c0:c0 + 128], xT[:, k, 0:n], k == 0, k == 7,
                           [wn, "xT"], [nme])

            def s1():
                act(sgall[:, m, :, 0:n], st_["t"][:, 0:2 * CT].rearrange("p (w c) -> p w c", w=2)[:, :, 0:n], AF.Tanh,
                    [st_["n"]], ["sg%d_0" % m, "sg%d_1" % m], scale=0.5)
            return [s0, s1]

        def norm_steps(slot, ntok):
            xsn, xnn = "xs%d" % slot, "xn%d" % slot
            ssc = small[0:ntok, slot:slot + 1]
            rsc = small[0:ntok, 2 + slot:3 + slot]

            def s0():
                act(junk[0:ntok, :], xs[slot][0:ntok, :], AF.Square, [xsn], ["junk", "ss%d" % slot], accum_out=ssc)

            def s1():
                tsc("gpsimd", rsc, ssc, 1.0 / D, 1e-6, ALU.mult, ALU.add, ["ss%d" % slot], ["rs%d" % slot])
                tt("gpsimd", rsc, rsc, mhalf[0:ntok, :], ALU.pow, ["rs%d" % slot, "mhalf"], ["rs%d" % slot])

            def s2():
                act(xn[slot][0:ntok, :], xs[slot][0:ntok, :], AF.Copy, [xsn, "rs%d" % slot], [xnn], scale=rsc)
            return [s0, s1, s2]

        def preload_table(func):
            act(small[:, 17:18], small[:, 16:17], func, ["dmy_in"], ["dmy_out"])

        STB = [((stp[:, 0, :], stp[:, 1, :]), ("stp", "stp"), stp), ((stp2[:, 0, :], stp2[:, 1, :]), ("tm0", "tm1"), stp2)]

        def attn_chunk(bis, fill, drain):
            items = [(j, bi, g) for j, bi in enumerate(bis) for g in range(2)]

            def ST(i):
                j, bi, g = items[i]
                cs = slice(j * 128, (j + 1) * 128)
                hs = slice(g * 64, (g + 1) * 64)
                gs = slice(g * 4, (g + 1) * 4)
                (b0, b1), (n0, n1), bfull = STB[i % 2]
                pv_, cu_ = (bi - 1) % 3, bi % 3
                mm(b0, kring[hs, pv_, :], qT[hs, :, cs], True, True, ["k%d" % pv_, "qT"], [n0])
                mm(b1, kring[hs, cu_, :], qT[hs, :, cs], True, True, ["k%d" % cu_, "qT"], [n1])
                if bi == 1:
                    tt("vector", b0, b0, biasT0[:, gs, :], ALU.add, [n0, "biasT0"], [n0])
                    tt("vector", b1, b1, biasT[:, 1, gs, :], ALU.add, [n1, "biasT"], [n1])
                else:
                    tt("vector", bfull[:, :, :], bfull[:, :, :], biasT[:, :, gs, :], ALU.add, [n0, n1, "biasT"], [n0, n1])
                act(PT[:, i % 2, :, :], bfull[:, :, :], AF.Exp, [n0, n1], ["PT%d" % (i % 2)])

            def PV(i):
                j, bi, g = items[i]
                gs = slice(g * 4, (g + 1) * 4)
                for jh in range(4):
                    for kb in range(2):
                        rs_ = (bi - 1 + kb) % 3
                        mm(opv[:, jh, :], PT[:, i % 2, kb, jh * 128:(jh + 1) * 128], vring[:, rs_, g, :], kb == 0, kb == 1,
                           ["PT%d" % (i % 2), "vring%d" % rs_], ["opv"])
                tt("vector", dtmp[:, gs], opv[:, :, 64], sinkexp[:, gs], ALU.add, ["opv", "sinkexp"], ["dtmp%d" % g])
                S.add("vector", lambda e, gs=gs: e.reciprocal(out=rden[:, gs], in_=dtmp[:, gs]),
                      reads=["dtmp%d" % g], writes=["rden%d" % g])
                tt("vector", attn[:, g * 256:(g + 1) * 256].rearrange("p (h d) -> p h d", h=4), opv[:, :, 0:64],
                   rden[:, gs].unsqueeze(2).broadcast_to([128, 4, 64]), ALU.mult, ["opv", "rden%d" % g], ["attn"])

            def tail(j):
                att_tail(j * 128, 128, "zat%d" % j, zat[:, j, :])

            ni = len(items)
            ST(0); fill(2)
            ST(1); fill(2)
            for i in range(ni):
                PV(i); fill(2)
                if i + 2 < ni:
                    ST(i + 2); fill(2)
                if i % 2 == 1:
                    tail(i // 2)
                    fill(2)
            drain()

        def merge_stage(n, have_sg=True):
            for m in range(8):
                if not have_sg:
                    gpga_group(m, 0, n)
                    gpga_group(m, 1, n)
                ts_ = m % 2
                n0_, n1_ = "t0" if ts_ == 0 else "t0b", "t1" if ts_ == 0 else "t1b"
                fbp, nbp = feat_group(n, wbrp, "wbrp", 4, m * 128, lambda k: prodp[:, k, 0:n], ["prodp"])
                S.add("vector", lambda e, fbp=fbp, m=m, ts_=ts_: e.scalar_tensor_tensor(
                    out=t12[:, ts_, 0, 0:n], in0=sgall[:, m, 0, 0:n], scalar=1.0, in1=fbp[:, 0:n], op0=ALU.add, op1=ALU.mult),
                    reads=[nbp, "sg%d_0" % m], writes=[n0_])
                fba, nba = feat_group(n, wbra, "wbra", 4, m * 128, lambda k: attT[:, k, 0:n], ["attT"])
                S.add("vector", lambda e, fba=fba, m=m, ts_=ts_: e.scalar_tensor_tensor(
                    out=t12[:, ts_, 1, 0:n], in0=sgall[:, m, 1, 0:n], scalar=1.0, in1=fba[:, 0:n], op0=ALU.add, op1=ALU.mult),
                    reads=[nba, "sg%d_1" % m], writes=[n1_])
                tt("gpsimd", mergedT[:, m, 0:n], t12[:, ts_, 0, 0:n], t12[:, ts_, 1, 0:n], ALU.add, [n0_, n1_], ["mergedT"])

        def fm_full():
            return alloc_bank()

        def final_parts(src_ap, dst_ap, j, ntok, key):
            cs = slice(j * 128, j * 128 + ntok)
            sl = final_i[0] % 2
            final_i[0] += 1
            xrb, xrn = (xr, "xr") if sl == 0 else (xr2, "xr2")
            r, rn = rr[sl], "r%d" % sl

            dma(xrb[0:ntok, :], src_ap, [], [xrn], xrn)

            st_ = {}

            def half_mm(e_):
                tmt, tmn = fm_full()
                st_[e_] = (tmt, tmn)
                for k in range(8):
                    mm(tmt[0:ntok, :], mergedT[:, k, cs], wout[:, k, e_ * 512:(e_ + 1) * 512], k == 0, k == 7,
                       ["mergedT", "wout"], [tmn])

            def half_ev(e_):
                tmt, tmn = st_[e_]
                S.add("vector", lambda e: e.scalar_tensor_tensor(
                    out=r[0:ntok, e_ * 512:(e_ + 1) * 512], in0=tmt[0:ntok, :], scalar=0.5,
                    in1=xrb[0:ntok, e_ * 512:(e_ + 1) * 512], op0=ALU.mult, op1=ALU.add), reads=[tmn, xrn], writes=[rn])

            ssc = small[0:ntok, 4 + sl:5 + sl]
            rsc = small[0:ntok, 6 + sl:7 + sl]

            def s0():
                half_mm(0)

            def s1():
                half_ev(0)
                half_mm(1)

            def s2():
                half_ev(1)

            def s3():
                act(junk[0:ntok, :], r[0:ntok, :], AF.Square, [rn], ["junk", "fs%d" % sl], accum_out=ssc)

            def s4():
                tsc("gpsimd", rsc, ssc, 1.0 / D, 1e-6, ALU.mult, ALU.add, ["fs%d" % sl], ["fr%d" % sl])
                tt("gpsimd", rsc, rsc, mhalf[0:ntok, :], ALU.pow, ["fr%d" % sl, "mhalf"], ["fr%d" % sl])

            def s5():
                S.add("vector", lambda e: e.scalar_tensor_tensor(out=r[0:ntok, :], in0=r[0:ntok, :], scalar=rsc, in1=gfin[0:ntok, :],
                                                                 op0=ALU.mult, op1=ALU.mult),
                      reads=[rn, "fr%d" % sl, "gfin"], writes=[rn])
                finals.append(dma(dst_ap, r[0:ntok, :], [rn], [], key + str(sl)))
            return [s0, s1, s2, s3, s4, s5]

        def final_stage(src_ap, dst_ap, j, ntok, key):
            for f_ in final_parts(src_ap, dst_ap, j, ntok, key):
                f_()

        final_i = [0]

        def sample_stage():
            n = NS
            KS = int(os.environ.get("KS", "99"))
            spf = sp_d.rearrange("b i c -> (b i) c")
            dma(xr[:, 0:512], spf[0:128, :], [], ["xr"], "xr")
            dma(xr[0:112, 512:1024], spf[128:240, :], [], ["xr"], "xr")
            act(uring[:, 1, :], xr[:, 0:512], AF.Copy, ["xr"], ["u1"])
            act(uring[0:112, 2, :], xr[0:112, 512:1024], AF.Copy, ["xr"], ["u2"])
            dma(rr[0][:, 0:64], sel_d[0:128, :], [], ["r0"], "st0")
            dma(rr[0][0:112, 64:128], sel_d[128:240, :], [], ["r0"], "st0")
            dma(rr[0][0:NS, 128:192], seld_d, [], ["r0"], "st0")
            vcopy("vector", selb[:, 0, :], rr[0][:, 0:64], ["r0"], ["selb"])
            vcopy("vector", selb[0:112, 1, :], rr[0][0:112, 64:128], ["r0"], ["selb"])
            vcopy("vector", selb[0:NS, 2, :], rr[0][0:NS, 128:192], ["r0"], ["selb"])

            if KS < 1:
                return
            load_norm_T(xs_d, 0, n, 0)
            tmt, tmn = tok_group(0, n, U0, 512)
            vcopy("vector", uo[0:n, :], tmt[0:n, :], [tmn], ["uo"])
            vcopy("vector", uring[0:n, 0, :], tmt[0:n, :], [tmn], ["u0"])
            finals.append(dma(ps_out[:, 14, :], uo[0:n, :], ["uo"], [], "sp1"))
            tmt, tmn = tok_group(0, n, V0, 128)
            vcopy("vector", kvo[0:n, 1, :], tmt[0:n, 0:128], [tmn], ["kvo1"])
            finals.append(dma(vs_out[:, 127, :], kvo[0:n, 1, :], ["kvo1"], ["vs_out"], "sv1"))
            tmt, tmn = tok_group(0, n, K0, 128)
            vcopy("vector", kvo[0:n, 0, :], tmt[0:n, 0:128], [tmn], ["kvo0"])
            finals.append(dma(ks_out[:, 127, :], kvo[0:n, 0, :], ["kvo0"], ["ks_out"], "sk1"))
            tmt, tmn = tok_group(0, n, ZA0, 512)
            act(zat[0:n, 0, :], tmt[0:n, :], AF.Silu, [tmn], ["zat0"])
            if KS < 2:
                return
            feat_stage(None, n)
            for g in range(4):
                gc = slice(g * 128, (g + 1) * 128)
                fmt, fmn = fm_slot()
                mm(fmt[:, 0:n], uring[:, 1, gc], selb[:, 0, g * n:(g + 1) * n], True, False, ["u1", "selb"], [fmn])
                mm(fmt[:, 0:n], uring[0:112, 2, gc], selb[0:112, 2 - 1, g * n:(g + 1) * n], False, False, ["u2", "selb"], [fmn])
                mm(fmt[:, 0:n], uring[0:n, 0, gc], selb[0:n, 2, g * n:(g + 1) * n], False, True, ["u0", "selb"], [fmn])
                act(pooledT[:, g, 0:n], fmt[:, 0:n], AF.Copy, [fmn], ["pooledT"])
            pg_stage(n)
            if KS < 3:
                return
            for hb in range(2):
                bs = slice(hb * 8, (hb + 1) * 8)
                dma(rr[hb][:, :].rearrange("s (b c) -> s b c", b=8), ks_out[bs].rearrange("b s c -> s b c"),
                    ["ks_out"], ["r%d" % hb], "st%d" % hb)
                dma(xs[hb][:, :].rearrange("s (b c) -> s b c", b=8), vs_out[bs].rearrange("b s c -> s b c"),
                    ["vs_out"], ["xs%d" % hb], "xs%d" % hb)
                act(mergedT[:, hb * 4:(hb + 1) * 4, :], rr[hb][:, :].rearrange("s (a c) -> s a c", a=4), AF.Copy, ["r%d" % hb], ["mergedT"])
                vcopy("vector", vaugs[:, bs, :, 0:64], xs[hb][:, :].rearrange("s (b g d) -> s b g d", b=8, g=2),
                      ["xs%d" % hb], ["vaugs"])
            if KS < 4:
                return
            for q4 in range(4):
                for i in range(4):
                    b_ = q4 * 4 + i
                    tr(bank0[:, i * 128:(i + 1) * 128], mergedT[:, b_ // 2, (b_ % 2) * 128:(b_ % 2 + 1) * 128], ident[:, :], ["mergedT", "ident"], ["bank0"])
                vcopy("vector", PT[:, q4 // 2, q4 % 2, :], bank0[:, 0:512], ["bank0"], ["PT%d" % (q4 // 2)])
            if KS < 5:
                return
            preload_table(AF.Exp)
            for b_ in range(n):
                for g in range(2):
                    hs = slice(g * 64, (g + 1) * 64)
                    c0 = b_ * 8 + g * 4
                    mm(stp[:, 0, c0:c0 + 4], PT[hs, b_ // 8, (b_ // 4) % 2, (b_ % 4) * 128:(b_ % 4 + 1) * 128], qTs[hs, b_, :], True, True, ["PT0", "PT1", "qT"], ["stp"])
            tt("vector", stp[:, 0, 0:128].rearrange("p (b h) -> p b h", b=n),
               stp[:, 0, 0:128].rearrange("p (b h) -> p b h", b=n),
               biasS[:, :].unsqueeze(1).broadcast_to([128, n, 8]), ALU.add, ["stp", "biasS"], ["stp"])
            pts = attg[:, 0:128]
            act(pts, stp[:, 0, 0:128], AF.Exp, ["stp"], ["attg"])
            if KS < 6:
                return
            opf, opn = tm_slot()
            for b_ in range(n):
                for g in range(2):
                    c0 = b_ * 8 + g * 4
                    mm(opf[0:65, c0:c0 + 4], vaugs[:, b_, g, :], pts[:, c0:c0 + 4], True, True, ["vaugs", "attg"], [opn])
            if KS < 7:
                return
            tt("vector", nrm[64:65, 0:128].rearrange("p (b h) -> p b h", b=n),
               opf[64:65, 0:128].rearrange("p (b h) -> p b h", b=n),
               sinkexp[64:65, :].unsqueeze(1).broadcast_to([1, n, 8]), ALU.add, [opn, "sinkexp"], ["attn"])
            S.add("vector", lambda e: e.reciprocal(out=nrm[64:65, 128:256], in_=nrm[64:65, 0:128]), reads=["attn"], writes=["attn"])
            tmt, tmn = tm_slot()
            mm(tmt[0:64, 0:128], onesf[64:65, :], nrm[64:65, 128:256], True, True, ["attn", "onesf"], [tmn])
            vcopy("vector", nrm[0:64, 256:384], tmt[0:64, 0:128], [tmn], ["attn"])
            tt("vector", nrm[0:64, 384:512], opf[0:64, 0:128], nrm[0:64, 256:384], ALU.mult, [opn, "attn"], ["attn"])
            if KS < 8:
                return
            tmt, tmn = tm_slot()
            for h in range(8):
                tr(tmt[0:n, h * 64:(h + 1) * 64], nrm[0:64, 384 + h:512:8], identf[0:64, 0:64], ["attn", "identf"], [tmn])
            vcopy("vector", attn[0:n, :], tmt[0:n, :], [tmn], ["attn"])
            if KS < 9:
                return
            att_tail(0, n, "zat0", zat[0:n, 0, :])
            merge_stage(n, have_sg=False)
            final_stage(xs_d, ys_out, 0, n, "o_ys")

        def chunk_blocks(c):
            return [1 + c * CB + j for j in range(CB)]

        def proj_stage(c):
            bis = chunk_blocks(c)
            for j, bi in enumerate(bis):
                tok_stage(bi, j * 128, j, bi == NBLK)
            feat_stage(bis, CT)
            pool_stage(bis)

        bias_tables()
        load_pieces(0, 2)
        load_norm_T(xc[0], 0, 128, 0)
        tok_stage(0, 0, None, False)
        k_stage([0], 128)
        load_x(xc[1], 1, 128)
        load_x(xc[2], 0, 128)
        for j, bi in enumerate(chunk_blocks(0)):
            norm_x(bi % 2, 128)
            transp_x(bi % 2, 128, j * 128)
        side_weights()
        load_pieces(2, 5)
        for bi in chunk_blocks(1):
            load_x(xc[bi], bi % 2, 128)
        proj_stage(0)
        load_pieces(5, 7)
        finals.append(dma(ks_out[:, 0:127, :], ck_d[:, 1:128, :], [], ["ks_out"], "sk0"))
        finals.append(dma(vs_out[:, 0:127, :], cv_d[:, 1:128, :], [], ["vs_out"], "sv0"))
        finals.append(dma(ps_out[:, 0:14, :], sp_d[:, 1:15, :], [], [], "sp0"))
        preload_table(AF.Exp)

        carry = []
        for c in range(NCH):
            bis = chunk_blocks(c)
            nxt = chunk_blocks(c + 1) if c + 1 < NCH else []
            gates = [gate_steps(m, CT) for m in range(8)]
            others = carry + [norm_steps(bi % 2, 128) for bi in nxt]
            carry = []
            queue = []
            while gates or others:
                if gates:
                    queue.append(gates.pop(0))
                if others:
                    queue.append(others.pop(0))
            active = []

            def fill(k, max_alloc=1):
                fm_allocs[0] = 0
                for f_ in list(active):
                    f_.pop(0)()
                    if not f_:
                        active.remove(f_)
                started = 0
                while queue and started < k and fm_allocs[0] < max_alloc:
                    f_ = queue.pop(0)
                    f_.pop(0)()
                    started += 1
                    if f_:
                        active.append(f_)

            def drain():
                fm_only[0] = False
                while queue or active:
                    fill(3, 3)

            fm_only[0] = True
            attn_chunk(bis, fill, drain)
            fm_only[0] = False
            preload_table(AF.Silu)
            if c + 2 < NCH:
                for bi in chunk_blocks(c + 2):
                    load_x(xc[bi], bi % 2, 128)
            for j, bi in enumerate(nxt):
                transp_x(bi % 2, 128, j * 128)
            merge_stage(CT)
            if nxt:
                proj_stage(c + 1)
                preload_table(AF.Exp)
            for j, bi in enumerate(bis):
                carry.append(final_parts(xc[bi], y_out[bi - 1], j, 128, "o_y"))
        for f_ in carry:
            for st_ in f_:
                st_()

        sample_stage()

        S.emit(nc, ctx, final_ops=finals)
    return nc


_PROG = None


def kernel(x_prompt, x_sample, cache_k, cache_v, state_pool, rel_bias, g_norm, w_in,
           pool_w_grp, pool_scale, attn_sinks, w_br_pool, w_br_attn, w_out, g_final):
    global _PROG
    f = lambda a: np.ascontiguousarray(np.asarray(a, dtype=np.float32))
    x_prompt, x_sample, cache_k, cache_v, state_pool = map(f, (x_prompt, x_sample, cache_k, cache_v, state_pool))
    eb, mcp, m0, cinv0, sel, seld = _consts()
    B, T = x_prompt.shape[0], x_prompt.shape[1]
    half = T // 2
    shared = dict(
        w_in=f(w_in)[0], rel_bias=f(rel_bias),
        gn=np.ascontiguousarray(f(g_norm)[0].reshape(8, 128).T),
        w_grp=f(pool_w_grp)[0],
        pscale=np.ascontiguousarray(f(pool_scale)[0].reshape(4, 128).T),
        sinks=f(attn_sinks)[0].reshape(1, 8),
        w_br_pool=f(w_br_pool)[0], w_br_attn=f(w_br_attn)[0], w_out=f(w_out)[0],
        g_final=f(g_final).reshape(1, D),
        ident=np.eye(128, dtype=np.float32), eb=eb,
        mcp=mcp.reshape(128, -1), sel=sel.reshape(240, -1), seld=seld.reshape(NS, -1),
    )
    in_maps = []
    for core in range(NCORES):
        b, hf = core // 2, core % 2
        xcore = np.zeros((NBLK + 1, 128, D), np.float32)
        xcore[1:] = x_prompt[b, hf * half:(hf + 1) * half].reshape(NBLK, 128, D)
        if hf == 1:
            xcore[0] = x_prompt[b, half - 128:half]
        m = dict(shared)
        m["xc"] = xcore
        m["m0"] = np.ascontiguousarray(m0[hf].reshape(128, -1))
        m["cinv0"] = np.ascontiguousarray(cinv0[hf].reshape(128, -1))
        m["hmask"] = np.full((128, 1), 0.0 if hf == 1 else NEG, np.float32)
        sl = slice(core * NS, (core + 1) * NS)
        m["xsamp"] = np.ascontiguousarray(x_sample[sl, 0, :])
        m["cache_k"] = np.ascontiguousarray(cache_k[0, sl].reshape(NS, 128, 128))
        m["cache_v"] = np.ascontiguousarray(cache_v[0, sl].reshape(NS, 128, 128))
        m["state_pool"] = np.ascontiguousarray(state_pool[0, sl])
        in_maps.append(m)
    if _PROG is None:
        _PROG = build_program()
    res = run_bass_kernel_spmd(_PROG, in_maps, core_ids=list(range(NCORES)))
    rs = res.results
    y_prompt = np.zeros((B, T, D), np.float32)
    nk = np.zeros((1, B, 128, 2, 64), np.float32)
    nv = np.zeros((1, B, 128, 2, 64), np.float32)
    npool = np.zeros((1, B, 15, 512), np.float32)
    y_s = np.zeros((128, 1, D), np.float32)
    ks = np.zeros((1, 128, 128, 2, 64), np.float32)
    vs = np.zeros((1, 128, 128, 2, 64), np.float32)
    pss = np.zeros((1, 128, 15, 512), np.float32)
    for core in range(NCORES):
        b, hf = core // 2, core % 2
        r = rs[core]
        y_prompt[b, hf * half:(hf + 1) * half] = np.asarray(r["y"]).reshape(half, D)
        if hf == 1:
            nk[0, b] = np.asarray(r["k_new"]).reshape(128, 2, 64)
            nv[0, b] = np.asarray(r["v_new"]).reshape(128, 2, 64)
            npool[0, b] = np.asarray(r["p_new"])
        sl = slice(core * NS, (core + 1) * NS)
        y_s[sl, 0] = np.asarray(r["ys"])
        ks[0, sl] = np.asarray(r["ks_new"]).reshape(NS, 128, 2, 64)
        vs[0, sl] = np.asarray(r["vs_new"]).reshape(NS, 128, 2, 64)
        pss[0, sl] = np.asarray(r["ps_new"])
    return (y_prompt, y_s, nk, nv, npool, ks, vs, pss)
```

```python
import math
import os
from contextlib import ExitStack

import numpy as np
import concourse.bass as bass
import concourse.mybir as mybir
from concourse.bass_utils import run_bass_kernel_spmd

F32 = mybir.dt.float32
BF16 = mybir.dt.bfloat16
AF = mybir.ActivationFunctionType
ALU = mybir.AluOpType

NCORES = 8
D = 1024
NBLK = 16
CB = 2
NCH = NBLK // CB
CT = CB * 128
NS = 16
INC = 4352
U0, ZP0, Q0, K0, V0, ZA0, GP0, GA0 = 0, 512, 1024, 1536, 1664, 1792, 2304, 3328
POOLW = (2, 4, 8, 16)
NEG = -30000.0


class Op:
    __slots__ = ("eng", "fn", "deps", "sig", "sigval", "semkey", "idx")

    def __init__(self, eng, fn):
        self.eng = eng
        self.fn = fn
        self.deps = set()
        self.sig = False
        self.sigval = None
        self.semkey = None


class Sched:
    ENGS = ("tensor", "vector", "scalar", "gpsimd", "sync")

    def __init__(self):
        self.ops = {e: [] for e in self.ENGS}
        self.writers = {}
        self.readers = {}
        self.n = 0
        self.exclusive = set()

    def add(self, eng, fn, reads=(), writes=(), deps=(), semkey=None):
        op = Op(eng, fn)
        op.semkey = semkey
        writes = list(writes) + [r for r in reads if r in self.exclusive]
        reads = [r for r in reads if r not in self.exclusive]
        d = set(x for x in deps if x is not None)
        for r in reads:
            d.update(self.writers.get(r, {}).values())
        for wr in writes:
            d.update(self.writers.get(wr, {}).values())
            d.update(self.readers.get(wr, {}).values())
        for r in reads:
            self.readers.setdefault(r, {})[(eng, semkey)] = op
        for wr in writes:
            self.readers[wr] = {}
            self.writers.setdefault(wr, {})[(eng, semkey)] = op
        d.discard(op)
        if eng == "tensor":
            d = set(x for x in d if x.eng != "tensor")
        op.deps = d
        for x in d:
            x.sig = True
        op.idx = self.n
        self.n += 1
        self.ops[eng].append(op)
        return op

    def emit(self, nc, ctx, final_ops=()):
        eng_sem = {}
        for e in ("tensor", "vector", "scalar", "gpsimd"):
            eng_sem[e] = ctx.enter_context(nc.semaphore("s_" + e))
            c = 0
            for op in self.ops[e]:
                if op.sig and op.semkey is None:
                    c += 1
                    op.sigval = (eng_sem[e], c)
        dma_sem, dma_cnt = {}, {}
        for op in self.ops["sync"] + [o for o in self.ops["gpsimd"] if o.semkey is not None]:
            k = op.semkey
            assert k is not None
            if k not in dma_sem:
                dma_sem[k] = ctx.enter_context(nc.semaphore("d_%d" % len(dma_sem)))
                dma_cnt[k] = 0
            dma_cnt[k] += 16
            op.sigval = (dma_sem[k], dma_cnt[k])
            op.sig = True

        def run(e, h):
            waited = {}
            for op in self.ops[e]:
                for dep in sorted(op.deps, key=lambda x: x.idx):
                    sem, val = dep.sigval
                    if waited.get(id(sem), 0) >= val:
                        continue
                    waited[id(sem)] = val
                    h.wait_ge(sem, val)
                ins = op.fn(h)
                if op.sig:
                    ins.then_inc(op.sigval[0], 16 if op.semkey is not None else 1)

        with nc.Block() as block:
            @block.sync
            def _(h):
                run("sync", h)
                done = {}
                for op in final_ops:
                    sem, val = op.sigval
                    if done.get(id(sem), (None, 0))[1] < val:
                        done[id(sem)] = (sem, val)
                for sem, val in done.values():
                    h.wait_ge(sem, val)

            @block.tensor
            def _(h):
                run("tensor", h)

            @block.vector
            def _(h):
                run("vector", h)

            @block.scalar
            def _(h):
                run("scalar", h)

            @block.gpsimd
            def _(h):
                run("gpsimd", h)


def _t5_bucket(n):
    n = np.maximum(n, 0)
    nf = np.maximum(n, 1).astype(np.float32)
    lb = 16 + (np.log(nf / np.float32(16)) / np.float32(math.log(128 / 16)) * np.float32(16)).astype(np.int32)
    return np.where(n < 16, n, np.minimum(lb, 31))


def _consts():
    rp = np.arange(384)
    rel = rp - 128
    valid = (rel >= 0) & (rel <= 127)
    bk = _t5_bucket(rel)
    eb = np.zeros((33, 384), np.float32)
    eb[bk[valid], rp[valid]] = 1.0
    eb[32, rp[~valid]] = 1.0
    tp = np.arange(128)[:, None]
    t = np.arange(128)[None, :]
    mcp = np.zeros((128, 4, 2, 128), np.float32)
    m0 = np.zeros((2, 128, 4, 128), np.float32)
    cinv0 = np.ones((2, 128, 4, 128), np.float32)
    for g, w in enumerate(POOLW):
        cur = ((tp <= t) & (tp > t - w)).astype(np.float32)
        cur_reg = cur - w * (tp == t)
        prev = (tp - 128 > t - w).astype(np.float32)
        mcp[:, g, 0, :] = prev
        mcp[:, g, 1, :] = cur_reg
        cnt = np.minimum(t + 1, w).astype(np.float32)
        m0[0, :, g, :] = cur - cnt * (tp == t)
        m0[1, :, g, :] = cur_reg
        cinv0[0, :, g, :] = np.broadcast_to(w / cnt, (128, 128))
    sel = np.zeros((240, 4, NS), np.float32)
    for g, w in enumerate(POOLW):
        for b in range(NS):
            for i in range(15):
                if i >= 16 - w:
                    sel[b * 15 + i, g, b] = 1.0
    seld = np.zeros((NS, 4, NS), np.float32)
    for g, w in enumerate(POOLW):
        seld[:, g, :] = np.eye(NS) * (1.0 - w)
    return eb, mcp, m0, cinv0, sel, seld


def build_program():
    nc = bass.Bass("TRN2", target_bir_lowering=False)

    def din(name, shape):
        return nc.dram_tensor(name, list(shape), F32, kind="ExternalInput").ap()

    def dout(name, shape):
        return nc.dram_tensor(name, list(shape), F32, kind="ExternalOutput").ap()

    xc = din("xc", [NBLK + 1, 128, D])
    w_in = din("w_in", [D, INC])
    relb = din("rel_bias", [32, 8])
    gn = din("gn", [128, 8])
    wgrp_d = din("w_grp", [4, 128, 128])
    pscale_d = din("pscale", [128, 4])
    sinks_d = din("sinks", [1, 8])
    wbrp_d = din("w_br_pool", [512, D])
    wbra_d = din("w_br_attn", [512, D])
    wout_d = din("w_out", [D, D])
    gfin_d = din("g_final", [1, D])
    ident_d = din("ident", [128, 128])
    eb_d = din("eb", [33, 384])
    mcp_d = din("mcp", [128, 4 * 2 * 128])
    m0_d = din("m0", [128, 4 * 128])
    cinv0_d = din("cinv0", [128, 4 * 128])
    hmask_d = din("hmask", [128, 1])
    xs_d = din("xsamp", [NS, D])
    ck_d = din("cache_k", [NS, 128, 128])
    cv_d = din("cache_v", [NS, 128, 128])
    sp_d = din("state_pool", [NS, 15, 512])
    sel_d = din("sel", [240, 4 * NS])
    seld_d = din("seld", [NS, 4 * NS])

    y_out = dout("y", [NBLK, 128, D])
    k_out = dout("k_new", [128, 128])
    v_out = dout("v_new", [128, 128])
    p_out = dout("p_new", [15, 512])
    ys_out = dout("ys", [NS, D])
    ks_out = dout("ks_new", [NS, 128, 128])
    vs_out = dout("vs_new", [NS, 128, 128])
    ps_out = dout("ps_new", [NS, 15, 512])
    scr = nc.dram_tensor("scr", [8, 128, 384], F32, kind="Internal").ap()
    scr1 = nc.dram_tensor("scr1", [8, 384], F32, kind="Internal").ap()

    S = Sched()
    S.exclusive = {"bank0", "fm0", "fm1", "tm0", "tm1", "stp", "opv"}
    finals = []
    with ExitStack() as ctx:
        def sb(name, shape, dt=F32):
            return ctx.enter_context(nc.sbuf_tensor("sb_" + name, list(shape), dt))

        def ps(name, shape, dt=F32):
            return ctx.enter_context(nc.psum_tensor("ps_" + name, list(shape), dt))

        win = sb("win", [128, 8, INC], BF16)
        wbrp = sb("wbrp", [128, 4, D], BF16)
        wbra = sb("wbra", [128, 4, D], BF16)
        wout = sb("wout", [128, 8, D], BF16)
        wgrp = sb("wgrp", [128, 4, 128], BF16)
        biasT = sb("biasT", [128, 2, 8, 128])
        biasT0 = sb("biasT0", [128, 8, 128])
        biasS = sb("biasS", [128, 8])
        mcp = sb("mcp", [128, 4, 2, 128], BF16)
        m0 = sb("m0", [128, 4, 128], BF16)
        cinv0 = sb("cinv0", [128, 4, 128])
        ident = sb("ident", [128, 128], BF16)
        identf = sb("identf", [128, 128])
        gfin = sb("gfin", [128, D])
        gnt = sb("gnt", [128, 8])
        pscale = sb("pscale", [128, 4])
        sinkexp = sb("sinkexp", [128, 8])
        hmask = sb("hmask", [128, 1])
        mhalf = sb("mhalf", [128, 1])
        r33 = sb("r33", [33, 8])
        ebt = sb("ebt", [33, 384])
        small = sb("small", [128, 32])
        xs = [sb("xs%d" % i, [128, D]) for i in range(2)]
        xr = sb("xr", [128, D])
        xr2 = sb("xr2", [128, D])
        xn = [sb("xn%d" % i, [128, D], BF16) for i in range(2)]
        junk = sb("junk", [128, D], BF16)
        xT = sb("xT", [128, 8, CT], BF16)
        uring = sb("uring", [128, 3, 512], BF16)
        kring = sb("kring", [128, 3, 128], BF16)
        vring = sb("vring", [128, 3, 2, 65], BF16)
        zat = sb("zat", [128, CB, 512], BF16)
        zpT = sb("zpT", [128, 4, CT], BF16)
        qT = sb("qT", [128, 4, CT], BF16)
        pooledT = sb("pooledT", [128, 4, CT], BF16)
        prodp = sb("prodp", [128, 4, CT], BF16)
        PT = sb("PT", [128, 2, 2, 512], BF16)
        dtmp = sb("dtmp", [128, 8])
        rden = sb("rden", [128, 8])
        attn = sb("attn", [128, 512])
        attg = sb("attg", [128, 512], BF16)
        attT = sb("attT", [128, 4, CT], BF16)
        sgall = sb("sgall", [128, 8, 2, CT], BF16)
        t12 = sb("t12", [128, 2, 2, CT])
        mergedT = sb("mergedT", [128, 8, CT], BF16)
        rr = [sb("r%d" % i, [128, D]) for i in range(2)]
        kvo = sb("kvo", [128, 2, 128])
        uo = sb("uo", [128, 512])
        selb = sb("selb", [128, 3, 4 * NS], BF16)
        qTs = sb("qTs", [128, NS, 4], BF16)
        vaugs = sb("vaugs", [128, NS, 2, 65], BF16)
        nrm = attn
        onesf = sb("onesf", [128, 64])

        bank0 = ps("bank0", [128, 1024], BF16)
        fmb = [ps("fm%d" % i, [128, 512]) for i in range(2)]
        stp2 = ps("stp2", [128, 2, 512])
        tmb = [stp2[:, 0, :], stp2[:, 1, :]]
        stp = ps("stp", [128, 2, 512])
        opv = ps("opv", [128, 4, 65])

        fm_i = [0]

        ACC = [(fmb[0], "fm0"), (tmb[0], "tm0"), (fmb[1], "fm1"), (tmb[1], "tm1")]
        acc_i = [0]

        fm_only = [False]
        fm_allocs = [0]

        def alloc_bank():
            fm_allocs[0] += 1
            if fm_only[0]:
                i = fm_i[0] % 2
                fm_i[0] += 1
                return fmb[i], "fm%d" % i
            i = acc_i[0] % 4
            acc_i[0] += 1
            return ACC[i]

        def acc_slot():
            return alloc_bank()

        def fm_slot():
            t, nme = acc_slot()
            return t[:, 0:CT], nme

        tm_i = [0]

        def tm_slot():
            return acc_slot()

        def dma(out, in_, reads, writes, key, **kw):
            return S.add("sync", lambda e: e.dma_start(out=out, in_=in_, **kw), reads=reads, writes=writes, semkey=key)

        def act(out, in_, func, reads, writes, **kw):
            return S.add("scalar", lambda e: e.activation(out=out, in_=in_, func=func, **kw), reads=reads, writes=writes)

        def vcopy(eng, out, in_, reads, writes):
            return S.add(eng, lambda e: e.tensor_copy(out=out, in_=in_), reads=reads, writes=writes)

        def tt(eng, out, in0, in1, op, reads, writes):
            return S.add(eng, lambda e: e.tensor_tensor(out=out, in0=in0, in1=in1, op=op), reads=reads, writes=writes)

        def tsc(eng, out, in0, s1, s2, op0, op1, reads, writes):
            return S.add(eng, lambda e: e.tensor_scalar(out=out, in0=in0, scalar1=s1, scalar2=s2, op0=op0, op1=op1),
                         reads=reads, writes=writes)

        def mm(out, lhsT, rhs, start, stop, reads, writes):
            return S.add("tensor", lambda e: e.matmul(out, lhsT=lhsT, rhs=rhs, start=start, stop=stop),
                         reads=reads, writes=writes)

        def tr(out, in_, idn, reads, writes):
            return S.add("tensor", lambda e: e.transpose(out=out, in_=in_, identity=idn), reads=reads, writes=writes)

        dma(identf[:], ident_d, [], ["identf"], "c0")
        vcopy("vector", ident[:], identf[:], ["identf"], ["ident"])
        dma(gnt[:], gn, [], ["gnt"], "c1")
        dma(pscale[:], pscale_d, [], ["pscale"], "c2")
        for g, w in enumerate(POOLW):
            tsc("gpsimd", pscale[:, g:g + 1], pscale[:, g:g + 1], 1.0 / w, 1.0, ALU.mult, ALU.mult, ["pscale"], ["pscale"])
        dma(sinkexp[:], sinks_d.partition_broadcast(128), [], ["sinkexp"], "c3")
        act(sinkexp[:], sinkexp[:], AF.Exp, ["sinkexp"], ["sinkexp"])
        dma(gfin[:], gfin_d.partition_broadcast(128), [], ["gfin"], "c4")
        dma(hmask[:], hmask_d, [], ["hmask"], "c5")
        S.add("gpsimd", lambda e: e.memset(mhalf[:], -0.5), writes=["mhalf"])
        S.add("gpsimd", lambda e: e.memset(small[:, 16:18], 0.0), writes=["dmy_in"])
        S.add("gpsimd", lambda e: e.memset(vring[:], 1.0), writes=["vring0", "vring1", "vring2"])
        S.add("gpsimd", lambda e: e.memset(vaugs[:], 1.0), writes=["vaugs"])
        S.add("gpsimd", lambda e: e.memset(onesf[:], 1.0), writes=["onesf"])
        dma(cinv0[:], cinv0_d.rearrange("p (g t) -> p g t", g=4), [], ["cinv0"], "c6")
        dma(rr[0][:, 0:1024], mcp_d, [], ["r0"], "st0")
        vcopy("vector", mcp[:], rr[0][:, 0:1024].rearrange("p (g k t) -> p g k t", g=4, k=2), ["r0"], ["mcp"])
        dma(rr[1][:, 0:512], m0_d, [], ["r1"], "st1")
        vcopy("vector", m0[:], rr[1][:, 0:512].rearrange("p (g t) -> p g t", g=4), ["r1"], ["m0"])

        dma(r33[0:32, :], relb, [], ["r33"], "c7")
        S.add("gpsimd", lambda e: e.memset(r33[32:33, :], NEG), writes=["r33"])
        dma(ebt[:], eb_d, [], ["ebt"], "c8")

        def bias_tables():
            tmt, tmn = tm_slot()
            S.add("tensor", lambda e: e.matmul(tmt[0:8, 0:384], lhsT=r33[:, :], rhs=ebt[:], start=True, stop=True),
                  reads=["r33", "ebt"], writes=[tmn])
            vcopy("vector", uo[0:8, 0:384], tmt[0:8, 0:384], [tmn], ["uo"])
            w1 = S.add("gpsimd", lambda e: e.dma_start(out=scr1, in_=uo[0:8, 0:384]), reads=["uo"], writes=["scr1"], semkey="b0")
            rep = bass.AP(tensor=scr1.tensor, offset=0, ap=[[384, 8], [0, 128], [1, 384]])
            w2 = S.add("gpsimd", lambda e: e.dma_start(out=scr, in_=rep), reads=["scr1"], writes=["scr"], deps=[w1], semkey="b1")
            for kb in range(2):
                src = bass.AP(tensor=scr.tensor, offset=(256 if kb == 0 else 128),
                              ap=[[383, 128], [128 * 384, 8], [1, 128]])
                S.add("gpsimd", lambda e, src=src, kb=kb: e.dma_start(out=biasT[:, kb, :, :], in_=src),
                      reads=["scr"], writes=["biasT"], deps=[w2], semkey="c9")
            srcS = bass.AP(tensor=scr.tensor, offset=255, ap=[[383, 128], [128 * 384, 8], [1, 1]])
            S.add("gpsimd", lambda e: e.dma_start(out=biasS[:].unsqueeze(2), in_=srcS, allow_slow_non_contiguous=True),
                  reads=["scr"], writes=["biasS"], deps=[w2], semkey="c10")
            tsc("vector", biasT0[:], biasT[:, 0, :, :], hmask[:, 0:1], None, ALU.add, ALU.bypass, ["biasT", "hmask"], ["biasT0"])

        cast_i = [0]

        def cast(out, in_, scale_ap, reads, writes):
            i = cast_i[0] % 2
            cast_i[0] += 1
            if i == 1:
                return act(out, in_, AF.Copy, reads, writes, scale=scale_ap)
            return tsc("vector", out, in_, scale_ap, None, ALU.mult, ALU.bypass, reads, writes)

        def f32view(ap2d):
            return ap2d.bitcast(F32)

        stg_big = [(rr[0][:, :], ["r0"], "st0"), (rr[1][:, :], ["r1"], "st1"), (xr[:, :], ["xr"], "xr"),
                   (xr2[:, :], ["xr2"], "xr2"),
                   (f32view(mergedT[:, :, :].rearrange("p a b -> p (a b)")), ["mergedT"], "st4")]
        stg_small = [(f32view(PT[:, 0, :, :].rearrange("p a b -> p (a b)")), ["PT0"], "st5"),
                     (f32view(PT[:, 1, :, :].rearrange("p a b -> p (a b)")), ["PT1"], "st6"),
                     (f32view(attT[:, :, :].rearrange("p a b -> p (a b)")), ["attT"], "st7"),
                     (f32view(prodp[:, :, :].rearrange("p a b -> p (a b)")), ["prodp"], "st8")]
        st_i = [0, 0]

        def stage(n):
            if n <= 512:
                pool_ = stg_small + stg_big
                i = st_i[0] % len(pool_)
                st_i[0] += 1
                return pool_[i]
            i = st_i[1] % len(stg_big)
            st_i[1] += 1
            return stg_big[i]

        def gdma(out, in_, writes, key):
            return S.add("gpsimd", lambda e: e.dma_start(out=out, in_=in_), writes=writes, semkey=key)

        pieces = [(U0, 512, "w_u"), (K0, 256, "w_kv"), (ZA0, 512, "w_za"), (ZP0, 512, "w_zp"), (Q0, 512, "w_q"),
                  (GP0, 1024, "w_gp"), (GA0, 1024, "w_ga")]
        def load_pieces(lo, hi):
          for pi in range(lo, hi):
            c0, n, wn = pieces[pi]
            for k in range(8):
                st, sns, sk = stage(n)
                dma(st[:, 0:n], w_in[k * 128:(k + 1) * 128, c0:c0 + n], sns, sns, sk)
                if wn == "w_q":
                    for g in range(2):
                        src = st[:, g * 256:(g + 1) * 256].rearrange("p (j d) -> p j d", j=4)
                        dst = win[:, k, Q0:Q0 + 512].rearrange("p (j g d) -> p j g d", j=4, g=2)[:, :, g, :]
                        if g == 0:
                            vcopy("vector", dst, src, sns, [wn])
                        else:
                            act(dst, src, AF.Copy, sns, [wn])
                else:
                    cast_i[0] += 1
                    if cast_i[0] % 2 == 0:
                        vcopy("vector", win[:, k, c0:c0 + n], st[:, 0:n], sns, [wn])
                    else:
                        act(win[:, k, c0:c0 + n], st[:, 0:n], AF.Copy, sns, [wn])

        def side_weights():
            gdma(wgrp[:, :, :], wgrp_d.rearrange("g c e -> c g e"), ["wgrp"], "gw0")
            for k in range(4):
                gdma(wbrp[:, k, :], wbrp_d[k * 128:(k + 1) * 128, :], ["wbrp"], "gw1")
            for k in range(4):
                gdma(wbra[:, k, :], wbra_d[k * 128:(k + 1) * 128, :], ["wbra"], "gw2")

        def side_weights_late():
            for k in range(8):
                gdma(wout[:, k, :], wout_d[k * 128:(k + 1) * 128, :], ["wout"], "gw3")

        def load_x(src_ap, slot, ntok):
            dma(xs[slot][0:ntok, :], src_ap, [], ["xs%d" % slot], "xs%d" % slot)

        def norm_x(slot, ntok):
            xsn, xnn = "xs%d" % slot, "xn%d" % slot
            ssc = small[0:ntok, slot:slot + 1]
            rsc = small[0:ntok, 2 + slot:3 + slot]
            act(junk[0:ntok, :], xs[slot][0:ntok, :], AF.Square, [xsn], ["junk", "ss%d" % slot], accum_out=ssc)
            tsc("gpsimd", rsc, ssc, 1.0 / D, 1e-6, ALU.mult, ALU.add, ["ss%d" % slot], ["rs%d" % slot])
            tt("gpsimd", rsc, rsc, mhalf[0:ntok, :], ALU.pow, ["rs%d" % slot, "mhalf"], ["rs%d" % slot])
            act(xn[slot][0:ntok, :], xs[slot][0:ntok, :], AF.Copy, [xsn, "rs%d" % slot], [xnn], scale=rsc)

        tx_i = [0]

        def transp_x(slot, ntok, col):
            xnn = "xn%d" % slot
            for k in range(8):
                tr(bank0[:, k * 128:k * 128 + ntok], xn[slot][0:ntok, k * 128:(k + 1) * 128],
                   ident[0:ntok, 0:ntok], [xnn, "ident"], ["bank0"])
            dst = xT[:, :, col:col + ntok]
            srcp = bank0[:, :].rearrange("p (k t) -> p k t", k=8)[:, :, 0:ntok]
            tt("vector", dst, srcp, gnt[:, :].unsqueeze(2).broadcast_to([128, 8, ntok]), ALU.mult,
               ["bank0", "gnt"], ["xT"])

        def load_norm_T(src_ap, slot, ntok, col, preloaded=False):
            if not preloaded:
                load_x(src_ap, slot, ntok)
            norm_x(slot, ntok)
            transp_x(slot, ntok, col)

        WRES = {U0: "w_u", V0: "w_kv", K0: "w_kv", ZA0: "w_za"}

        def tok_group(col, ntok, c0, ncol):
            tmt, tmn = tm_slot()
            for k in range(8):
                mm(tmt[0:ntok, 0:ncol], xT[:, k, col:col + ntok], win[:, k, c0:c0 + ncol], k == 0, k == 7,
                   ["xT", WRES[c0]], [tmn])
            return tmt, tmn

        def feat_group(ncols, wt, wn, nk, c0, rhs_fn, rnames):
            fmt, fmn = fm_slot()
            for k in range(nk):
                mm(fmt[:, 0:ncols], wt[:, k, c0:c0 + 128], rhs_fn(k), k == 0, k == nk - 1, [wn] + rnames, [fmn])
            return fmt, fmn

        def tok_stage(bi, col, j, last):
            us = bi % 3
            tmt, tmn = tok_group(col, 128, U0, 512)
            act(uring[:, us, :], tmt[:, :], AF.Copy, [tmn], ["u%d" % us])
            if last:
                vcopy("vector", uo[:], tmt[:, :], [tmn], ["uo"])
                finals.append(dma(p_out, uo[113:128, :], ["uo"], [], "o_p"))
            tmt, tmn = tok_group(col, 128, V0, 128)
            vcopy("vector", vring[:, us, :, 0:64], tmt[:, 0:128].rearrange("p (g d) -> p g d", g=2), [tmn], ["vring%d" % us])
            if last:
                vcopy("vector", kvo[:, 1, :], tmt[:, 0:128], [tmn], ["kvo1"])
                finals.append(dma(v_out, kvo[:, 1, :], ["kvo1"], [], "o_v"))
                tmt, tmn = tok_group(col, 128, K0, 128)
                vcopy("vector", kvo[:, 0, :], tmt[:, 0:128], [tmn], ["kvo0"])
                finals.append(dma(k_out, kvo[:, 0, :], ["kvo0"], [], "o_k"))
            if j is not None:
                tmt, tmn = tok_group(col, 128, ZA0, 512)
                act(zat[:, j, :], tmt[:, :], AF.Silu, [tmn], ["zat%d" % j])

        def k_stage(bis, ncols):
            fmt, fmn = feat_group(ncols, win, "w_kv", 8, K0, lambda k: xT[:, k, 0:ncols], ["xT"])
            for j, bi in enumerate(bis):
                vcopy("vector", kring[:, bi % 3, :], fmt[:, j * 128:(j + 1) * 128], [fmn], ["k%d" % (bi % 3)])

        def feat_stage(bis, n):
            for m in range(4):
                fmt, fmn = feat_group(n, win, "w_zp", 8, ZP0 + m * 128, lambda k: xT[:, k, 0:n], ["xT"])
                act(zpT[:, m, 0:n], fmt[:, 0:n], AF.Silu, [fmn], ["zpT"])
            for m in range(4):
                fmt, fmn = feat_group(n, win, "w_q", 8, Q0 + m * 128, lambda k: xT[:, k, 0:n], ["xT"])
                qdst = qT[:, m, 0:n] if bis is not None else qTs[:, :, m]
                tsc("vector", qdst, fmt[:, 0:n], 0.125, None, ALU.mult, ALU.bypass, [fmn], ["qT"])
            if bis:
                k_stage(bis, n)

        def pg_stage(n):
            for g in range(4):
                fmt, fmn = fm_slot()
                mm(fmt[:, 0:n], wgrp[:, g, :], pooledT[:, g, 0:n], True, True, ["wgrp", "pooledT"], [fmn])
                S.add("vector", lambda e, fmt=fmt, g=g: e.scalar_tensor_tensor(
                    out=prodp[:, g, 0:n], in0=fmt[:, 0:n], scalar=pscale[:, g:g + 1], in1=zpT[:, g, 0:n],
                    op0=ALU.mult, op1=ALU.mult), reads=[fmn, "pscale", "zpT"], writes=["prodp"])

        def att_tail(j0, n, zname, zap):
            tt("gpsimd", attg[0:n, :], attn[0:n, :], zap, ALU.mult, ["attn", zname], ["attg"])
            for m in range(4):
                tr(bank0[:, 512 + m * 128:512 + m * 128 + n], attg[0:n, m * 128:(m + 1) * 128], ident[0:n, 0:n],
                   ["attg", "ident"], ["bank0"])
            act(attT[:, :, j0:j0 + n], bank0[:, 512:1024].rearrange("p (m t) -> p m t", m=4)[:, :, 0:n], AF.Copy,
                ["bank0"], ["attT"])

        def pool_stage(bis):
            for g in range(4):
                fmt, fmn = fm_slot()
                for j, bi in enumerate(bis):
                    o = fmt[:, j * 128:(j + 1) * 128]
                    mm(o, uring[:, (bi - 1) % 3, g * 128:(g + 1) * 128], mcp[:, g, 0, :], True, False,
                       ["u%d" % ((bi - 1) % 3), "mcp"], [fmn])
                    if bi == 1:
                        mm(o, uring[:, bi % 3, g * 128:(g + 1) * 128], m0[:, g, :], False, True, ["u%d" % (bi % 3), "m0"], [fmn])
                    else:
                        mm(o, uring[:, bi % 3, g * 128:(g + 1) * 128], mcp[:, g, 1, :], False, True,
                           ["u%d" % (bi % 3), "mcp"], [fmn])
                if bis[0] == 1:
                    tt("vector", pooledT[:, g, 0:128], fmt[:, 0:128], cinv0[:, g, :], ALU.mult, [fmn, "cinv0"], ["pooledT"])
                    vcopy("vector", pooledT[:, g, 128:256], fmt[:, 128:256], [fmn], ["pooledT"])
                else:
                    act(pooledT[:, g, :], fmt[:, :], AF.Copy, [fmn], ["pooledT"])
            pg_stage(CT)

        def gpga_group(m, which, n):
            c0 = (GP0 if which == 0 else GA0) + m * 128
            f, nme = feat_group(n, win, "w_gp" if which == 0 else "w_ga", 8, c0, lambda k: xT[:, k, 0:n], ["xT"])
            act(sgall[:, m, which, 0:n], f[:, 0:n], AF.Tanh, [nme], ["sg%d_%d" % (m, which)], scale=0.5)

        def gate_steps(m, n):
            st_ = {}

            def s0():
                t, nme = alloc_bank()
                st_["t"], st_["n"] = t, nme
                for which in range(2):
                    c0 = (GP0 if which == 0 else GA0) + m * 128
                    wn = "w_gp" if which == 0 else "w_ga"
                    for k in range(8):
                        mm(t[:, which * CT:which * CT + n], win[:, k, c0:c0 + 128], xT[:, k, 0:n], k == 0, k == 7,
                           [wn, "xT"], [nme])

            def s1():
                act(sgall[:, m, :, 0:n], st_["t"][:, 0:2 * CT].rearrange("p (w c) -> p w c", w=2)[:, :, 0:n], AF.Tanh,
                    [st_["n"]], ["sg%d_0" % m, "sg%d_1" % m], scale=0.5)
            return [s0, s1]

        def norm_steps(slot, ntok):
            xsn, xnn = "xs%d" % slot, "xn%d" % slot
            ssc = small[0:ntok, slot:slot + 1]
            rsc = small[0:ntok, 2 + slot:3 + slot]

            def s0():
                act(junk[0:ntok, :], xs[slot][0:ntok, :], AF.Square, [xsn], ["junk", "ss%d" % slot], accum_out=ssc)

            def s1():
                tsc("gpsimd", rsc, ssc, 1.0 / D, 1e-6, ALU.mult, ALU.add, ["ss%d" % slot], ["rs%d" % slot])
                tt("gpsimd", rsc, rsc, mhalf[0:ntok, :], ALU.pow, ["rs%d" % slot, "mhalf"], ["rs%d" % slot])

            def s2():
                act(xn[slot][0:ntok, :], xs[slot][0:ntok, :], AF.Copy, [xsn, "rs%d" % slot], [xnn], scale=rsc)
            return [s0, s1, s2]

        def preload_table(func):
            act(small[:, 17:18], small[:, 16:17], func, ["dmy_in"], ["dmy_out"])

        STB = [((stp[:, 0, :], stp[:, 1, :]), ("stp", "stp"), stp), ((stp2[:, 0, :], stp2[:, 1, :]), ("tm0", "tm1"), stp2)]

        def attn_chunk(bis, fill, drain):
            items = [(j, bi, g) for j, bi in enumerate(bis) for g in range(2)]

            def ST(i):
                j, bi, g = items[i]
                cs = slice(j * 128, (j + 1) * 128)
                hs = slice(g * 64, (g + 1) * 64)
                gs = slice(g * 4, (g + 1) * 4)
                (b0, b1), (n0, n1), bfull = STB[i % 2]
                pv_, cu_ = (bi - 1) % 3, bi % 3
                mm(b0, kring[hs, pv_, :], qT[hs, :, cs], True, True, ["k%d" % pv_, "qT"], [n0])
                mm(b1, kring[hs, cu_, :], qT[hs, :, cs], True, True, ["k%d" % cu_, "qT"], [n1])
                if bi == 1:
                    tt("vector", b0, b0, biasT0[:, gs, :], ALU.add, [n0, "biasT0"], [n0])
                    tt("vector", b1, b1, biasT[:, 1, gs, :], ALU.add, [n1, "biasT"], [n1])
                else:
                    tt("vector", bfull[:, :, :], bfull[:, :, :], biasT[:, :, gs, :], ALU.add, [n0, n1, "biasT"], [n0, n1])
                act(PT[:, i % 2, :, :], bfull[:, :, :], AF.Exp, [n0, n1], ["PT%d" % (i % 2)])

            def PV(i):
                j, bi, g = items[i]
                gs = slice(g * 4, (g + 1) * 4)
                for jh in range(4):
                    for kb in range(2):
                        rs_ = (bi - 1 + kb) % 3
                        mm(opv[:, jh, :], PT[:, i % 2, kb, jh * 128:(jh + 1) * 128], vring[:, rs_, g, :], kb == 0, kb == 1,
                           ["PT%d" % (i % 2), "vring%d" % rs_], ["opv"])
                tt("vector", dtmp[:, gs], opv[:, :, 64], sinkexp[:, gs], ALU.add, ["opv", "sinkexp"], ["dtmp%d" % g])
                S.add("vector", lambda e, gs=gs: e.reciprocal(out=rden[:, gs], in_=dtmp[:, gs]),
                      reads=["dtmp%d" % g], writes=["rden%d" % g])
                tt("vector", attn[:, g * 256:(g + 1) * 256].rearrange("p (h d) -> p h d", h=4), opv[:, :, 0:64],
                   rden[:, gs].unsqueeze(2).broadcast_to([128, 4, 64]), ALU.mult, ["opv", "rden%d" % g], ["attn"])

            def tail(j):
                att_tail(j * 128, 128, "zat%d" % j, zat[:, j, :])

            ni = len(items)
            ST(0); fill(2)
            ST(1); fill(2)
            for i in range(ni):
                PV(i); fill(2)
                if i + 2 < ni:
                    ST(i + 2); fill(2)
                if i % 2 == 1:
                    tail(i // 2)
                    fill(2)
            drain()

        def merge_stage(n, have_sg=True):
            for m in range(8):
                if not have_sg:
                    gpga_group(m, 0, n)
                    gpga_group(m, 1, n)
                ts_ = m % 2
                n0_, n1_ = "t0" if ts_ == 0 else "t0b", "t1" if ts_ == 0 else "t1b"
                t, nme = alloc_bank()
                for k in range(4):
                    mm(t[:, 0:n], wbrp[:, k, m * 128:(m + 1) * 128], prodp[:, k, 0:n], k == 0, k == 3,
                       ["wbrp", "prodp"], [nme])
                for k in range(4):
                    mm(t[:, CT:CT + n], wbra[:, k, m * 128:(m + 1) * 128], attT[:, k, 0:n], k == 0, k == 3,
                       ["wbra", "attT"], [nme])
                S.add("vector", lambda e, t=t, m=m, ts_=ts_: e.scalar_tensor_tensor(
                    out=t12[:, ts_, :, 0:n], in0=sgall[:, m, :, 0:n], scalar=1.0,
                    in1=t[:, 0:2 * CT].rearrange("p (w c) -> p w c", w=2)[:, :, 0:n], op0=ALU.add, op1=ALU.mult),
                    reads=[nme, "sg%d_0" % m, "sg%d_1" % m], writes=[n0_, n1_])
                tt("gpsimd", mergedT[:, m, 0:n], t12[:, ts_, 0, 0:n], t12[:, ts_, 1, 0:n], ALU.add, [n0_, n1_], ["mergedT"])

        def fm_full():
            return alloc_bank()

        def final_parts(src_ap, dst_ap, j, ntok, key):
            cs = slice(j * 128, j * 128 + ntok)
            sl = final_i[0] % 2
            final_i[0] += 1
            xrb, xrn = (xr, "xr") if sl == 0 else (xr2, "xr2")
            r, rn = rr[sl], "r%d" % sl

            dma(xrb[0:ntok, :], src_ap, [], [xrn], xrn)

            st_ = {}

            def half_mm(e_):
                tmt, tmn = fm_full()
                st_[e_] = (tmt, tmn)
                for k in range(8):
                    mm(tmt[0:ntok, :], mergedT[:, k, cs], wout[:, k, e_ * 512:(e_ + 1) * 512], k == 0, k == 7,
                       ["mergedT", "wout"], [tmn])

            def half_ev(e_):
                tmt, tmn = st_[e_]
                S.add("vector", lambda e: e.scalar_tensor_tensor(
                    out=r[0:ntok, e_ * 512:(e_ + 1) * 512], in0=tmt[0:ntok, :], scalar=0.5,
                    in1=xrb[0:ntok, e_ * 512:(e_ + 1) * 512], op0=ALU.mult, op1=ALU.add), reads=[tmn, xrn], writes=[rn])

            ssc = small[0:ntok, 4 + sl:5 + sl]
            rsc = small[0:ntok, 6 + sl:7 + sl]

            def s0():
                half_mm(0)

            def s1():
                half_ev(0)
                half_mm(1)

            def s2():
                half_ev(1)

            def s3():
                act(junk[0:ntok, :], r[0:ntok, :], AF.Square, [rn], ["junk", "fs%d" % sl], accum_out=ssc)

            def s4():
                tsc("gpsimd", rsc, ssc, 1.0 / D, 1e-6, ALU.mult, ALU.add, ["fs%d" % sl], ["fr%d" % sl])
                tt("gpsimd", rsc, rsc, mhalf[0:ntok, :], ALU.pow, ["fr%d" % sl, "mhalf"], ["fr%d" % sl])

            def s5():
                S.add("vector", lambda e: e.scalar_tensor_tensor(out=r[0:ntok, :], in0=r[0:ntok, :], scalar=rsc, in1=gfin[0:ntok, :],
                                                                 op0=ALU.mult, op1=ALU.mult),
                      reads=[rn, "fr%d" % sl, "gfin"], writes=[rn])
                finals.append(dma(dst_ap, r[0:ntok, :], [rn], [], key + str(sl)))
            return [s0, s1, s2, s3, s4, s5]

        def final_stage(src_ap, dst_ap, j, ntok, key):
            for f_ in final_parts(src_ap, dst_ap, j, ntok, key):
                f_()

        final_i = [0]

        def sample_stage():
            n = NS
            KS = int(os.environ.get("KS", "99"))
            spf = sp_d.rearrange("b i c -> (b i) c")
            dma(xr[:, 0:512], spf[0:128, :], [], ["xr"], "xr")
            dma(xr[0:112, 512:1024], spf[128:240, :], [], ["xr"], "xr")
            act(uring[:, 1, :], xr[:, 0:512], AF.Copy, ["xr"], ["u1"])
            act(uring[0:112, 2, :], xr[0:112, 512:1024], AF.Copy, ["xr"], ["u2"])
            dma(rr[0][:, 0:64], sel_d[0:128, :], [], ["r0"], "st0")
            dma(rr[0][0:112, 64:128], sel_d[128:240, :], [], ["r0"], "st0")
            dma(rr[0][0:NS, 128:192], seld_d, [], ["r0"], "st0")
            vcopy("vector", selb[:, 0, :], rr[0][:, 0:64], ["r0"], ["selb"])
            vcopy("vector", selb[0:112, 1, :], rr[0][0:112, 64:128], ["r0"], ["selb"])
            vcopy("vector", selb[0:NS, 2, :], rr[0][0:NS, 128:192], ["r0"], ["selb"])

            if KS < 1:
                return
            load_norm_T(xs_d, 0, n, 0)
            tmt, tmn = tok_group(0, n, U0, 512)
            vcopy("vector", uo[0:n, :], tmt[0:n, :], [tmn], ["uo"])
            vcopy("vector", uring[0:n, 0, :], tmt[0:n, :], [tmn], ["u0"])
            finals.append(dma(ps_out[:, 14, :], uo[0:n, :], ["uo"], [], "sp1"))
            tmt, tmn = tok_group(0, n, V0, 128)
            vcopy("vector", kvo[0:n, 1, :], tmt[0:n, 0:128], [tmn], ["kvo1"])
            finals.append(dma(vs_out[:, 127, :], kvo[0:n, 1, :], ["kvo1"], ["vs_out"], "sv1"))
            tmt, tmn = tok_group(0, n, K0, 128)
            vcopy("vector", kvo[0:n, 0, :], tmt[0:n, 0:128], [tmn], ["kvo0"])
            finals.append(dma(ks_out[:, 127, :], kvo[0:n, 0, :], ["kvo0"], ["ks_out"], "sk1"))
            tmt, tmn = tok_group(0, n, ZA0, 512)
            act(zat[0:n, 0, :], tmt[0:n, :], AF.Silu, [tmn], ["zat0"])
            if KS < 2:
                return
            feat_stage(None, n)
            for g in range(4):
                gc = slice(g * 128, (g + 1) * 128)
                fmt, fmn = fm_slot()
                mm(fmt[:, 0:n], uring[:, 1, gc], selb[:, 0, g * n:(g + 1) * n], True, False, ["u1", "selb"], [fmn])
                mm(fmt[:, 0:n], uring[0:112, 2, gc], selb[0:112, 2 - 1, g * n:(g + 1) * n], False, False, ["u2", "selb"], [fmn])
                mm(fmt[:, 0:n], uring[0:n, 0, gc], selb[0:n, 2, g * n:(g + 1) * n], False, True, ["u0", "selb"], [fmn])
                act(pooledT[:, g, 0:n], fmt[:, 0:n], AF.Copy, [fmn], ["pooledT"])
            pg_stage(n)
            if KS < 3:
                return
            for hb in range(2):
                bs = slice(hb * 8, (hb + 1) * 8)
                dma(rr[hb][:, :].rearrange("s (b c) -> s b c", b=8), ks_out[bs].rearrange("b s c -> s b c"),
                    ["ks_out"], ["r%d" % hb], "st%d" % hb)
                dma(xs[hb][:, :].rearrange("s (b c) -> s b c", b=8), vs_out[bs].rearrange("b s c -> s b c"),
                    ["vs_out"], ["xs%d" % hb], "xs%d" % hb)
                act(mergedT[:, hb * 4:(hb + 1) * 4, :], rr[hb][:, :].rearrange("s (a c) -> s a c", a=4), AF.Copy, ["r%d" % hb], ["mergedT"])
                vcopy("vector", vaugs[:, bs, :, 0:64], xs[hb][:, :].rearrange("s (b g d) -> s b g d", b=8, g=2),
                      ["xs%d" % hb], ["vaugs"])
            if KS < 4:
                return
            for q4 in range(4):
                for i in range(4):
                    b_ = q4 * 4 + i
                    tr(bank0[:, i * 128:(i + 1) * 128], mergedT[:, b_ // 2, (b_ % 2) * 128:(b_ % 2 + 1) * 128], ident[:, :], ["mergedT", "ident"], ["bank0"])
                vcopy("vector", PT[:, q4 // 2, q4 % 2, :], bank0[:, 0:512], ["bank0"], ["PT%d" % (q4 // 2)])
            if KS < 5:
                return
            preload_table(AF.Exp)
            for b_ in range(n):
                for g in range(2):
                    hs = slice(g * 64, (g + 1) * 64)
                    c0 = b_ * 8 + g * 4
                    mm(stp[:, 0, c0:c0 + 4], PT[hs, b_ // 8, (b_ // 4) % 2, (b_ % 4) * 128:(b_ % 4 + 1) * 128], qTs[hs, b_, :], True, True, ["PT0", "PT1", "qT"], ["stp"])
            tt("vector", stp[:, 0, 0:128].rearrange("p (b h) -> p b h", b=n),
               stp[:, 0, 0:128].rearrange("p (b h) -> p b h", b=n),
               biasS[:, :].unsqueeze(1).broadcast_to([128, n, 8]), ALU.add, ["stp", "biasS"], ["stp"])
            pts = attg[:, 0:128]
            act(pts, stp[:, 0, 0:128], AF.Exp, ["stp"], ["attg"])
            if KS < 6:
                return
            opf, opn = tm_slot()
            for b_ in range(n):
                for g in range(2):
                    c0 = b_ * 8 + g * 4
                    mm(opf[0:65, c0:c0 + 4], vaugs[:, b_, g, :], pts[:, c0:c0 + 4], True, True, ["vaugs", "attg"], [opn])
            if KS < 7:
                return
            tt("vector", nrm[64:65, 0:128].rearrange("p (b h) -> p b h", b=n),
               opf[64:65, 0:128].rearrange("p (b h) -> p b h", b=n),
               sinkexp[64:65, :].unsqueeze(1).broadcast_to([1, n, 8]), ALU.add, [opn, "sinkexp"], ["attn"])
            S.add("vector", lambda e: e.reciprocal(out=nrm[64:65, 128:256], in_=nrm[64:65, 0:128]), reads=["attn"], writes=["attn"])
            tmt, tmn = tm_slot()
            mm(tmt[0:64, 0:128], onesf[64:65, :], nrm[64:65, 128:256], True, True, ["attn", "onesf"], [tmn])
            vcopy("vector", nrm[0:64, 256:384], tmt[0:64, 0:128], [tmn], ["attn"])
            tt("vector", nrm[0:64, 384:512], opf[0:64, 0:128], nrm[0:64, 256:384], ALU.mult, [opn, "attn"], ["attn"])
            if KS < 8:
                return
            tmt, tmn = tm_slot()
            for h in range(8):
                tr(tmt[0:n, h * 64:(h + 1) * 64], nrm[0:64, 384 + h:512:8], identf[0:64, 0:64], ["attn", "identf"], [tmn])
            vcopy("vector", attn[0:n, :], tmt[0:n, :], [tmn], ["attn"])
            if KS < 9:
                return
            att_tail(0, n, "zat0", zat[0:n, 0, :])
            merge_stage(n, have_sg=False)
            final_stage(xs_d, ys_out, 0, n, "o_ys")

        def chunk_blocks(c):
            return [1 + c * CB + j for j in range(CB)]

        def proj_stage(c):
            bis = chunk_blocks(c)
            for j, bi in enumerate(bis):
                tok_stage(bi, j * 128, j, bi == NBLK)
            feat_stage(bis, CT)
            pool_stage(bis)

        bias_tables()
        load_pieces(0, 2)
        load_norm_T(xc[0], 0, 128, 0)
        tok_stage(0, 0, None, False)
        k_stage([0], 128)
        load_x(xc[1], 1, 128)
        load_x(xc[2], 0, 128)
        for j, bi in enumerate(chunk_blocks(0)):
            norm_x(bi % 2, 128)
            transp_x(bi % 2, 128, j * 128)
        side_weights()
        load_pieces(2, 5)
        for bi in chunk_blocks(1):
            load_x(xc[bi], bi % 2, 128)
        proj_stage(0)
        load_pieces(5, 7)
        side_weights_late()
        finals.append(dma(ks_out[:, 0:127, :], ck_d[:, 1:128, :], [], ["ks_out"], "sk0"))
        finals.append(dma(vs_out[:, 0:127, :], cv_d[:, 1:128, :], [], ["vs_out"], "sv0"))
        finals.append(dma(ps_out[:, 0:14, :], sp_d[:, 1:15, :], [], [], "sp0"))
        preload_table(AF.Exp)

        carry = []
        for c in range(NCH):
            bis = chunk_blocks(c)
            nxt = chunk_blocks(c + 1) if c + 1 < NCH else []
            gates = [gate_steps(m, CT) for m in range(8)]
            others = carry + [norm_steps(bi % 2, 128) for bi in nxt]
            carry = []
            queue = []
            while gates or others:
                if gates:
                    queue.append(gates.pop(0))
                if others:
                    queue.append(others.pop(0))
            active = []

            def fill(k, max_alloc=1):
                fm_allocs[0] = 0
                for f_ in list(active):
                    f_.pop(0)()
                    if not f_:
                        active.remove(f_)
                started = 0
                while queue and started < k and fm_allocs[0] < max_alloc:
                    f_ = queue.pop(0)
                    f_.pop(0)()
                    started += 1
                    if f_:
                        active.append(f_)

            def drain():
                fm_only[0] = False
                while queue or active:
                    fill(3, 3)

            fm_only[0] = True
            attn_chunk(bis, fill, drain)
            fm_only[0] = False
            preload_table(AF.Silu)
            if c + 2 < NCH:
                for bi in chunk_blocks(c + 2):
                    load_x(xc[bi], bi % 2, 128)
            for j, bi in enumerate(nxt):
                transp_x(bi % 2, 128, j * 128)
            merge_stage(CT)
            if nxt:
                proj_stage(c + 1)
                preload_table(AF.Exp)
            for j, bi in enumerate(bis):
                carry.append(final_parts(xc[bi], y_out[bi - 1], j, 128, "o_y"))
        for steps in zip(*carry):
            for st_ in steps:
                st_()

        sample_stage()

        S.emit(nc, ctx, final_ops=finals)
    return nc


_PROG = None


def kernel(x_prompt, x_sample, cache_k, cache_v, state_pool, rel_bias, g_norm, w_in,
           pool_w_grp, pool_scale, attn_sinks, w_br_pool, w_br_attn, w_out, g_final):
    global _PROG
    f = lambda a: np.ascontiguousarray(np.asarray(a, dtype=np.float32))
    x_prompt, x_sample, cache_k, cache_v, state_pool = map(f, (x_prompt, x_sample, cache_k, cache_v, state_pool))
    eb, mcp, m0, cinv0, sel, seld = _consts()
    B, T = x_prompt.shape[0], x_prompt.shape[1]
    half = T // 2
    shared = dict(
        w_in=f(w_in)[0], rel_bias=f(rel_bias),
        gn=np.ascontiguousarray(f(g_norm)[0].reshape(8, 128).T),
        w_grp=f(pool_w_grp)[0],
        pscale=np.ascontiguousarray(f(pool_scale)[0].reshape(4, 128).T),
        sinks=f(attn_sinks)[0].reshape(1, 8),
        w_br_pool=f(w_br_pool)[0], w_br_attn=f(w_br_attn)[0], w_out=f(w_out)[0],
        g_final=f(g_final).reshape(1, D),
        ident=np.eye(128, dtype=np.float32), eb=eb,
        mcp=mcp.reshape(128, -1), sel=sel.reshape(240, -1), seld=seld.reshape(NS, -1),
    )
    in_maps = []
    for core in range(NCORES):
        b, hf = core // 2, core % 2
        xcore = np.zeros((NBLK + 1, 128, D), np.float32)
        xcore[1:] = x_prompt[b, hf * half:(hf + 1) * half].reshape(NBLK, 128, D)
        if hf == 1:
            xcore[0] = x_prompt[b, half - 128:half]
        m = dict(shared)
        m["xc"] = xcore
        m["m0"] = np.ascontiguousarray(m0[hf].reshape(128, -1))
        m["cinv0"] = np.ascontiguousarray(cinv0[hf].reshape(128, -1))
        m["hmask"] = np.full((128, 1), 0.0 if hf == 1 else NEG, np.float32)
        sl = slice(core * NS, (core + 1) * NS)
        m["xsamp"] = np.ascontiguousarray(x_sample[sl, 0, :])
        m["cache_k"] = np.ascontiguousarray(cache_k[0, sl].reshape(NS, 128, 128))
        m["cache_v"] = np.ascontiguousarray(cache_v[0, sl].reshape(NS, 128, 128))
        m["state_pool"] = np.ascontiguousarray(state_pool[0, sl])
        in_maps.append(m)
    if _PROG is None:
        _PROG = build_program()
    res = run_bass_kernel_spmd(_PROG, in_maps, core_ids=list(range(NCORES)))
    rs = res.results
    y_prompt = np.zeros((B, T, D), np.float32)
    nk = np.zeros((1, B, 128, 2, 64), np.float32)
    nv = np.zeros((1, B, 128, 2, 64), np.float32)
    npool = np.zeros((1, B, 15, 512), np.float32)
    y_s = np.zeros((128, 1, D), np.float32)
    ks = np.zeros((1, 128, 128, 2, 64), np.float32)
    vs = np.zeros((1, 128, 128, 2, 64), np.float32)
    pss = np.zeros((1, 128, 15, 512), np.float32)
    for core in range(NCORES):
        b, hf = core // 2, core % 2
        r = rs[core]
        y_prompt[b, hf * half:(hf + 1) * half] = np.asarray(r["y"]).reshape(half, D)
        if hf == 1:
            nk[0, b] = np.asarray(r["k_new"]).reshape(128, 2, 64)
            nv[0, b] = np.asarray(r["v_new"]).reshape(128, 2, 64)
            npool[0, b] = np.asarray(r["p_new"])
        sl = slice(core * NS, (core + 1) * NS)
        y_s[sl, 0] = np.asarray(r["ys"])
        ks[0, sl] = np.asarray(r["ks_new"]).reshape(NS, 128, 2, 64)
        vs[0, sl] = np.asarray(r["vs_new"]).reshape(NS, 128, 2, 64)
        pss[0, sl] = np.asarray(r["ps_new"])
    return (y_prompt, y_s, nk, nv, npool, ks, vs, pss)
```

```python
import math
import os
from contextlib import ExitStack

import numpy as np
import concourse.bass as bass
import concourse.mybir as mybir
from concourse.bass_utils import run_bass_kernel_spmd

F32 = mybir.dt.float32
BF16 = mybir.dt.bfloat16
AF = mybir.ActivationFunctionType
ALU = mybir.AluOpType

NCORES = 8
D = 1024
NBLK = 16
CB = 2
NCH = NBLK // CB
CT = CB * 128
NS = 16
INC = 4352
U0, ZP0, Q0, K0, V0, ZA0, GP0, GA0 = 0, 512, 1024, 1536, 1664, 1792, 2304, 3328
POOLW = (2, 4, 8, 16)
NEG = -30000.0


class Op:
    __slots__ = ("eng", "fn", "deps", "sig", "sigval", "semkey", "idx")

    def __init__(self, eng, fn):
        self.eng = eng
        self.fn = fn
        self.deps = set()
        self.sig = False
        self.sigval = None
        self.semkey = None


class Sched:
    ENGS = ("tensor", "vector", "scalar", "gpsimd", "sync")

    def __init__(self):
        self.ops = {e: [] for e in self.ENGS}
        self.writers = {}
        self.readers = {}
        self.n = 0
        self.exclusive = set()

    def add(self, eng, fn, reads=(), writes=(), deps=(), semkey=None):
        op = Op(eng, fn)
        op.semkey = semkey
        writes = list(writes) + [r for r in reads if r in self.exclusive]
        reads = [r for r in reads if r not in self.exclusive]
        d = set(x for x in deps if x is not None)
        for r in reads:
            d.update(self.writers.get(r, {}).values())
        for wr in writes:
            d.update(self.writers.get(wr, {}).values())
            d.update(self.readers.get(wr, {}).values())
        for r in reads:
            self.readers.setdefault(r, {})[(eng, semkey)] = op
        for wr in writes:
            self.readers[wr] = {}
            self.writers.setdefault(wr, {})[(eng, semkey)] = op
        d.discard(op)
        if eng == "tensor":
            d = set(x for x in d if x.eng != "tensor")
        op.deps = d
        for x in d:
            x.sig = True
        op.idx = self.n
        self.n += 1
        self.ops[eng].append(op)
        return op

    def emit(self, nc, ctx, final_ops=()):
        eng_sem = {}
        for e in ("tensor", "vector", "scalar", "gpsimd"):
            eng_sem[e] = ctx.enter_context(nc.semaphore("s_" + e))
            c = 0
            for op in self.ops[e]:
                if op.sig and op.semkey is None:
                    c += 1
                    op.sigval = (eng_sem[e], c)
        dma_sem, dma_cnt = {}, {}
        for op in self.ops["sync"] + [o for o in self.ops["gpsimd"] if o.semkey is not None]:
            k = op.semkey
            assert k is not None
            if k not in dma_sem:
                dma_sem[k] = ctx.enter_context(nc.semaphore("d_%d" % len(dma_sem)))
                dma_cnt[k] = 0
            dma_cnt[k] += 16
            op.sigval = (dma_sem[k], dma_cnt[k])
            op.sig = True

        def run(e, h):
            waited = {}
            for op in self.ops[e]:
                for dep in sorted(op.deps, key=lambda x: x.idx):
                    sem, val = dep.sigval
                    if waited.get(id(sem), 0) >= val:
                        continue
                    waited[id(sem)] = val
                    h.wait_ge(sem, val)
                ins = op.fn(h)
                if op.sig:
                    ins.then_inc(op.sigval[0], 16 if op.semkey is not None else 1)

        with nc.Block() as block:
            @block.sync
            def _(h):
                run("sync", h)
                done = {}
                for op in final_ops:
                    sem, val = op.sigval
                    if done.get(id(sem), (None, 0))[1] < val:
                        done[id(sem)] = (sem, val)
                for sem, val in done.values():
                    h.wait_ge(sem, val)

            @block.tensor
            def _(h):
                run("tensor", h)

            @block.vector
            def _(h):
                run("vector", h)

            @block.scalar
            def _(h):
                run("scalar", h)

            @block.gpsimd
            def _(h):
                run("gpsimd", h)


def _t5_bucket(n):
    n = np.maximum(n, 0)
    nf = np.maximum(n, 1).astype(np.float32)
    lb = 16 + (np.log(nf / np.float32(16)) / np.float32(math.log(128 / 16)) * np.float32(16)).astype(np.int32)
    return np.where(n < 16, n, np.minimum(lb, 31))


def _consts():
    rp = np.arange(384)
    rel = rp - 128
    valid = (rel >= 0) & (rel <= 127)
    bk = _t5_bucket(rel)
    eb = np.zeros((33, 384), np.float32)
    eb[bk[valid], rp[valid]] = 1.0
    eb[32, rp[~valid]] = 1.0
    tp = np.arange(128)[:, None]
    t = np.arange(128)[None, :]
    mcp = np.zeros((128, 4, 2, 128), np.float32)
    m0 = np.zeros((2, 128, 4, 128), np.float32)
    cinv0 = np.ones((2, 128, 4, 128), np.float32)
    for g, w in enumerate(POOLW):
        cur = ((tp <= t) & (tp > t - w)).astype(np.float32)
        cur_reg = cur - w * (tp == t)
        prev = (tp - 128 > t - w).astype(np.float32)
        mcp[:, g, 0, :] = prev
        mcp[:, g, 1, :] = cur_reg
        cnt = np.minimum(t + 1, w).astype(np.float32)
        m0[0, :, g, :] = cur - cnt * (tp == t)
        m0[1, :, g, :] = cur_reg
        cinv0[0, :, g, :] = np.broadcast_to(w / cnt, (128, 128))
    sel = np.zeros((240, 4, NS), np.float32)
    for g, w in enumerate(POOLW):
        for b in range(NS):
            for i in range(15):
                if i >= 16 - w:
                    sel[b * 15 + i, g, b] = 1.0
    seld = np.zeros((NS, 4, NS), np.float32)
    for g, w in enumerate(POOLW):
        seld[:, g, :] = np.eye(NS) * (1.0 - w)
    return eb, mcp, m0, cinv0, sel, seld


def build_program():
    nc = bass.Bass("TRN2", target_bir_lowering=False)

    def din(name, shape):
        return nc.dram_tensor(name, list(shape), F32, kind="ExternalInput").ap()

    def dout(name, shape):
        return nc.dram_tensor(name, list(shape), F32, kind="ExternalOutput").ap()

    xc = din("xc", [NBLK + 1, 128, D])
    w_in = din("w_in", [D, INC])
    relb = din("rel_bias", [32, 8])
    gn = din("gn", [128, 8])
    wgrp_d = din("w_grp", [4, 128, 128])
    pscale_d = din("pscale", [128, 4])
    sinks_d = din("sinks", [1, 8])
    wbrp_d = din("w_br_pool", [512, D])
    wbra_d = din("w_br_attn", [512, D])
    wout_d = din("w_out", [D, D])
    gfin_d = din("g_final", [1, D])
    ident_d = din("ident", [128, 128])
    eb_d = din("eb", [33, 384])
    mcp_d = din("mcp", [128, 4 * 2 * 128])
    m0_d = din("m0", [128, 4 * 128])
    cinv0_d = din("cinv0", [128, 4 * 128])
    hmask_d = din("hmask", [128, 1])
    xs_d = din("xsamp", [NS, D])
    ck_d = din("cache_k", [NS, 128, 128])
    cv_d = din("cache_v", [NS, 128, 128])
    sp_d = din("state_pool", [NS, 15, 512])
    sel_d = din("sel", [240, 4 * NS])
    seld_d = din("seld", [NS, 4 * NS])

    y_out = dout("y", [NBLK, 128, D])
    k_out = dout("k_new", [128, 128])
    v_out = dout("v_new", [128, 128])
    p_out = dout("p_new", [15, 512])
    ys_out = dout("ys", [NS, D])
    ks_out = dout("ks_new", [NS, 128, 128])
    vs_out = dout("vs_new", [NS, 128, 128])
    ps_out = dout("ps_new", [NS, 15, 512])
    scr = nc.dram_tensor("scr", [8, 128, 384], F32, kind="Internal").ap()
    scr1 = nc.dram_tensor("scr1", [8, 384], F32, kind="Internal").ap()

    S = Sched()
    S.exclusive = {"bank0", "fm0", "fm1", "tm0", "tm1", "stp", "opv"}
    finals = []
    with ExitStack() as ctx:
        def sb(name, shape, dt=F32):
            return ctx.enter_context(nc.sbuf_tensor("sb_" + name, list(shape), dt))

        def ps(name, shape, dt=F32):
            return ctx.enter_context(nc.psum_tensor("ps_" + name, list(shape), dt))

        win = sb("win", [128, 8, INC], BF16)
        wbrp = sb("wbrp", [128, 4, D], BF16)
        wbra = sb("wbra", [128, 4, D], BF16)
        wout = sb("wout", [128, 8, D], BF16)
        wgrp = sb("wgrp", [128, 4, 128], BF16)
        biasT = sb("biasT", [128, 2, 8, 128])
        biasT0 = sb("biasT0", [128, 8, 128])
        biasS = sb("biasS", [128, 8])
        mcp = sb("mcp", [128, 4, 2, 128], BF16)
        m0 = sb("m0", [128, 4, 128], BF16)
        cinv0 = sb("cinv0", [128, 4, 128])
        ident = sb("ident", [128, 128], BF16)
        identf = sb("identf", [128, 128])
        gfin = sb("gfin", [128, D])
        gnt = sb("gnt", [128, 8])
        pscale = sb("pscale", [128, 4])
        sinkexp = sb("sinkexp", [128, 8])
        hmask = sb("hmask", [128, 1])
        mhalf = sb("mhalf", [128, 1])
        r33 = sb("r33", [33, 8])
        ebt = sb("ebt", [33, 384])
        small = sb("small", [128, 32])
        xs = [sb("xs%d" % i, [128, D]) for i in range(2)]
        xr = sb("xr", [128, D])
        xr2 = sb("xr2", [128, D])
        xn = [sb("xn%d" % i, [128, D], BF16) for i in range(2)]
        junk = sb("junk", [128, D], BF16)
        xT = sb("xT", [128, 8, CT], BF16)
        uring = sb("uring", [128, 3, 512], BF16)
        kring = sb("kring", [128, 3, 128], BF16)
        vring = sb("vring", [128, 3, 2, 65], BF16)
        zat = sb("zat", [128, CB, 512], BF16)
        zpT = sb("zpT", [128, 4, CT], BF16)
        qT = sb("qT", [128, 4, CT], BF16)
        pooledT = sb("pooledT", [128, 4, CT], BF16)
        prodp = sb("prodp", [128, 4, CT], BF16)
        PT = sb("PT", [128, 2, 2, 512], BF16)
        dtmp = sb("dtmp", [128, 8])
        rden = sb("rden", [128, 8])
        attn = sb("attn", [128, 512])
        attg = sb("attg", [128, 512], BF16)
        attT = sb("attT", [128, 4, CT], BF16)
        sgall = sb("sgall", [128, 8, 2, CT], BF16)
        t12 = sb("t12", [128, 2, 2, CT])
        mergedT = sb("mergedT", [128, 8, CT], BF16)
        rr = [sb("r%d" % i, [128, D]) for i in range(2)]
        kvo = sb("kvo", [128, 2, 128])
        uo = sb("uo", [128, 512])
        selb = sb("selb", [128, 3, 4 * NS], BF16)
        qTs = sb("qTs", [128, NS, 4], BF16)
        vaugs = sb("vaugs", [128, NS, 2, 65], BF16)
        nrm = attn
        onesf = sb("onesf", [128, 64])

        bank0 = ps("bank0", [128, 1024], BF16)
        fmb = [ps("fm%d" % i, [128, 512]) for i in range(2)]
        stp2 = ps("stp2", [128, 2, 512])
        tmb = [stp2[:, 0, :], stp2[:, 1, :]]
        stp = ps("stp", [128, 2, 512])
        opv = ps("opv", [128, 4, 65])

        fm_i = [0]

        ACC = [(fmb[0], "fm0"), (tmb[0], "tm0"), (fmb[1], "fm1"), (tmb[1], "tm1")]
        acc_i = [0]

        fm_only = [False]
        fm_allocs = [0]

        def alloc_bank():
            fm_allocs[0] += 1
            if fm_only[0]:
                i = fm_i[0] % 2
                fm_i[0] += 1
                return fmb[i], "fm%d" % i
            i = acc_i[0] % 4
            acc_i[0] += 1
            return ACC[i]

        def acc_slot():
            return alloc_bank()

        def fm_slot():
            t, nme = acc_slot()
            return t[:, 0:CT], nme

        tm_i = [0]

        def tm_slot():
            return acc_slot()

        def dma(out, in_, reads, writes, key, **kw):
            return S.add("sync", lambda e: e.dma_start(out=out, in_=in_, **kw), reads=reads, writes=writes, semkey=key)

        def act(out, in_, func, reads, writes, **kw):
            return S.add("scalar", lambda e: e.activation(out=out, in_=in_, func=func, **kw), reads=reads, writes=writes)

        def vcopy(eng, out, in_, reads, writes):
            return S.add(eng, lambda e: e.tensor_copy(out=out, in_=in_), reads=reads, writes=writes)

        def tt(eng, out, in0, in1, op, reads, writes):
            return S.add(eng, lambda e: e.tensor_tensor(out=out, in0=in0, in1=in1, op=op), reads=reads, writes=writes)

        def tsc(eng, out, in0, s1, s2, op0, op1, reads, writes):
            return S.add(eng, lambda e: e.tensor_scalar(out=out, in0=in0, scalar1=s1, scalar2=s2, op0=op0, op1=op1),
                         reads=reads, writes=writes)

        def mm(out, lhsT, rhs, start, stop, reads, writes):
            return S.add("tensor", lambda e: e.matmul(out, lhsT=lhsT, rhs=rhs, start=start, stop=stop),
                         reads=reads, writes=writes)

        def tr(out, in_, idn, reads, writes):
            return S.add("tensor", lambda e: e.transpose(out=out, in_=in_, identity=idn), reads=reads, writes=writes)

        dma(identf[:], ident_d, [], ["identf"], "c0")
        vcopy("vector", ident[:], identf[:], ["identf"], ["ident"])
        dma(gnt[:], gn, [], ["gnt"], "c1")
        dma(pscale[:], pscale_d, [], ["pscale"], "c2")
        for g, w in enumerate(POOLW):
            tsc("gpsimd", pscale[:, g:g + 1], pscale[:, g:g + 1], 1.0 / w, 1.0, ALU.mult, ALU.mult, ["pscale"], ["pscale"])
        dma(sinkexp[:], sinks_d.partition_broadcast(128), [], ["sinkexp"], "c3")
        act(sinkexp[:], sinkexp[:], AF.Exp, ["sinkexp"], ["sinkexp"])
        dma(gfin[:], gfin_d.partition_broadcast(128), [], ["gfin"], "c4")
        dma(hmask[:], hmask_d, [], ["hmask"], "c5")
        S.add("gpsimd", lambda e: e.memset(mhalf[:], -0.5), writes=["mhalf"])
        S.add("gpsimd", lambda e: e.memset(small[:, 16:18], 0.0), writes=["dmy_in"])
        S.add("gpsimd", lambda e: e.memset(vring[:], 1.0), writes=["vring0", "vring1", "vring2"])
        S.add("gpsimd", lambda e: e.memset(vaugs[:], 1.0), writes=["vaugs"])
        S.add("gpsimd", lambda e: e.memset(onesf[:], 1.0), writes=["onesf"])
        dma(cinv0[:], cinv0_d.rearrange("p (g t) -> p g t", g=4), [], ["cinv0"], "c6")
        dma(rr[0][:, 0:1024], mcp_d, [], ["r0"], "st0")
        vcopy("vector", mcp[:], rr[0][:, 0:1024].rearrange("p (g k t) -> p g k t", g=4, k=2), ["r0"], ["mcp"])
        dma(rr[1][:, 0:512], m0_d, [], ["r1"], "st1")
        vcopy("vector", m0[:], rr[1][:, 0:512].rearrange("p (g t) -> p g t", g=4), ["r1"], ["m0"])

        dma(r33[0:32, :], relb, [], ["r33"], "c7")
        S.add("gpsimd", lambda e: e.memset(r33[32:33, :], NEG), writes=["r33"])
        dma(ebt[:], eb_d, [], ["ebt"], "c8")

        def bias_tables():
            tmt, tmn = tm_slot()
            S.add("tensor", lambda e: e.matmul(tmt[0:8, 0:384], lhsT=r33[:, :], rhs=ebt[:], start=True, stop=True),
                  reads=["r33", "ebt"], writes=[tmn])
            vcopy("vector", uo[0:8, 0:384], tmt[0:8, 0:384], [tmn], ["uo"])
            w1 = S.add("gpsimd", lambda e: e.dma_start(out=scr1, in_=uo[0:8, 0:384]), reads=["uo"], writes=["scr1"], semkey="b0")
            rep = bass.AP(tensor=scr1.tensor, offset=0, ap=[[384, 8], [0, 128], [1, 384]])
            w2 = S.add("gpsimd", lambda e: e.dma_start(out=scr, in_=rep), reads=["scr1"], writes=["scr"], deps=[w1], semkey="b1")
            for kb in range(2):
                src = bass.AP(tensor=scr.tensor, offset=(256 if kb == 0 else 128),
                              ap=[[383, 128], [128 * 384, 8], [1, 128]])
                S.add("gpsimd", lambda e, src=src, kb=kb: e.dma_start(out=biasT[:, kb, :, :], in_=src),
                      reads=["scr"], writes=["biasT"], deps=[w2], semkey="c9")
            srcS = bass.AP(tensor=scr.tensor, offset=255, ap=[[383, 128], [128 * 384, 8], [1, 1]])
            S.add("gpsimd", lambda e: e.dma_start(out=biasS[:].unsqueeze(2), in_=srcS, allow_slow_non_contiguous=True),
                  reads=["scr"], writes=["biasS"], deps=[w2], semkey="c10")
            tsc("vector", biasT0[:], biasT[:, 0, :, :], hmask[:, 0:1], None, ALU.add, ALU.bypass, ["biasT", "hmask"], ["biasT0"])

        cast_i = [0]

        def cast(out, in_, scale_ap, reads, writes):
            i = cast_i[0] % 2
            cast_i[0] += 1
            if i == 1:
                return act(out, in_, AF.Copy, reads, writes, scale=scale_ap)
            return tsc("vector", out, in_, scale_ap, None, ALU.mult, ALU.bypass, reads, writes)

        def f32view(ap2d):
            return ap2d.bitcast(F32)

        stg_big = [(rr[0][:, :], ["r0"], "st0"), (rr[1][:, :], ["r1"], "st1"), (xr[:, :], ["xr"], "xr"),
                   (xr2[:, :], ["xr2"], "xr2"),
                   (f32view(mergedT[:, :, :].rearrange("p a b -> p (a b)")), ["mergedT"], "st4")]
        stg_small = [(f32view(PT[:, 0, :, :].rearrange("p a b -> p (a b)")), ["PT0"], "st5"),
                     (f32view(PT[:, 1, :, :].rearrange("p a b -> p (a b)")), ["PT1"], "st6"),
                     (f32view(attT[:, :, :].rearrange("p a b -> p (a b)")), ["attT"], "st7"),
                     (f32view(prodp[:, :, :].rearrange("p a b -> p (a b)")), ["prodp"], "st8")]
        st_i = [0, 0]

        def stage(n):
            if n <= 512:
                pool_ = stg_small + stg_big
                i = st_i[0] % len(pool_)
                st_i[0] += 1
                return pool_[i]
            i = st_i[1] % len(stg_big)
            st_i[1] += 1
            return stg_big[i]

        def gdma(out, in_, writes, key):
            return S.add("gpsimd", lambda e: e.dma_start(out=out, in_=in_), writes=writes, semkey=key)

        pieces = [(U0, 512, "w_u"), (K0, 256, "w_kv"), (ZA0, 512, "w_za"), (ZP0, 512, "w_zp"), (Q0, 512, "w_q"),
                  (GP0, 1024, "w_gp"), (GA0, 1024, "w_ga")]
        def load_pieces(lo, hi):
          for pi in range(lo, hi):
            c0, n, wn = pieces[pi]
            for k in range(8):
                st, sns, sk = stage(n)
                dma(st[:, 0:n], w_in[k * 128:(k + 1) * 128, c0:c0 + n], sns, sns, sk)
                if wn == "w_q":
                    for g in range(2):
                        src = st[:, g * 256:(g + 1) * 256].rearrange("p (j d) -> p j d", j=4)
                        dst = win[:, k, Q0:Q0 + 512].rearrange("p (j g d) -> p j g d", j=4, g=2)[:, :, g, :]
                        if g == 0:
                            vcopy("vector", dst, src, sns, [wn])
                        else:
                            act(dst, src, AF.Copy, sns, [wn])
                else:
                    cast_i[0] += 1
                    if cast_i[0] % 2 == 0:
                        vcopy("vector", win[:, k, c0:c0 + n], st[:, 0:n], sns, [wn])
                    else:
                        act(win[:, k, c0:c0 + n], st[:, 0:n], AF.Copy, sns, [wn])

        def side_weights():
            gdma(wgrp[:, :, :], wgrp_d.rearrange("g c e -> c g e"), ["wgrp"], "gw0")
            for k in range(4):
                gdma(wbrp[:, k, :], wbrp_d[k * 128:(k + 1) * 128, :], ["wbrp"], "gw1")
            for k in range(4):
                gdma(wbra[:, k, :], wbra_d[k * 128:(k + 1) * 128, :], ["wbra"], "gw2")

        def side_weights_late():
            for k in range(8):
                gdma(wout[:, k, :], wout_d[k * 128:(k + 1) * 128, :], ["wout"], "gw3")

        def load_x(src_ap, slot, ntok):
            dma(xs[slot][0:ntok, :], src_ap, [], ["xs%d" % slot], "xs%d" % slot)

        def norm_x(slot, ntok):
            xsn, xnn = "xs%d" % slot, "xn%d" % slot
            ssc = small[0:ntok, slot:slot + 1]
            rsc = small[0:ntok, 2 + slot:3 + slot]
            act(junk[0:ntok, :], xs[slot][0:ntok, :], AF.Square, [xsn], ["junk", "ss%d" % slot], accum_out=ssc)
            tsc("gpsimd", rsc, ssc, 1.0 / D, 1e-6, ALU.mult, ALU.add, ["ss%d" % slot], ["rs%d" % slot])
            tt("gpsimd", rsc, rsc, mhalf[0:ntok, :], ALU.pow, ["rs%d" % slot, "mhalf"], ["rs%d" % slot])
            act(xn[slot][0:ntok, :], xs[slot][0:ntok, :], AF.Copy, [xsn, "rs%d" % slot], [xnn], scale=rsc)

        tx_i = [0]

        def transp_x(slot, ntok, col):
            xnn = "xn%d" % slot
            for k in range(8):
                tr(bank0[:, k * 128:k * 128 + ntok], xn[slot][0:ntok, k * 128:(k + 1) * 128],
                   ident[0:ntok, 0:ntok], [xnn, "ident"], ["bank0"])
            dst = xT[:, :, col:col + ntok]
            srcp = bank0[:, :].rearrange("p (k t) -> p k t", k=8)[:, :, 0:ntok]
            tt("vector", dst, srcp, gnt[:, :].unsqueeze(2).broadcast_to([128, 8, ntok]), ALU.mult,
               ["bank0", "gnt"], ["xT"])

        def load_norm_T(src_ap, slot, ntok, col, preloaded=False):
            if not preloaded:
                load_x(src_ap, slot, ntok)
            norm_x(slot, ntok)
            transp_x(slot, ntok, col)

        WRES = {U0: "w_u", V0: "w_kv", K0: "w_kv", ZA0: "w_za"}

        def tok_group(col, ntok, c0, ncol):
            tmt, tmn = tm_slot()
            for k in range(8):
                mm(tmt[0:ntok, 0:ncol], xT[:, k, col:col + ntok], win[:, k, c0:c0 + ncol], k == 0, k == 7,
                   ["xT", WRES[c0]], [tmn])
            return tmt, tmn

        def feat_group(ncols, wt, wn, nk, c0, rhs_fn, rnames):
            fmt, fmn = fm_slot()
            for k in range(nk):
                mm(fmt[:, 0:ncols], wt[:, k, c0:c0 + 128], rhs_fn(k), k == 0, k == nk - 1, [wn] + rnames, [fmn])
            return fmt, fmn

        def tok_stage(bi, col, j, last):
            us = bi % 3
            tmt, tmn = tok_group(col, 128, U0, 512)
            act(uring[:, us, :], tmt[:, :], AF.Copy, [tmn], ["u%d" % us])
            if last:
                vcopy("vector", uo[:], tmt[:, :], [tmn], ["uo"])
                finals.append(dma(p_out, uo[113:128, :], ["uo"], [], "o_p"))
            tmt, tmn = tok_group(col, 128, V0, 128)
            vcopy("vector", vring[:, us, :, 0:64], tmt[:, 0:128].rearrange("p (g d) -> p g d", g=2), [tmn], ["vring%d" % us])
            if last:
                vcopy("vector", kvo[:, 1, :], tmt[:, 0:128], [tmn], ["kvo1"])
                finals.append(dma(v_out, kvo[:, 1, :], ["kvo1"], [], "o_v"))
                tmt, tmn = tok_group(col, 128, K0, 128)
                vcopy("vector", kvo[:, 0, :], tmt[:, 0:128], [tmn], ["kvo0"])
                finals.append(dma(k_out, kvo[:, 0, :], ["kvo0"], [], "o_k"))
            if j is not None:
                tmt, tmn = tok_group(col, 128, ZA0, 512)
                act(zat[:, j, :], tmt[:, :], AF.Silu, [tmn], ["zat%d" % j])

        def k_stage(bis, ncols):
            fmt, fmn = feat_group(ncols, win, "w_kv", 8, K0, lambda k: xT[:, k, 0:ncols], ["xT"])
            for j, bi in enumerate(bis):
                vcopy("vector", kring[:, bi % 3, :], fmt[:, j * 128:(j + 1) * 128], [fmn], ["k%d" % (bi % 3)])

        def feat_stage(bis, n):
            for m in range(4):
                fmt, fmn = feat_group(n, win, "w_zp", 8, ZP0 + m * 128, lambda k: xT[:, k, 0:n], ["xT"])
                act(zpT[:, m, 0:n], fmt[:, 0:n], AF.Silu, [fmn], ["zpT"])
            for m in range(4):
                fmt, fmn = feat_group(n, win, "w_q", 8, Q0 + m * 128, lambda k: xT[:, k, 0:n], ["xT"])
                qdst = qT[:, m, 0:n] if bis is not None else qTs[:, :, m]
                tsc("vector", qdst, fmt[:, 0:n], 0.125, None, ALU.mult, ALU.bypass, [fmn], ["qT"])
            if bis:
                k_stage(bis, n)

        def pg_stage(n):
            for g in range(4):
                fmt, fmn = fm_slot()
                mm(fmt[:, 0:n], wgrp[:, g, :], pooledT[:, g, 0:n], True, True, ["wgrp", "pooledT"], [fmn])
                S.add("vector", lambda e, fmt=fmt, g=g: e.scalar_tensor_tensor(
                    out=prodp[:, g, 0:n], in0=fmt[:, 0:n], scalar=pscale[:, g:g + 1], in1=zpT[:, g, 0:n],
                    op0=ALU.mult, op1=ALU.mult), reads=[fmn, "pscale", "zpT"], writes=["prodp"])

        def att_tail(j0, n, zname, zap):
            tt("gpsimd", attg[0:n, :], attn[0:n, :], zap, ALU.mult, ["attn", zname], ["attg"])
            for m in range(4):
                tr(bank0[:, 512 + m * 128:512 + m * 128 + n], attg[0:n, m * 128:(m + 1) * 128], ident[0:n, 0:n],
                   ["attg", "ident"], ["bank0"])
            act(attT[:, :, j0:j0 + n], bank0[:, 512:1024].rearrange("p (m t) -> p m t", m=4)[:, :, 0:n], AF.Copy,
                ["bank0"], ["attT"])

        def pool_stage(bis):
            for g in range(4):
                fmt, fmn = fm_slot()
                for j, bi in enumerate(bis):
                    o = fmt[:, j * 128:(j + 1) * 128]
                    mm(o, uring[:, (bi - 1) % 3, g * 128:(g + 1) * 128], mcp[:, g, 0, :], True, False,
                       ["u%d" % ((bi - 1) % 3), "mcp"], [fmn])
                    if bi == 1:
                        mm(o, uring[:, bi % 3, g * 128:(g + 1) * 128], m0[:, g, :], False, True, ["u%d" % (bi % 3), "m0"], [fmn])
                    else:
                        mm(o, uring[:, bi % 3, g * 128:(g + 1) * 128], mcp[:, g, 1, :], False, True,
                           ["u%d" % (bi % 3), "mcp"], [fmn])
                if bis[0] == 1:
                    tt("vector", pooledT[:, g, 0:128], fmt[:, 0:128], cinv0[:, g, :], ALU.mult, [fmn, "cinv0"], ["pooledT"])
                    vcopy("vector", pooledT[:, g, 128:256], fmt[:, 128:256], [fmn], ["pooledT"])
                else:
                    act(pooledT[:, g, :], fmt[:, :], AF.Copy, [fmn], ["pooledT"])
            pg_stage(CT)

        def gpga_group(m, which, n):
            c0 = (GP0 if which == 0 else GA0) + m * 128
            f, nme = feat_group(n, win, "w_gp" if which == 0 else "w_ga", 8, c0, lambda k: xT[:, k, 0:n], ["xT"])
            act(sgall[:, m, which, 0:n], f[:, 0:n], AF.Tanh, [nme], ["sg%d_%d" % (m, which)], scale=0.5)

        def gate_steps(m, n):
            st_ = {}

            def s0():
                t, nme = alloc_bank()
                st_["t"], st_["n"] = t, nme
                for which in range(2):
                    c0 = (GP0 if which == 0 else GA0) + m * 128
                    wn = "w_gp" if which == 0 else "w_ga"
                    for k in range(8):
                        mm(t[:, which * CT:which * CT + n], win[:, k, c0:c0 + 128], xT[:, k, 0:n], k == 0, k == 7,
                           [wn, "xT"], [nme])

            def s1():
                act(sgall[:, m, :, 0:n], st_["t"][:, 0:2 * CT].rearrange("p (w c) -> p w c", w=2)[:, :, 0:n], AF.Tanh,
                    [st_["n"]], ["sg%d_0" % m, "sg%d_1" % m], scale=0.5)
            return [s0, s1]

        def norm_steps(slot, ntok):
            xsn, xnn = "xs%d" % slot, "xn%d" % slot
            ssc = small[0:ntok, slot:slot + 1]
            rsc = small[0:ntok, 2 + slot:3 + slot]

            def s0():
                act(junk[0:ntok, :], xs[slot][0:ntok, :], AF.Square, [xsn], ["junk", "ss%d" % slot], accum_out=ssc)

            def s1():
                tsc("gpsimd", rsc, ssc, 1.0 / D, 1e-6, ALU.mult, ALU.add, ["ss%d" % slot], ["rs%d" % slot])
                tt("gpsimd", rsc, rsc, mhalf[0:ntok, :], ALU.pow, ["rs%d" % slot, "mhalf"], ["rs%d" % slot])

            def s2():
                act(xn[slot][0:ntok, :], xs[slot][0:ntok, :], AF.Copy, [xsn, "rs%d" % slot], [xnn], scale=rsc)
            return [s0, s1, s2]

        def preload_table(func):
            act(small[:, 17:18], small[:, 16:17], func, ["dmy_in"], ["dmy_out"])

        STB = [((stp[:, 0, :], stp[:, 1, :]), ("stp", "stp"), stp), ((stp2[:, 0, :], stp2[:, 1, :]), ("tm0", "tm1"), stp2)]

        def attn_chunk(bis, fill, drain):
            items = [(j, bi, g) for j, bi in enumerate(bis) for g in range(2)]

            def ST(i):
                j, bi, g = items[i]
                cs = slice(j * 128, (j + 1) * 128)
                hs = slice(g * 64, (g + 1) * 64)
                gs = slice(g * 4, (g + 1) * 4)
                (b0, b1), (n0, n1), bfull = STB[i % 2]
                pv_, cu_ = (bi - 1) % 3, bi % 3
                mm(b0, kring[hs, pv_, :], qT[hs, :, cs], True, True, ["k%d" % pv_, "qT"], [n0])
                mm(b1, kring[hs, cu_, :], qT[hs, :, cs], True, True, ["k%d" % cu_, "qT"], [n1])
                if bi == 1:
                    tt("vector", b0, b0, biasT0[:, gs, :], ALU.add, [n0, "biasT0"], [n0])
                    tt("vector", b1, b1, biasT[:, 1, gs, :], ALU.add, [n1, "biasT"], [n1])
                else:
                    tt("vector", bfull[:, :, :], bfull[:, :, :], biasT[:, :, gs, :], ALU.add, [n0, n1, "biasT"], [n0, n1])
                act(PT[:, i % 2, :, :], bfull[:, :, :], AF.Exp, [n0, n1], ["PT%d" % (i % 2)])

            def PV(i):
                j, bi, g = items[i]
                gs = slice(g * 4, (g + 1) * 4)
                for jh in range(4):
                    for kb in range(2):
                        rs_ = (bi - 1 + kb) % 3
                        mm(opv[:, jh, :], PT[:, i % 2, kb, jh * 128:(jh + 1) * 128], vring[:, rs_, g, :], kb == 0, kb == 1,
                           ["PT%d" % (i % 2), "vring%d" % rs_], ["opv"])
                tt("vector", dtmp[:, gs], opv[:, :, 64], sinkexp[:, gs], ALU.add, ["opv", "sinkexp"], ["dtmp%d" % g])
                S.add("vector", lambda e, gs=gs: e.reciprocal(out=rden[:, gs], in_=dtmp[:, gs]),
                      reads=["dtmp%d" % g], writes=["rden%d" % g])
                tt("vector", attn[:, g * 256:(g + 1) * 256].rearrange("p (h d) -> p h d", h=4), opv[:, :, 0:64],
                   rden[:, gs].unsqueeze(2).broadcast_to([128, 4, 64]), ALU.mult, ["opv", "rden%d" % g], ["attn"])

            def tail(j):
                att_tail(j * 128, 128, "zat%d" % j, zat[:, j, :])

            ni = len(items)
            ST(0); fill(2)
            ST(1); fill(2)
            for i in range(ni):
                PV(i); fill(2)
                if i + 2 < ni:
                    ST(i + 2); fill(2)
                if i % 2 == 1:
                    tail(i // 2)
                    fill(2)
            drain()

        def merge_stage(n, have_sg=True):
            for m in range(8):
                if not have_sg:
                    gpga_group(m, 0, n)
                    gpga_group(m, 1, n)
                ts_ = m % 2
                n0_, n1_ = "t0" if ts_ == 0 else "t0b", "t1" if ts_ == 0 else "t1b"
                t, nme = alloc_bank()
                for k in range(4):
                    mm(t[:, 0:n], wbrp[:, k, m * 128:(m + 1) * 128], prodp[:, k, 0:n], k == 0, k == 3,
                       ["wbrp", "prodp"], [nme])
                for k in range(4):
                    mm(t[:, CT:CT + n], wbra[:, k, m * 128:(m + 1) * 128], attT[:, k, 0:n], k == 0, k == 3,
                       ["wbra", "attT"], [nme])
                S.add("vector", lambda e, t=t, m=m, ts_=ts_: e.scalar_tensor_tensor(
                    out=t12[:, ts_, :, 0:n], in0=sgall[:, m, :, 0:n], scalar=1.0,
                    in1=t[:, 0:2 * CT].rearrange("p (w c) -> p w c", w=2)[:, :, 0:n], op0=ALU.add, op1=ALU.mult),
                    reads=[nme, "sg%d_0" % m, "sg%d_1" % m], writes=[n0_, n1_])
                tt("gpsimd", mergedT[:, m, 0:n], t12[:, ts_, 0, 0:n], t12[:, ts_, 1, 0:n], ALU.add, [n0_, n1_], ["mergedT"])

        def fm_full():
            return alloc_bank()

        def final_parts(src_ap, dst_ap, j, ntok, key):
            cs = slice(j * 128, j * 128 + ntok)
            sl = final_i[0] % 2
            final_i[0] += 1
            xrb, xrn = (xr, "xr") if sl == 0 else (xr2, "xr2")
            r, rn = rr[sl], "r%d" % sl

            dma(xrb[0:ntok, :], src_ap, [], [xrn], xrn)

            st_ = {}

            def half_mm(e_):
                tmt, tmn = fm_full()
                st_[e_] = (tmt, tmn)
                for k in range(8):
                    mm(tmt[0:ntok, :], mergedT[:, k, cs], wout[:, k, e_ * 512:(e_ + 1) * 512], k == 0, k == 7,
                       ["mergedT", "wout"], [tmn])

            def half_ev(e_):
                tmt, tmn = st_[e_]
                S.add("vector", lambda e: e.scalar_tensor_tensor(
                    out=r[0:ntok, e_ * 512:(e_ + 1) * 512], in0=tmt[0:ntok, :], scalar=0.5,
                    in1=xrb[0:ntok, e_ * 512:(e_ + 1) * 512], op0=ALU.mult, op1=ALU.add), reads=[tmn, xrn], writes=[rn])

            ssc = small[0:ntok, 4 + sl:5 + sl]
            rsc = small[0:ntok, 6 + sl:7 + sl]

            def s0():
                half_mm(0)

            def s1():
                half_ev(0)
                half_mm(1)

            def s2():
                half_ev(1)

            def s3():
                act(junk[0:ntok, :], r[0:ntok, :], AF.Square, [rn], ["junk", "fs%d" % sl], accum_out=ssc)

            def s4():
                tsc("gpsimd", rsc, ssc, 1.0 / D, 1e-6, ALU.mult, ALU.add, ["fs%d" % sl], ["fr%d" % sl])
                tt("gpsimd", rsc, rsc, mhalf[0:ntok, :], ALU.pow, ["fr%d" % sl, "mhalf"], ["fr%d" % sl])

            def s5():
                S.add("vector", lambda e: e.scalar_tensor_tensor(out=r[0:ntok, :], in0=r[0:ntok, :], scalar=rsc, in1=gfin[0:ntok, :],
                                                                 op0=ALU.mult, op1=ALU.mult),
                      reads=[rn, "fr%d" % sl, "gfin"], writes=[rn])
                finals.append(dma(dst_ap, r[0:ntok, :], [rn], [], key + str(sl)))
            return [s0, s1, s2, s3, s4, s5]

        def final_stage(src_ap, dst_ap, j, ntok, key):
            for f_ in final_parts(src_ap, dst_ap, j, ntok, key):
                f_()

        final_i = [0]

        def sample_stage():
            n = NS
            KS = int(os.environ.get("KS", "99"))
            spf = sp_d.rearrange("b i c -> (b i) c")
            dma(xr[:, 0:512], spf[0:128, :], [], ["xr"], "xr")
            dma(xr[0:112, 512:1024], spf[128:240, :], [], ["xr"], "xr")
            act(uring[:, 1, :], xr[:, 0:512], AF.Copy, ["xr"], ["u1"])
            act(uring[0:112, 2, :], xr[0:112, 512:1024], AF.Copy, ["xr"], ["u2"])
            dma(rr[0][:, 0:64], sel_d[0:128, :], [], ["r0"], "st0")
            dma(rr[0][0:112, 64:128], sel_d[128:240, :], [], ["r0"], "st0")
            dma(rr[0][0:NS, 128:192], seld_d, [], ["r0"], "st0")
            vcopy("vector", selb[:, 0, :], rr[0][:, 0:64], ["r0"], ["selb"])
            vcopy("vector", selb[0:112, 1, :], rr[0][0:112, 64:128], ["r0"], ["selb"])
            vcopy("vector", selb[0:NS, 2, :], rr[0][0:NS, 128:192], ["r0"], ["selb"])

            if KS < 1:
                return
            load_norm_T(xs_d, 0, n, 0, preloaded=True)
            tmt, tmn = tok_group(0, n, U0, 512)
            vcopy("vector", uo[0:n, :], tmt[0:n, :], [tmn], ["uo"])
            vcopy("vector", uring[0:n, 0, :], tmt[0:n, :], [tmn], ["u0"])
            finals.append(dma(ps_out[:, 14, :], uo[0:n, :], ["uo"], [], "sp1"))
            tmt, tmn = tok_group(0, n, V0, 128)
            vcopy("vector", kvo[0:n, 1, :], tmt[0:n, 0:128], [tmn], ["kvo1"])
            finals.append(dma(vs_out[:, 127, :], kvo[0:n, 1, :], ["kvo1"], ["vs_out"], "sv1"))
            tmt, tmn = tok_group(0, n, K0, 128)
            vcopy("vector", kvo[0:n, 0, :], tmt[0:n, 0:128], [tmn], ["kvo0"])
            finals.append(dma(ks_out[:, 127, :], kvo[0:n, 0, :], ["kvo0"], ["ks_out"], "sk1"))
            tmt, tmn = tok_group(0, n, ZA0, 512)
            act(zat[0:n, 0, :], tmt[0:n, :], AF.Silu, [tmn], ["zat0"])
            if KS < 2:
                return
            feat_stage(None, n)
            for g in range(4):
                gc = slice(g * 128, (g + 1) * 128)
                fmt, fmn = fm_slot()
                mm(fmt[:, 0:n], uring[:, 1, gc], selb[:, 0, g * n:(g + 1) * n], True, False, ["u1", "selb"], [fmn])
                mm(fmt[:, 0:n], uring[0:112, 2, gc], selb[0:112, 2 - 1, g * n:(g + 1) * n], False, False, ["u2", "selb"], [fmn])
                mm(fmt[:, 0:n], uring[0:n, 0, gc], selb[0:n, 2, g * n:(g + 1) * n], False, True, ["u0", "selb"], [fmn])
                act(pooledT[:, g, 0:n], fmt[:, 0:n], AF.Copy, [fmn], ["pooledT"])
            pg_stage(n)
            if KS < 3:
                return
            for hb in range(2):
                bs = slice(hb * 8, (hb + 1) * 8)
                dma(rr[hb][:, :].rearrange("s (b c) -> s b c", b=8), ks_out[bs].rearrange("b s c -> s b c"),
                    ["ks_out"], ["r%d" % hb], "st%d" % hb)
                dma(xs[hb][:, :].rearrange("s (b c) -> s b c", b=8), vs_out[bs].rearrange("b s c -> s b c"),
                    ["vs_out"], ["xs%d" % hb], "xs%d" % hb)
                act(mergedT[:, hb * 4:(hb + 1) * 4, :], rr[hb][:, :].rearrange("s (a c) -> s a c", a=4), AF.Copy, ["r%d" % hb], ["mergedT"])
                vcopy("vector", vaugs[:, bs, :, 0:64], xs[hb][:, :].rearrange("s (b g d) -> s b g d", b=8, g=2),
                      ["xs%d" % hb], ["vaugs"])
            if KS < 4:
                return
            for q4 in range(4):
                for i in range(4):
                    b_ = q4 * 4 + i
                    tr(bank0[:, i * 128:(i + 1) * 128], mergedT[:, b_ // 2, (b_ % 2) * 128:(b_ % 2 + 1) * 128], ident[:, :], ["mergedT", "ident"], ["bank0"])
                vcopy("vector", PT[:, q4 // 2, q4 % 2, :], bank0[:, 0:512], ["bank0"], ["PT%d" % (q4 // 2)])
            if KS < 5:
                return
            preload_table(AF.Exp)
            for b_ in range(n):
                for g in range(2):
                    hs = slice(g * 64, (g + 1) * 64)
                    c0 = b_ * 8 + g * 4
                    mm(stp[:, 0, c0:c0 + 4], PT[hs, b_ // 8, (b_ // 4) % 2, (b_ % 4) * 128:(b_ % 4 + 1) * 128], qTs[hs, b_, :], True, True, ["PT0", "PT1", "qT"], ["stp"])
            tt("vector", stp[:, 0, 0:128].rearrange("p (b h) -> p b h", b=n),
               stp[:, 0, 0:128].rearrange("p (b h) -> p b h", b=n),
               biasS[:, :].unsqueeze(1).broadcast_to([128, n, 8]), ALU.add, ["stp", "biasS"], ["stp"])
            pts = attg[:, 0:128]
            act(pts, stp[:, 0, 0:128], AF.Exp, ["stp"], ["attg"])
            if KS < 6:
                return
            opf, opn = tm_slot()
            for b_ in range(n):
                for g in range(2):
                    c0 = b_ * 8 + g * 4
                    mm(opf[0:65, c0:c0 + 4], vaugs[:, b_, g, :], pts[:, c0:c0 + 4], True, True, ["vaugs", "attg"], [opn])
            if KS < 7:
                return
            tt("vector", nrm[64:65, 0:128].rearrange("p (b h) -> p b h", b=n),
               opf[64:65, 0:128].rearrange("p (b h) -> p b h", b=n),
               sinkexp[64:65, :].unsqueeze(1).broadcast_to([1, n, 8]), ALU.add, [opn, "sinkexp"], ["attn"])
            S.add("vector", lambda e: e.reciprocal(out=nrm[64:65, 128:256], in_=nrm[64:65, 0:128]), reads=["attn"], writes=["attn"])
            tmt, tmn = tm_slot()
            mm(tmt[0:64, 0:128], onesf[64:65, :], nrm[64:65, 128:256], True, True, ["attn", "onesf"], [tmn])
            vcopy("vector", nrm[0:64, 256:384], tmt[0:64, 0:128], [tmn], ["attn"])
            tt("vector", nrm[0:64, 384:512], opf[0:64, 0:128], nrm[0:64, 256:384], ALU.mult, [opn, "attn"], ["attn"])
            if KS < 8:
                return
            tmt, tmn = tm_slot()
            for h in range(8):
                tr(tmt[0:n, h * 64:(h + 1) * 64], nrm[0:64, 384 + h:512:8], identf[0:64, 0:64], ["attn", "identf"], [tmn])
            vcopy("vector", attn[0:n, :], tmt[0:n, :], [tmn], ["attn"])
            if KS < 9:
                return
            att_tail(0, n, "zat0", zat[0:n, 0, :])
            merge_stage(n, have_sg=False)
            final_stage(xs_d, ys_out, 0, n, "o_ys")

        def chunk_blocks(c):
            return [1 + c * CB + j for j in range(CB)]

        def proj_stage(c):
            bis = chunk_blocks(c)
            for j, bi in enumerate(bis):
                tok_stage(bi, j * 128, j, bi == NBLK)
            feat_stage(bis, CT)
            pool_stage(bis)

        bias_tables()
        load_pieces(0, 2)
        load_norm_T(xc[0], 0, 128, 0)
        tok_stage(0, 0, None, False)
        k_stage([0], 128)
        load_x(xc[1], 1, 128)
        load_x(xc[2], 0, 128)
        for j, bi in enumerate(chunk_blocks(0)):
            norm_x(bi % 2, 128)
            transp_x(bi % 2, 128, j * 128)
        side_weights()
        load_pieces(2, 5)
        for bi in chunk_blocks(1):
            load_x(xc[bi], bi % 2, 128)
        proj_stage(0)
        load_pieces(5, 7)
        side_weights_late()
        finals.append(dma(ks_out[:, 0:127, :], ck_d[:, 1:128, :], [], ["ks_out"], "sk0"))
        finals.append(dma(vs_out[:, 0:127, :], cv_d[:, 1:128, :], [], ["vs_out"], "sv0"))
        finals.append(dma(ps_out[:, 0:14, :], sp_d[:, 1:15, :], [], [], "sp0"))
        preload_table(AF.Exp)

        carry = []
        for c in range(NCH):
            bis = chunk_blocks(c)
            nxt = chunk_blocks(c + 1) if c + 1 < NCH else []
            gates = [gate_steps(m, CT) for m in range(8)]
            others = carry + [norm_steps(bi % 2, 128) for bi in nxt]
            carry = []
            queue = []
            while gates or others:
                if gates:
                    queue.append(gates.pop(0))
                if others:
                    queue.append(others.pop(0))
            active = []

            def fill(k, max_alloc=1):
                fm_allocs[0] = 0
                for f_ in list(active):
                    f_.pop(0)()
                    if not f_:
                        active.remove(f_)
                started = 0
                while queue and started < k and fm_allocs[0] < max_alloc:
                    f_ = queue.pop(0)
                    f_.pop(0)()
                    started += 1
                    if f_:
                        active.append(f_)

            def drain():
                fm_only[0] = False
                while queue or active:
                    fill(3, 3)

            fm_only[0] = True
            attn_chunk(bis, fill, drain)
            fm_only[0] = False
            preload_table(AF.Silu)
            if c + 2 < NCH:
                for bi in chunk_blocks(c + 2):
                    load_x(xc[bi], bi % 2, 128)
            elif c + 2 == NCH:
                load_x(xs_d, 0, NS)
            for j, bi in enumerate(nxt):
                transp_x(bi % 2, 128, j * 128)
            merge_stage(CT)
            if nxt:
                proj_stage(c + 1)
                preload_table(AF.Exp)
            for j, bi in enumerate(bis):
                carry.append(final_parts(xc[bi], y_out[bi - 1], j, 128, "o_y"))
        for steps in zip(*carry):
            for st_ in steps:
                st_()

        sample_stage()

        S.emit(nc, ctx, final_ops=finals)
    return nc


_PROG = None


def kernel(x_prompt, x_sample, cache_k, cache_v, state_pool, rel_bias, g_norm, w_in,
           pool_w_grp, pool_scale, attn_sinks, w_br_pool, w_br_attn, w_out, g_final):
    global _PROG
    f = lambda a: np.ascontiguousarray(np.asarray(a, dtype=np.float32))
    x_prompt, x_sample, cache_k, cache_v, state_pool = map(f, (x_prompt, x_sample, cache_k, cache_v, state_pool))
    eb, mcp, m0, cinv0, sel, seld = _consts()
    B, T = x_prompt.shape[0], x_prompt.shape[1]
    half = T // 2
    shared = dict(
        w_in=f(w_in)[0], rel_bias=f(rel_bias),
        gn=np.ascontiguousarray(f(g_norm)[0].reshape(8, 128).T),
        w_grp=f(pool_w_grp)[0],
        pscale=np.ascontiguousarray(f(pool_scale)[0].reshape(4, 128).T),
        sinks=f(attn_sinks)[0].reshape(1, 8),
        w_br_pool=f(w_br_pool)[0], w_br_attn=f(w_br_attn)[0], w_out=f(w_out)[0],
        g_final=f(g_final).reshape(1, D),
        ident=np.eye(128, dtype=np.float32), eb=eb,
        mcp=mcp.reshape(128, -1), sel=sel.reshape(240, -1), seld=seld.reshape(NS, -1),
    )
    in_maps = []
    for core in range(NCORES):
        b, hf = core // 2, core % 2
        xcore = np.zeros((NBLK + 1, 128, D), np.float32)
        xcore[1:] = x_prompt[b, hf * half:(hf + 1) * half].reshape(NBLK, 128, D)
        if hf == 1:
            xcore[0] = x_prompt[b, half - 128:half]
        m = dict(shared)
        m["xc"] = xcore
        m["m0"] = np.ascontiguousarray(m0[hf].reshape(128, -1))
        m["cinv0"] = np.ascontiguousarray(cinv0[hf].reshape(128, -1))
        m["hmask"] = np.full((128, 1), 0.0 if hf == 1 else NEG, np.float32)
        sl = slice(core * NS, (core + 1) * NS)
        m["xsamp"] = np.ascontiguousarray(x_sample[sl, 0, :])
        m["cache_k"] = np.ascontiguousarray(cache_k[0, sl].reshape(NS, 128, 128))
        m["cache_v"] = np.ascontiguousarray(cache_v[0, sl].reshape(NS, 128, 128))
        m["state_pool"] = np.ascontiguousarray(state_pool[0, sl])
        in_maps.append(m)
    if _PROG is None:
        _PROG = build_program()
    res = run_bass_kernel_spmd(_PROG, in_maps, core_ids=list(range(NCORES)))
    rs = res.results
    y_prompt = np.zeros((B, T, D), np.float32)
    nk = np.zeros((1, B, 128, 2, 64), np.float32)
    nv = np.zeros((1, B, 128, 2, 64), np.float32)
    npool = np.zeros((1, B, 15, 512), np.float32)
    y_s = np.zeros((128, 1, D), np.float32)
    ks = np.zeros((1, 128, 128, 2, 64), np.float32)
    vs = np.zeros((1, 128, 128, 2, 64), np.float32)
    pss = np.zeros((1, 128, 15, 512), np.float32)
    for core in range(NCORES):
        b, hf = core // 2, core % 2
        r = rs[core]
        y_prompt[b, hf * half:(hf + 1) * half] = np.asarray(r["y"]).reshape(half, D)
        if hf == 1:
            nk[0, b] = np.asarray(r["k_new"]).reshape(128, 2, 64)
            nv[0, b] = np.asarray(r["v_new"]).reshape(128, 2, 64)
            npool[0, b] = np.asarray(r["p_new"])
        sl = slice(core * NS, (core + 1) * NS)
        y_s[sl, 0] = np.asarray(r["ys"])
        ks[0, sl] = np.asarray(r["ks_new"]).reshape(NS, 128, 2, 64)
        vs[0, sl] = np.asarray(r["vs_new"]).reshape(NS, 128, 2, 64)
        pss[0, sl] = np.asarray(r["ps_new"])
    return (y_prompt, y_s, nk, nv, npool, ks, vs, pss)
```
